# Optimizing a Trainium2 kernel written in Bass

```python
import math
import jax, jax.numpy as jnp
from jax import lax
import numpy as np

D_MODEL = 1024
BATCH = 4
SEQ = 8192
DEPTH = 1
DEC_BATCH = 32
DEC_SEQ = 2048
PAST_LEN = 128

N_MEM = 256
DA_HEADS = 8
DA_HEAD_DIM = 64
DA_Q_WIDTH = DA_HEADS * 2 * DA_HEAD_DIM
DA_V_WIDTH = DA_HEADS * 2 * DA_HEAD_DIM
ROT_DIM = DA_HEAD_DIM // 4
ROPE_THETA = 500000.0
Q_BLOCK = 128
SUBLN_EPS = 1e-5
LRU_WIDTH = 1024
LRU_BLOCKS = 8
LRU_BLOCK_DIM = LRU_WIDTH // LRU_BLOCKS
CONV_WIDTH = 4
CONV_PAD_LEFT = 2
LRU_C = 8.0
IN_SPLITS = (DA_Q_WIDTH,
             2 * DA_Q_WIDTH,
             2 * DA_Q_WIDTH + DA_V_WIDTH,
             2 * DA_Q_WIDTH + DA_V_WIDTH + LRU_WIDTH,
             2 * DA_Q_WIDTH + DA_V_WIDTH + 2 * LRU_WIDTH,
             2 * DA_Q_WIDTH + DA_V_WIDTH + 2 * LRU_WIDTH + D_MODEL)
IN_WIDTH = 2 * DA_Q_WIDTH + DA_V_WIDTH + 2 * LRU_WIDTH + 2 * D_MODEL
XA_HEADS = 4
XA_HEAD_DIM = D_MODEL // XA_HEADS
D_FF = ((8 * D_MODEL + 3 * 256 - 1) // (3 * 256)) * 256
DEEPNORM_ALPHA = (2.0 * DEPTH) ** 0.25
DEEPNORM_BETA = (8.0 * DEPTH) ** -0.25
LN_EPS = 1e-5

kernel_name = "hybrid_diffattn_rglru_encoder"


def layer_norm(x, g, b):
    xf = x.astype(jnp.float32)
    mu = jnp.mean(xf, axis=-1, keepdims=True)
    xc = xf - mu
    var = jnp.mean(xc * xc, axis=-1, keepdims=True)
    y = xc * lax.rsqrt(var + LN_EPS) * g.astype(jnp.float32) + b.astype(jnp.float32)
    return y.astype(x.dtype)


def rope_tables(T):
    inv = ROPE_THETA ** (-jnp.arange(0, ROT_DIM, 2, dtype=jnp.float32) / ROT_DIM)
    ang = jnp.arange(T, dtype=jnp.float32)[:, None] * inv[None, :]
    return jnp.cos(ang), jnp.sin(ang)


def apply_partial_rope(x, cos, sin):
    xr = x[..., :ROT_DIM].astype(jnp.float32)
    half = ROT_DIM // 2
    x1, x2 = xr[..., :half], xr[..., half:]
    c = cos[None, :, None, None, :]
    s = sin[None, :, None, None, :]
    rot = jnp.concatenate([x1 * c - x2 * s, x2 * c + x1 * s], axis=-1).astype(x.dtype)
    return jnp.concatenate([rot, x[..., ROT_DIM:]], axis=-1)


def diff_attention(q, k, v, lam):
    B, T = q.shape[0], q.shape[1]
    nblk = T // Q_BLOCK
    scale = DA_HEAD_DIM ** -0.5
    qb = (q * scale).reshape(B, nblk, Q_BLOCK, DA_HEADS, 2, DA_HEAD_DIM).transpose(1, 0, 2, 3, 4, 5)

    def one_block(qblk):
        s = jnp.einsum('bqhcd,bkhcd->bhcqk', qblk, k).astype(jnp.float32)
        p = jax.nn.softmax(s, axis=-1)
        p = p[:, :, 0] - lam * p[:, :, 1]
        return jnp.einsum('bhqk,bkhe->bqhe', p.astype(v.dtype), v)

    o = lax.map(one_block, qb)
    return o.transpose(1, 0, 2, 3, 4).reshape(B, T, DA_HEADS, 2 * DA_HEAD_DIM)


def centred_conv(x, w, b):
    T = x.shape[1]
    xp = jnp.pad(x, ((0, 0), (CONV_PAD_LEFT, CONV_WIDTH - 1 - CONV_PAD_LEFT), (0, 0)))
    out = xp[:, 0:T] * w[0]
    for j in range(1, CONV_WIDTH):
        out = out + xp[:, j:j + T] * w[j]
    return out + b


def block_diag(x, w, b):
    xb = x.reshape(x.shape[:-1] + (LRU_BLOCKS, LRU_BLOCK_DIM))
    y = jnp.einsum('btnd,nde->btne', xb, w).reshape(x.shape)
    return y + b


def lin_scan_combine(c1, c2):
    a1, b1 = c1
    a2, b2 = c2
    return a1 * a2, a2 * b1 + b2


def rg_lru_direction(x, w_a, b_a, w_x, b_x, a_param, reverse):
    T = x.shape[1]
    r = jax.nn.sigmoid(block_diag(x, w_a, b_a).astype(jnp.float32))
    i = jax.nn.sigmoid(block_diag(x, w_x, b_x).astype(jnp.float32))
    log_a = -LRU_C * r * jax.nn.softplus(-a_param.astype(jnp.float32))
    a = jnp.exp(log_a)
    mult = jnp.sqrt(-jnp.expm1(2.0 * log_a))
    start = T - 1 if reverse else 0
    is_start = (jnp.arange(T) == start)[None, :, None]
    mult = jnp.where(is_start, 1.0, mult)
    u = mult * i * x.astype(jnp.float32)
    _, h = lax.associative_scan(lin_scan_combine, (a, u), reverse=reverse, axis=1)
    return h


def mixer(x, w_in, lambda_q1, lambda_k1, lambda_q2, lambda_k2, subln_g, conv_w, conv_b,
          lru_wa, lru_ba, lru_wx, lru_bx, lru_a, p_attn, p_lru, w_mix_out, lambda_init):
    B, T, _ = x.shape
    proj = x @ w_in
    q, k, v, xr, yr, g_attn, g_lru = jnp.split(proj, IN_SPLITS, axis=-1)

    cos, sin = rope_tables(T)
    q = apply_partial_rope(q.reshape(B, T, DA_HEADS, 2, DA_HEAD_DIM), cos, sin)
    k = apply_partial_rope(k.reshape(B, T, DA_HEADS, 2, DA_HEAD_DIM), cos, sin)
    v = v.reshape(B, T, DA_HEADS, 2 * DA_HEAD_DIM)
    f32 = jnp.float32
    lam = (jnp.exp(jnp.sum(lambda_q1.astype(f32) * lambda_k1.astype(f32)))
           - jnp.exp(jnp.sum(lambda_q2.astype(f32) * lambda_k2.astype(f32))) + lambda_init)
    o = diff_attention(q, k, v, lam).astype(f32)
    o = o * lax.rsqrt(jnp.mean(o * o, axis=-1, keepdims=True) + SUBLN_EPS) * subln_g.astype(f32)
    attn_out = (o * (1.0 - lambda_init)).reshape(B, T, DA_V_WIDTH).astype(x.dtype)

    xc = centred_conv(xr, conv_w, conv_b)
    h = (rg_lru_direction(xc, lru_wa[0], lru_ba[0], lru_wx[0], lru_bx[0], lru_a[0], False)
         + rg_lru_direction(xc, lru_wa[1], lru_ba[1], lru_wx[1], lru_bx[1], lru_a[1], True))
    lru_out = (h * jax.nn.gelu(yr.astype(f32))).astype(x.dtype)

    merged = jax.nn.sigmoid(g_attn) * (attn_out @ p_attn) + jax.nn.sigmoid(g_lru) * (lru_out @ p_lru)
    return merged @ w_mix_out


def cross_attention(x, mem, xa_wq, xa_wkv, xa_wo):
    B, T, _ = x.shape
    M = mem.shape[1]
    q = (x @ xa_wq).reshape(B, T, XA_HEADS, XA_HEAD_DIM) * (XA_HEAD_DIM ** -0.5)
    k, v = jnp.split(mem @ xa_wkv, 2, axis=-1)
    k = k.reshape(B, M, XA_HEADS, XA_HEAD_DIM)
    v = v.reshape(B, M, XA_HEADS, XA_HEAD_DIM)
    s = jnp.einsum('bqhd,bkhd->bhqk', q, k).astype(jnp.float32)
    p = jax.nn.softmax(s, axis=-1)
    o = jnp.einsum('bhqk,bkhd->bqhd', p.astype(v.dtype), v).reshape(B, T, D_MODEL)
    return o @ xa_wo


def swiglu(x, ffn_w_in, ffn_w_out):
    g, u = jnp.split(x @ ffn_w_in, 2, axis=-1)
    return (jax.nn.silu(g) * u) @ ffn_w_out


def encoder_layer(x, mem, layer_idx, w_in, lambda_q1, lambda_k1, lambda_q2, lambda_k2, subln_g,
                  conv_w, conv_b, lru_wa, lru_ba, lru_wx, lru_bx, lru_a, p_attn, p_lru, w_mix_out,
                  ln1_g, ln1_b, xa_wq, xa_wkv, xa_wo, ln2_g, ln2_b, ffn_w_in, ffn_w_out, ln3_g, ln3_b):
    lambda_init = 0.8 - 0.6 * math.exp(-0.3 * layer_idx)
    m = mixer(x, w_in, lambda_q1, lambda_k1, lambda_q2, lambda_k2, subln_g, conv_w, conv_b,
              lru_wa, lru_ba, lru_wx, lru_bx, lru_a, p_attn, p_lru, w_mix_out, lambda_init)
    x = layer_norm(DEEPNORM_ALPHA * x + m, ln1_g, ln1_b)
    x = layer_norm(DEEPNORM_ALPHA * x + cross_attention(x, mem, xa_wq, xa_wkv, xa_wo), ln2_g, ln2_b)
    x = layer_norm(DEEPNORM_ALPHA * x + swiglu(x, ffn_w_in, ffn_w_out), ln3_g, ln3_b)
    return x


def trunk(x, mem, weights):
    for l in range(DEPTH):
        layer_w = [w[l] for w in weights]
        x = encoder_layer(x, mem, l, *layer_w)
    return x


def setup_inputs(seed: int = 0) -> dict:
    key = jax.random.key(seed)
    ks = jax.random.split(key, 40)
    f32 = jnp.float32

    def nrm(k, shape, scale):
        return jax.random.normal(k, shape, f32) * scale

    def gain(k, shape):
        return 1.0 + 0.02 * jax.random.normal(k, shape, f32)

    a0 = jax.random.uniform(ks[16], (DEPTH, 2, LRU_WIDTH), f32, 0.9, 0.999)
    s0 = a0 ** (1.0 / LRU_C)
    lru_a = jnp.log(s0) - jnp.log1p(-s0)

    return {
        "x_prompt": nrm(ks[0], (BATCH, SEQ, D_MODEL), 1.0),
        "x_sample": nrm(ks[1], (DEC_BATCH, DEC_SEQ, D_MODEL), 1.0),
        "mem_prompt": nrm(ks[2], (BATCH, N_MEM, D_MODEL), 1.0),
        "mem_sample": nrm(ks[3], (DEC_BATCH, N_MEM, D_MODEL), 1.0),
        "w_in": nrm(ks[4], (DEPTH, D_MODEL, IN_WIDTH), D_MODEL ** -0.5),
        "lambda_q1": nrm(ks[5], (DEPTH, DA_HEAD_DIM), 0.1),
        "lambda_k1": nrm(ks[6], (DEPTH, DA_HEAD_DIM), 0.1),
        "lambda_q2": nrm(ks[7], (DEPTH, DA_HEAD_DIM), 0.1),
        "lambda_k2": nrm(ks[8], (DEPTH, DA_HEAD_DIM), 0.1),
        "subln_g": gain(ks[9], (DEPTH, 2 * DA_HEAD_DIM)),
        "conv_w": nrm(ks[10], (DEPTH, CONV_WIDTH, LRU_WIDTH), CONV_WIDTH ** -0.5),
        "conv_b": nrm(ks[11], (DEPTH, LRU_WIDTH), 0.02),
        "lru_wa": nrm(ks[12], (DEPTH, 2, LRU_BLOCKS, LRU_BLOCK_DIM, LRU_BLOCK_DIM), LRU_BLOCK_DIM ** -0.5),
        "lru_ba": nrm(ks[13], (DEPTH, 2, LRU_WIDTH), 0.02),
        "lru_wx": nrm(ks[14], (DEPTH, 2, LRU_BLOCKS, LRU_BLOCK_DIM, LRU_BLOCK_DIM), LRU_BLOCK_DIM ** -0.5),
        "lru_bx": nrm(ks[15], (DEPTH, 2, LRU_WIDTH), 0.02),
        "lru_a": lru_a,
        "p_attn": nrm(ks[17], (DEPTH, DA_V_WIDTH, D_MODEL), DA_V_WIDTH ** -0.5),
        "p_lru": nrm(ks[18], (DEPTH, LRU_WIDTH, D_MODEL), LRU_WIDTH ** -0.5),
        "w_mix_out": nrm(ks[19], (DEPTH, D_MODEL, D_MODEL), DEEPNORM_BETA * D_MODEL ** -0.5),
        "ln1_g": gain(ks[20], (DEPTH, D_MODEL)),
        "ln1_b": nrm(ks[21], (DEPTH, D_MODEL), 0.02),
        "xa_wq": nrm(ks[22], (DEPTH, D_MODEL, D_MODEL), D_MODEL ** -0.5),
        "xa_wkv": nrm(ks[23], (DEPTH, D_MODEL, 2 * D_MODEL), D_MODEL ** -0.5),
        "xa_wo": nrm(ks[24], (DEPTH, D_MODEL, D_MODEL), DEEPNORM_BETA * D_MODEL ** -0.5),
        "ln2_g": gain(ks[25], (DEPTH, D_MODEL)),
        "ln2_b": nrm(ks[26], (DEPTH, D_MODEL), 0.02),
        "ffn_w_in": nrm(ks[27], (DEPTH, D_MODEL, 2 * D_FF), D_MODEL ** -0.5),
        "ffn_w_out": nrm(ks[28], (DEPTH, D_FF, D_MODEL), DEEPNORM_BETA * D_FF ** -0.5),
        "ln3_g": gain(ks[29], (DEPTH, D_MODEL)),
        "ln3_b": nrm(ks[30], (DEPTH, D_MODEL), 0.02),
    }


def reference(x_prompt, x_sample, mem_prompt, mem_sample, w_in, lambda_q1, lambda_k1, lambda_q2,
              lambda_k2, subln_g, conv_w, conv_b, lru_wa, lru_ba, lru_wx, lru_bx, lru_a, p_attn, p_lru,
              w_mix_out, ln1_g, ln1_b, xa_wq, xa_wkv, xa_wo, ln2_g, ln2_b, ffn_w_in, ffn_w_out,
              ln3_g, ln3_b):
    weights = (w_in, lambda_q1, lambda_k1, lambda_q2, lambda_k2, subln_g, conv_w, conv_b,
               lru_wa, lru_ba, lru_wx, lru_bx, lru_a, p_attn, p_lru, w_mix_out,
               ln1_g, ln1_b, xa_wq, xa_wkv, xa_wo, ln2_g, ln2_b, ffn_w_in, ffn_w_out, ln3_g, ln3_b)
    y_prompt = trunk(x_prompt, mem_prompt, weights)
    y_sample = trunk(x_sample, mem_sample, weights)
    return (y_prompt, y_sample)
```

```python
import math
from contextlib import ExitStack

import numpy as np
import concourse.bass as bass
import concourse.mybir as mybir
from concourse.bass_utils import run_bass_kernel_spmd

F32 = mybir.dt.float32
BF16 = mybir.dt.bfloat16
AF = mybir.ActivationFunctionType
ALU = mybir.AluOpType
AX = mybir.AxisListType

D = 1024
NCH = 8
IN_W = 7168
DFF = 2816
NFF = 22
NMEM = 256
LAMBDA_INIT = 0.8 - 0.6 * math.exp(-0.3 * 0)
ALPHA = 2.0 ** 0.25
LN_EPS = 1e-5
SUBLN_EPS = 1e-5
ROPE_THETA = 500000.0

PP_TAP, PP_CB, PP_BAA, PP_BXA, PP_BAB, PP_BXB, PP_LAA, PP_LAB, NPP = 0, 40, 48, 56, 64, 72, 80, 88, 96
BC_LN1G, BC_LN1B, BC_LN2G, BC_LN2B, BC_LN3G, BC_LN3B, BC_SUBG, BC_LAM, NBC = (
    0, 1024, 2048, 3072, 4096, 5120, 6144, 6272, 6528)


class Chan:
    __slots__ = ("name", "sem", "dcount")

    def __init__(self, name):
        self.name = name
        self.sem = None
        self.dcount = 0


class Buf:
    __slots__ = ("name", "w", "r", "chan", "excl")

    def __init__(self, name, excl=False, own=False):
        self.name = name
        self.w = None
        self.r = {}
        self.chan = Chan(name) if own else None
        self.excl = excl


class TR:
    def __init__(self, nc, es):
        self.nc = nc
        self.es = es
        self.eng = {"sync": nc.sync, "act": nc.scalar, "dve": nc.vector, "pool": nc.gpsimd,
                    "pe": nc.tensor}
        self.bufs = []
        self.chans = []
        self.bar = es.enter_context(nc.semaphore("bar"))
        self.barcount = 0
        self.phase = 0
        self.sem = {e: es.enter_context(nc.semaphore(f"s_{e}")) for e in self.eng}
        self.cnt = {e: 0 for e in self.eng}
        self.seen = {e: {f: 0 for f in self.eng} for e in self.eng}
        self.seen_d = {e: {} for e in self.eng}
        self._new_phase_chans()

    def _new_phase_chans(self):
        self.st_chan = {q: Chan(f"st{self.phase}{q}") for q in self.eng}
        self.ld_chan = {q: Chan(f"ld{self.phase}{q}") for q in self.eng}
        self.phase += 1

    def buf(self, name, excl=False, own=False):
        b = Buf(name, excl, own)
        self.bufs.append(b)
        return b

    def _waits(self, eng, reads, writes):
        toks = []
        for b in reads:
            if b.w is not None:
                toks.append((b.w, True))
            if b.excl:
                toks.extend((t, False) for t in b.r.values())
        for b in writes:
            if b.w is not None:
                toks.append((b.w, False))
            toks.extend((t, False) for t in b.r.values())
        e = self.eng[eng]
        for t, raw in toks:
            if t[0] == "e":
                _, f, val = t
                if f == eng and (not raw or eng in ("pe", "sync")):
                    continue
                if self.seen[eng][f] >= val:
                    continue
                e.wait_ge(self.sem[f], val)
                self.seen[eng][f] = val
            else:
                _, ch, val = t
                if self.seen_d[eng].get(id(ch), 0) >= val:
                    continue
                e.wait_ge(ch.sem, ch.dcount)
                self.seen_d[eng][id(ch)] = ch.dcount

    def op(self, eng, fn, reads=(), writes=(), flag=True):
        self._waits(eng, reads, writes)
        ins = fn(self.eng[eng])
        if flag:
            self.cnt[eng] += 1
            ins.then_inc(self.sem[eng], 1)
            val = self.cnt[eng]
        else:
            val = self.cnt[eng] + 1
        tok = ("e", eng, val)
        for b in reads:
            b.r[eng] = tok
        for b in writes:
            b.w = tok
            b.r = {}
        return tok

    def dma(self, q, out, in_, chan=None, reads=(), writes=()):
        self._waits(q, reads, writes)
        if not writes:
            ch = self.st_chan[q]
        else:
            ch = writes[0].chan if writes[0].chan is not None else self.ld_chan[q]
        if ch.sem is None:
            ch.sem = self.es.enter_context(self.nc.semaphore(f"d_{ch.name}"))
            self.chans.append(ch)
        ch.dcount += 16
        self.eng[q].dma_start(out=out, in_=in_).then_inc(ch.sem, 16)
        tok = ("d", ch, ch.dcount)
        for b in reads:
            b.r[("d", id(ch))] = tok
        for b in writes:
            b.w = tok
            b.r = {}
        return tok

    def barrier(self):
        s = self.nc.sync
        for f in self.eng:
            if f != "sync" and self.cnt[f] > self.seen["sync"][f]:
                s.wait_ge(self.sem[f], self.cnt[f])
                self.seen["sync"][f] = self.cnt[f]
        for ch in self.chans:
            if ch.dcount > self.seen_d["sync"].get(id(ch), 0):
                s.wait_ge(ch.sem, ch.dcount)
        self.barcount += 1
        s.sem_inc(self.bar, 1)
        for f in self.eng:
            if f != "sync":
                self.eng[f].wait_ge(self.bar, self.barcount)
        for b in self.bufs:
            b.w = None
            b.r = {}
        for e in self.eng:
            for f in self.eng:
                self.seen[e][f] = self.cnt[f]
            for ch in self.chans:
                self.seen_d[e][id(ch)] = ch.dcount
        self._new_phase_chans()


class Rot:
    def __init__(self, tr, ph, nc, name, shape, dt, n, psum=False):
        self.items = []
        for i in range(n):
            if psum:
                t = ph.enter_context(nc.psum_tensor(f"{name}{i}", shape, dt))
            else:
                t = ph.enter_context(nc.sbuf_tensor(f"{name}{i}", shape, dt))
            self.items.append((t, tr.buf(f"{name}{i}", excl=psum, own=not psum)))
        self.i = 0

    def next(self):
        it = self.items[self.i % len(self.items)]
        self.i += 1
        return it


def sb(tr, ph, nc, name, shape, dt):
    t = ph.enter_context(nc.sbuf_tensor(name, shape, dt))
    return t, tr.buf(name)


def build_program(seqs, debug=False):
    n_seq = len(seqs)
    ctx_off, own_off = [], []
    a = b = 0
    for (tc, to) in seqs:
        ctx_off.append(a)
        own_off.append(b)
        a += tc
        b += to
    N_ctx, N_own = a, b

    nc = bass.Bass("TRN2", target_bir_lowering=False)

    def din(name, shape, dt=F32):
        return nc.dram_tensor(name, list(shape), dt, kind="ExternalInput").ap()

    def dscr(name, shape, dt):
        kind = "ExternalOutput" if debug else "Internal"
        return nc.dram_tensor(name, list(shape), dt, kind=kind).ap()

    xs = din("xs", [N_ctx, D])
    mems = din("mems", [n_seq, NMEM, D])
    rope = din("rope", [128, (N_ctx // 128) * 16])
    w_in = din("w_in", [D, IN_W])
    p_attn = din("p_attn", [D, D])
    p_lru = din("p_lru", [D, D])
    w_mix = din("w_mix", [D, D])
    xa_wq = din("xa_wq", [D, D])
    xa_wkv = din("xa_wkv", [D, 2 * D])
    xa_wo = din("xa_wo", [D, D])
    ffn_wi = din("ffn_wi", [D, 2 * DFF])
    ffn_wo = din("ffn_wo", [DFF, D])
    lru_w = din("lru_w", [4, NCH, 128, 128])
    pp = din("pp", [128, NPP])
    bc = din("bc", [128, NBC])
    ident = din("ident", [128, 128])
    y = nc.dram_tensor("y", [N_own, D], F32, kind="ExternalOutput").ap()

    qT_s = dscr("qT_s", [NCH, 128, N_own], BF16)
    kT_s = dscr("kT_s", [NCH, 128, N_ctx], BF16)
    v_s = dscr("v_s", [NCH, N_ctx, 128], BF16)
    xr_s = dscr("xr_s", [NCH, 128, N_ctx], F32)
    gy_s = dscr("gy_s", [NCH, 128, N_own], BF16)
    ga_s = dscr("ga_s", [NCH, 128, N_own], BF16)
    gl_s = dscr("gl_s", [NCH, 128, N_own], BF16)
    lo_s = dscr("lo_s", [NCH, 128, N_own], BF16)
    ao_s = dscr("ao_s", [NCH, 128, N_own], BF16)
    x2_s = dscr("x2_s", [N_own, D], F32)
    kmT_s = dscr("kmT_s", [n_seq, 128, NCH * NMEM], BF16)
    vm_s = dscr("vm_s", [n_seq, 128, 2 * 4 * 257], BF16)

    es = ExitStack()
    tr = TR(nc, es)
    dbg = dscr("dbg", [8, 128, 512], F32) if debug else None
    dbgc = dscr("dbgc", [128, 128], F32) if debug else None
    dbgl = dscr("dbgl", [128, 1], F32) if debug else None
    def dump(slot, ap, B):
        if debug:
            tr.dma("sync", dbg[slot], ap, chan=B, reads=[B])

    identF, B_identF = sb(tr, es, nc, "identF", [128, 128], F32)
    identB, B_identB = sb(tr, es, nc, "identB", [128, 128], BF16)
    pp_sb, B_pp = sb(tr, es, nc, "pp_sb", [128, NPP], F32)
    neglam, B_neglam = sb(tr, es, nc, "neglam", [128, 1], F32)
    gsub, B_gsub = sb(tr, es, nc, "gsub", [128, 128], F32)
    sc_a, B_sca = sb(tr, es, nc, "sc_a", [128, 16], F32)
    sc_2a, B_sc2a = sb(tr, es, nc, "sc_2a", [128, 16], F32)
    hbias, B_hb = sb(tr, es, nc, "hbias", [128, 32], F32)
    hsc, B_hsc = sb(tr, es, nc, "hsc", [128, 16], F32)
    nhalf, B_nhalf = sb(tr, es, nc, "nhalf", [128, 1], F32)

    def issue_w(t, B, w_ap, ncols, kchunks=NCH, colblk=1024):
        wv = w_ap.rearrange("(kc p) n -> p kc n", p=128)
        for kc in range(kchunks):
            for c0 in range(0, ncols, colblk):
                c1 = min(ncols, c0 + colblk)
                tr.dma("pool", t[:, kc, c0:c1], wv[:, kc, c0:c1], chan=B, writes=[B])

    def load_w(ph, name, w_ap, ncols, kchunks=NCH, colblk=1024, issue=True, own=False):
        t = ph.enter_context(nc.sbuf_tensor(name, [128, kchunks, ncols], BF16))
        B = tr.buf(name, own=own)
        if issue:
            issue_w(t, B, w_ap, ncols, kchunks, colblk)
        return t, B

    with ExitStack() as ph:
        tr.op("pool", lambda e: e.memset(nhalf[:], -0.5), writes=[B_nhalf])
        tr.dma("sync", identF[:], ident, chan=B_identF, writes=[B_identF])
        tr.dma("pool", identB[:], ident, chan=B_identB, writes=[B_identB])
        tr.dma("sync", pp_sb[:], pp, chan=B_pp, writes=[B_pp])
        lamv, B_lamv = sb(tr, ph, nc, "lamv", [128, 384], F32)
        tr.dma("sync", lamv[:], bc[:, BC_SUBG:BC_SUBG + 384], chan=B_lamv, writes=[B_lamv])
        tmp, B_tmp = sb(tr, ph, nc, "p0tmp", [128, 256], F32)
        sm, B_sm = sb(tr, ph, nc, "p0sm", [128, 64], F32)
        tr.op("dve", lambda e: e.tensor_tensor(out=tmp[:, 0:64], in0=lamv[:, 128:192], in1=lamv[:, 192:256], op=ALU.mult),
              reads=[B_lamv], writes=[B_tmp])
        tr.op("dve", lambda e: e.tensor_tensor(out=tmp[:, 64:128], in0=lamv[:, 256:320], in1=lamv[:, 320:384], op=ALU.mult),
              reads=[B_lamv], writes=[B_tmp])
        tr.op("dve", lambda e: e.reduce_sum(out=sm[:, 0:1], in_=tmp[:, 0:64], axis=AX.X), reads=[B_tmp], writes=[B_sm])
        tr.op("dve", lambda e: e.reduce_sum(out=sm[:, 1:2], in_=tmp[:, 64:128], axis=AX.X), reads=[B_tmp], writes=[B_sm])
        tr.op("act", lambda e: e.activation(out=sm[:, 2:4], in_=sm[:, 0:2], func=AF.Exp), reads=[B_sm], writes=[B_sm])
        tr.op("dve", lambda e: e.tensor_tensor(out=sm[:, 4:5], in0=sm[:, 3:4], in1=sm[:, 2:3], op=ALU.subtract),
              reads=[B_sm], writes=[B_sm])
        tr.op("dve", lambda e: e.tensor_scalar_add(out=neglam[:], in0=sm[:, 4:5], scalar1=-LAMBDA_INIT),
              reads=[B_sm], writes=[B_neglam])
        tr.op("dve", lambda e: e.tensor_scalar_mul(out=gsub[:], in0=lamv[:, 0:128], scalar1=1.0 - LAMBDA_INIT),
              reads=[B_lamv], writes=[B_gsub])
        la = pp_sb[:, PP_LAA:PP_LAA + 16]
        t_ay, t_z, t_w, t_ln, t_d, t_rd, t_l1p, t_relu = (sm[:, 8 + 0:8 + 16], sm[:, 24:40], sm[:, 40:56],
                                                         tmp[:, 128:144], tmp[:, 144:160], tmp[:, 160:176],
                                                         tmp[:, 176:192], tmp[:, 192:208])
        tr.op("dve", lambda e: e.tensor_scalar_mul(out=t_ay, in0=la, scalar1=-1.0), reads=[B_pp], writes=[B_sm])
        tr.op("dve", lambda e: e.tensor_tensor(out=t_ay, in0=t_ay, in1=la, op=ALU.max), reads=[B_pp, B_sm], writes=[B_sm])
        tr.op("act", lambda e: e.activation(out=t_z, in_=t_ay, func=AF.Exp, scale=-1.0), reads=[B_sm], writes=[B_sm])
        tr.op("dve", lambda e: e.tensor_scalar_add(out=t_w, in0=t_z, scalar1=1.0), reads=[B_sm], writes=[B_sm])
        tr.op("act", lambda e: e.activation(out=t_ln, in_=t_w, func=AF.Ln), reads=[B_sm], writes=[B_tmp])
        tr.op("dve", lambda e: e.tensor_scalar(out=t_d, in0=t_w, scalar1=-1.0, scalar2=1e-30, op0=ALU.add, op1=ALU.max),
              reads=[B_sm], writes=[B_tmp])
        tr.op("dve", lambda e: e.reciprocal(out=t_rd, in_=t_d), reads=[B_tmp], writes=[B_tmp])
        tr.op("dve", lambda e: e.tensor_tensor(out=t_l1p, in0=t_ln, in1=t_z, op=ALU.mult), reads=[B_tmp, B_sm], writes=[B_tmp])
        tr.op("dve", lambda e: e.tensor_tensor(out=t_l1p, in0=t_l1p, in1=t_rd, op=ALU.mult), reads=[B_tmp], writes=[B_tmp])
        tr.op("dve", lambda e: e.tensor_scalar(out=t_relu, in0=la, scalar1=-1.0, scalar2=0.0, op0=ALU.mult, op1=ALU.max),
              reads=[B_pp], writes=[B_tmp])
        tr.op("dve", lambda e: e.tensor_tensor(out=t_l1p, in0=t_l1p, in1=t_relu, op=ALU.add), reads=[B_tmp], writes=[B_tmp])
        tr.op("dve", lambda e: e.tensor_scalar_mul(out=sc_a[:], in0=t_l1p, scalar1=-8.0), reads=[B_tmp], writes=[B_sca])
        tr.op("dve", lambda e: e.tensor_scalar_mul(out=sc_2a[:], in0=t_l1p, scalar1=-16.0), reads=[B_tmp], writes=[B_sc2a])
        tr.op("dve", lambda e: e.tensor_scalar_mul(out=hsc[:], in0=t_l1p, scalar1=-4.0), reads=[B_tmp], writes=[B_hsc])
        tr.op("dve", lambda e: e.tensor_scalar_mul(out=hbias[:], in0=pp_sb[:, PP_BAA:PP_BAA + 32], scalar1=0.5), reads=[B_pp], writes=[B_hb])

        wkv, B_wkv = load_w(ph, "wkv", xa_wkv, 2 * D, colblk=2048)
        MF = Rot(tr, ph, nc, "mf", [128, D], F32, 2)
        PST = Rot(tr, ph, nc, "p0pst", [128, 4, 128], F32, 2, psum=True)
        PSK = Rot(tr, ph, nc, "p0psk", [128, 512], F32, 3, psum=True)
        memT, B_memT = sb(tr, ph, nc, "memT", [128, NCH, NMEM], BF16)
        kmT, B_kmT = sb(tr, ph, nc, "kmT", [128, NCH, NMEM], BF16)
        vm, B_vm = sb(tr, ph, nc, "vm", [128, 2, 4, 257], BF16)
        tr.op("dve", lambda e: e.memset(vm[:], 1.0), writes=[B_vm])
        cp = 0
        for s in range(n_seq):
            for t in range(2):
                mf, B_mf = MF.next()
                tr.dma("sync", mf[:], mems[s, t * 128:(t + 1) * 128, :], chan=B_mf, writes=[B_mf])
                for hf in range(2):
                    pst, B_pst = PST.next()
                    for i in range(4):
                        kk = hf * 4 + i
                        tr.op("pe", lambda e, pst=pst, i=i, kk=kk, mf=mf: e.transpose(out=pst[:, i, :], in_=mf[:, kk * 128:(kk + 1) * 128], identity=identF[:]),
                              reads=[B_mf, B_identF], writes=[B_pst], flag=(i == 3))
                    eng = "act" if cp % 2 == 0 else "dve"
                    cp += 1
                    if eng == "act":
                        tr.op("act", lambda e, pst=pst, hf=hf, t=t: e.copy(out=memT[:, hf * 4:hf * 4 + 4, t * 128:(t + 1) * 128], in_=pst[:]),
                              reads=[B_pst], writes=[B_memT])
                    else:
                        tr.op("dve", lambda e, pst=pst, hf=hf, t=t: e.tensor_copy(out=memT[:, hf * 4:hf * 4 + 4, t * 128:(t + 1) * 128], in_=pst[:]),
                              reads=[B_pst], writes=[B_memT])
            for m in range(NCH):
                ps, B_ps = PSK.next()
                for k in range(NCH):
                    tr.op("pe", lambda e, ps=ps, k=k, m=m: e.matmul(ps[:, 0:NMEM], lhsT=wkv[:, k, m * 128:(m + 1) * 128], rhs=memT[:, k, :], start=(k == 0), stop=(k == NCH - 1)),
                          reads=[B_wkv, B_memT], writes=[B_ps], flag=(k == NCH - 1))
                if m % 2 == 0:
                    tr.op("act", lambda e, ps=ps, m=m: e.copy(out=kmT[:, m, :], in_=ps[:, 0:NMEM]), reads=[B_ps], writes=[B_kmT])
                else:
                    tr.op("dve", lambda e, ps=ps, m=m: e.tensor_copy(out=kmT[:, m, :], in_=ps[:, 0:NMEM]), reads=[B_ps], writes=[B_kmT])
            tr.dma("sync", kmT_s[s], kmT[:].rearrange("p c m -> p (c m)"), chan=B_kmT, reads=[B_kmT])
            for t in range(2):
                for hf in range(2):
                    ps, B_ps = PSK.next()
                    for k in range(NCH):
                        tr.op("pe", lambda e, ps=ps, k=k, t=t, hf=hf: e.matmul(ps[:], lhsT=memT[:, k, t * 128:(t + 1) * 128], rhs=wkv[:, k, D + hf * 512:D + (hf + 1) * 512], start=(k == 0), stop=(k == NCH - 1)),
                              reads=[B_wkv, B_memT], writes=[B_ps], flag=(k == NCH - 1))
                    src = ps[:].rearrange("p (h d) -> p h d", h=2)
                    if (t + hf) % 2 == 0:
                        tr.op("act", lambda e, src=src, t=t, hf=hf: e.copy(out=vm[:, t, hf * 2:hf * 2 + 2, 0:256], in_=src), reads=[B_ps], writes=[B_vm])
                    else:
                        tr.op("dve", lambda e, src=src, t=t, hf=hf: e.tensor_copy(out=vm[:, t, hf * 2:hf * 2 + 2, 0:256], in_=src), reads=[B_ps], writes=[B_vm])
            tr.dma("sync", vm_s[s], vm[:].rearrange("p t h d -> p (t h d)"), chan=B_vm, reads=[B_vm])
        if debug:
            tr.dma("sync", dbgc[:, 0:16], sc_a[:], chan=B_sca, reads=[B_sca])
            tr.dma("sync", dbgc[:, 16:32], hsc[:], chan=B_hsc, reads=[B_hsc])
            tr.dma("sync", dbgc[:, 32:64], hbias[:], chan=B_hb, reads=[B_hb])
            tr.dma("sync", dbgl, neglam[:], chan=B_neglam, reads=[B_neglam])
            tr.dma("sync", dbgc[:, 64:128], sm[:], chan=B_sm, reads=[B_sm])
            tr.dma("sync", dbg[7, :, 0:256], tmp[:], chan=B_tmp, reads=[B_tmp])
            tr.dma("sync", dbg[6, :, 0:384], lamv[:], chan=B_lamv, reads=[B_lamv])
        tr.barrier()

    if debug == "p0":
        return finish(nc, tr, es, y, N_own)

    with ExitStack() as ph:
        win, B_win = load_w(ph, "win", w_in, IN_W)
        n_tiles128 = N_ctx // 128
        rope_sb, B_rope = sb(tr, ph, nc, "rope_sb", [128, n_tiles128, 16], F32)
        tr.dma("sync", rope_sb[:].rearrange("p t c -> p (t c)"), rope, chan=B_rope, writes=[B_rope])
        XF = Rot(tr, ph, nc, "xf", [128, D], F32, 3)
        XT = Rot(tr, ph, nc, "xT", [128, NCH, 512], BF16, 2)
        PST = Rot(tr, ph, nc, "p1pst", [128, 4, 128], F32, 2, psum=True)
        PSM = Rot(tr, ph, nc, "p1psm", [128, 512], F32, 4, psum=True)
        PTB = Rot(tr, ph, nc, "p1ptb", [128, 8, 128], BF16, 2, psum=True)
        QTM = Rot(tr, ph, nc, "qtm", [128, 512], BF16, 3)
        RT = Rot(tr, ph, nc, "ropet", [128, 4, 64], F32, 2)
        QTS = Rot(tr, ph, nc, "qts", [128, 4, 512], BF16, 2)
        VTM = Rot(tr, ph, nc, "vtm", [128, D], BF16, 2)
        STF = Rot(tr, ph, nc, "stf", [128, 512], F32, 3)
        STB = Rot(tr, ph, nc, "stb", [128, 512], BF16, 4)
        cpi = [0]

        def evac_copy(out, in_, rb, wb):
            cpi[0] += 1
            if cpi[0] % 2 == 0:
                tr.op("act", lambda e: e.copy(out=out, in_=in_), reads=rb, writes=wb)
            else:
                tr.op("dve", lambda e: e.tensor_copy(out=out, in_=in_), reads=rb, writes=wb)

        def tm_rope_group(xT, B_xT, col0, dst, tok_own0, tile128_0):
            its = [(hf, j) for hf in range(2) for j in range(4)]
            state = {}
            qts_of = {}

            def stA(i):
                hf, j = its[i]
                ps, B_ps = PSM.next()
                for k in range(NCH):
                    tr.op("pe", lambda e, k=k: e.matmul(ps[:], lhsT=xT[:, k, j * 128:(j + 1) * 128], rhs=win[:, k, col0 + hf * 512:col0 + (hf + 1) * 512], start=(k == 0), stop=(k == NCH - 1)),
                          reads=[B_xT, B_win], writes=[B_ps], flag=(k == NCH - 1))
                qtm, B_qtm = QTM.next()
                tr.op("act", lambda e: e.copy(out=qtm[:], in_=ps[:]), reads=[B_ps], writes=[B_qtm])
                ps3 = ps[:].rearrange("p (g d) -> p g d", g=8)
                q3 = qtm[:].rearrange("p (g d) -> p g d", g=8)
                x1, x2 = ps3[:, :, 0:8], ps3[:, :, 8:16]
                ti = tile128_0 + j
                cos = rope_sb[:, ti, 0:8].unsqueeze(1).broadcast_to([128, 8, 8])
                sin = rope_sb[:, ti, 8:16].unsqueeze(1).broadcast_to([128, 8, 8])
                rt, B_rt = RT.next()
                r3 = rt[:].rearrange("p a (g d) -> p a g d", g=8)
                tr.op("dve", lambda e: e.tensor_tensor(out=r3[:, 0], in0=x1, in1=cos, op=ALU.mult), reads=[B_ps, B_rope], writes=[B_rt])
                tr.op("dve", lambda e: e.tensor_tensor(out=r3[:, 1], in0=x2, in1=sin, op=ALU.mult), reads=[B_ps, B_rope], writes=[B_rt])
                tr.op("dve", lambda e: e.tensor_tensor(out=r3[:, 2], in0=x2, in1=cos, op=ALU.mult), reads=[B_ps, B_rope], writes=[B_rt])
                tr.op("dve", lambda e: e.tensor_tensor(out=r3[:, 3], in0=x1, in1=sin, op=ALU.mult), reads=[B_ps, B_rope], writes=[B_rt])
                tr.op("dve", lambda e: e.tensor_tensor(out=q3[:, :, 0:8], in0=r3[:, 0], in1=r3[:, 1], op=ALU.subtract), reads=[B_rt], writes=[B_qtm])
                tr.op("dve", lambda e: e.tensor_tensor(out=q3[:, :, 8:16], in0=r3[:, 2], in1=r3[:, 3], op=ALU.add), reads=[B_rt], writes=[B_qtm])
                state[i] = (qtm, B_qtm)

            def stC(i):
                hf, j = its[i]
                if j == 0:
                    qts_of[hf] = QTS.next()
                qts, B_qts = qts_of[hf]
                qtm, B_qtm = state.pop(i)
                ptb, B_ptb = PTB.next()
                for t in range(4):
                    tr.op("pe", lambda e, t=t: e.transpose(out=ptb[:, t, :], in_=qtm[:, t * 128:(t + 1) * 128], identity=identB[:]),
                          reads=[B_qtm, B_identB], writes=[B_ptb], flag=(t == 3))
                evac_copy(qts[:, :, j * 128:(j + 1) * 128], ptb[:, 0:4, :], [B_ptb], [B_qts])
                if j == 3:
                    tr.dma("sync", dst[hf * 4:hf * 4 + 4, :, tok_own0:tok_own0 + 512].rearrange("h p t -> p h t"), qts[:], reads=[B_qts])

            n = len(its)
            for i in range(n + 2):
                if i < n:
                    stA(i)
                if i >= 2:
                    stC(i - 2)

        def fm_group(xT, B_xT, col0, dst, tok0, kind):
            for m in range(NCH):
                ps, B_ps = PSM.next()
                for k in range(NCH):
                    tr.op("pe", lambda e, ps=ps, k=k, m=m: e.matmul(ps[:], lhsT=win[:, k, col0 + m * 128:col0 + (m + 1) * 128], rhs=xT[:, k, :], start=(k == 0), stop=(k == NCH - 1)),
                          reads=[B_xT, B_win], writes=[B_ps], flag=(k == NCH - 1))
                if kind == "f32":
                    st, B_st = STF.next()
                    evac_copy(st[:], ps[:], [B_ps], [B_st])
                else:
                    st, B_st = STB.next()
                    fn = AF.Gelu_apprx_tanh if kind == "gelu" else AF.Sigmoid
                    tr.op("act", lambda e, st=st, ps=ps, fn=fn: e.activation(out=st[:], in_=ps[:], func=fn), reads=[B_ps], writes=[B_st])
                tr.dma("sync", dst[m, :, tok0:tok0 + 512], st[:], chan=B_st, reads=[B_st])

        for s in range(n_seq):
            T_ctx, T_own = seqs[s]
            for tt in range(T_ctx // 512):
                own = tt * 512 < T_own
                tok0 = ctx_off[s] + tt * 512
                tokown0 = own_off[s] + tt * 512
                xT, B_xT = XT.next()
                for j in range(4):
                    xf, B_xf = XF.next()
                    tr.dma("sync", xf[:], xs[tok0 + j * 128:tok0 + (j + 1) * 128, :], chan=B_xf, writes=[B_xf])
                    for hf in range(2):
                        pst, B_pst = PST.next()
                        for i in range(4):
                            kk = hf * 4 + i
                            tr.op("pe", lambda e, pst=pst, i=i, kk=kk, xf=xf: e.transpose(out=pst[:, i, :], in_=xf[:, kk * 128:(kk + 1) * 128], identity=identF[:]),
                                  reads=[B_xf, B_identF], writes=[B_pst], flag=(i == 3))
                        evac_copy(xT[:, hf * 4:hf * 4 + 4, j * 128:(j + 1) * 128], pst[:], [B_pst], [B_xT])
                ONLY = ("xr", "q", "k", "v", "gy", "ga", "gl")
                if "xr" in ONLY:
                    fm_group(xT, B_xT, 3 * D, xr_s, tok0, "f32")
                if own and "q" in ONLY:
                    tm_rope_group(xT, B_xT, 0, qT_s, tokown0, tok0 // 128)
                if "k" in ONLY:
                    tm_rope_group(xT, B_xT, D, kT_s, tok0, tok0 // 128)
                for j in (range(4) if "v" in ONLY else []):
                    vtm, B_vtm = VTM.next()
                    for hf in range(2):
                        ps, B_ps = PSM.next()
                        for k in range(NCH):
                            tr.op("pe", lambda e, ps=ps, k=k, j=j, hf=hf: e.matmul(ps[:], lhsT=xT[:, k, j * 128:(j + 1) * 128], rhs=win[:, k, 2 * D + hf * 512:2 * D + (hf + 1) * 512], start=(k == 0), stop=(k == NCH - 1)),
                                  reads=[B_xT, B_win], writes=[B_ps], flag=(k == NCH - 1))
                        evac_copy(vtm[:, hf * 512:(hf + 1) * 512], ps[:], [B_ps], [B_vtm])
                    t0 = tok0 + j * 128
                    tr.dma("sync", v_s[:, t0:t0 + 128, :].rearrange("h p e -> p h e"), vtm[:].rearrange("p (h e) -> p h e", h=8), chan=B_vtm, reads=[B_vtm])
                if own and "gy" in ONLY:
                    fm_group(xT, B_xT, 4 * D, gy_s, tokown0, "gelu")
                if own and "ga" in ONLY:
                    fm_group(xT, B_xT, 5 * D, ga_s, tokown0, "sig")
                if own and "gl" in ONLY:
                    fm_group(xT, B_xT, 6 * D, gl_s, tokown0, "sig")
        tr.barrier()

    if debug == "p1":
        return finish(nc, tr, es, y, N_own)

    with ExitStack() as ph:
        GRP = 1024
        CP = 2048
        maxT = max(tc for tc, _ in seqs)
        maxO = max(to for _, to in seqs)
        lw, B_lw = sb(tr, ph, nc, "lw", [128, 4, NCH, 128], BF16)
        for a_ in range(4):
            tr.dma("pool", lw[:, a_], lru_w[a_].rearrange("c d e -> d c e"), writes=[B_lw])
        XR = Rot(tr, ph, nc, "xr", [128, maxT + 4], F32, 1)
        xc = ph.enter_context(nc.sbuf_tensor("xc", [128, maxT], F32))
        xcb = ph.enter_context(nc.sbuf_tensor("xcb", [128, maxT], BF16))
        npc = (maxT + CP - 1) // CP
        B_xcp = [tr.buf(f"xc{i}") for i in range(npc)]
        B_xcbp = [tr.buf(f"xcb{i}") for i in range(npc)]
        hB, B_hB = sb(tr, ph, nc, "hB", [128, maxO], F32)
        carry, B_carry = sb(tr, ph, nc, "carry", [128, 2], F32)
        PSG = Rot(tr, ph, nc, "psg", [128, 512], F32, 6, psum=True)
        g = max(min(GRP, to) for _, to in seqs)
        RT_ = Rot(tr, ph, nc, "r_t", [128, g], F32, 2)
        IT_ = Rot(tr, ph, nc, "i_t", [128, g], F32, 4)
        AT_ = Rot(tr, ph, nc, "a_t", [128, g], F32, 4)
        MT_ = Rot(tr, ph, nc, "m_t", [128, g], F32, 4)
        MX_ = Rot(tr, ph, nc, "mx_t", [128, g], F32, 4)
        HA_ = Rot(tr, ph, nc, "hA", [128, g], F32, 2)
        GY = Rot(tr, ph, nc, "gyt", [128, g], BF16, 2)
        LO = Rot(tr, ph, nc, "lot", [128, g], BF16, 2)
        for (xr, B_xr) in XR.items:
            tr.op("pool", lambda e, xr=xr: e.memset(xr[:], 0.0), writes=[B_xr])

        for s in range(n_seq):
            T_ctx, T_own = seqs[s]
            gs = min(GRP, T_own)
            cp = min(CP, T_ctx)
            for c in range(NCH):
                xr, B_xr = XR.next()
                tr.dma("sync", xr[:, 2:2 + T_ctx], xr_s[c, :, ctx_off[s]:ctx_off[s] + T_ctx], writes=[B_xr])
                if T_ctx < maxT:
                    tr.op("pool", lambda e: e.memset(xr[:, 2 + T_ctx:4 + T_ctx], 0.0), writes=[B_xr])
                tp = PP_TAP + c * 5
                for pi in range(T_ctx // cp - 1, -1, -1):
                    p0 = pi * cp
                    tr.op("pool", lambda e: e.tensor_scalar(out=xc[:, p0:p0 + cp], in0=xr[:, p0:p0 + cp], scalar1=pp_sb[:, tp:tp + 1], scalar2=pp_sb[:, PP_CB + c:PP_CB + c + 1], op0=ALU.mult, op1=ALU.add),
                          reads=[B_xr, B_pp], writes=[B_xcp[pi]])
                    for jj in range(1, 5):
                        tr.op("dve", lambda e: e.scalar_tensor_tensor(out=xc[:, p0:p0 + cp], in0=xr[:, p0 + jj:p0 + jj + cp], scalar=pp_sb[:, tp + jj:tp + jj + 1], in1=xc[:, p0:p0 + cp], op0=ALU.mult, op1=ALU.add),
                              reads=[B_xr, B_pp, B_xcp[pi]], writes=[B_xcp[pi]])
                    tr.op("act", lambda e: e.copy(out=xcb[:, p0:p0 + cp], in_=xc[:, p0:p0 + cp]), reads=[B_xcp[pi]], writes=[B_xcbp[pi]])

                ngB = T_ctx // gs
                jobs = [(1, gi * gs, gi == ngB - 1, gi) for gi in range(ngB - 1, -1, -1)]
                jobs += [(0, gi * gs, gi == 0, gi) for gi in range(T_own // gs)]
                n = gs

                def stage_a(job):
                    d, t0, first, gi = job
                    pi = t0 // cp
                    col = 0 if d == 0 else 16
                    colx = 8 if d == 0 else 24
                    r_t, B_r = RT_.next()
                    i_t, B_i = IT_.next()
                    a_t, B_a = AT_.next()
                    m_t, B_m = MT_.next()
                    for sub in range(n // 512):
                        c0 = sub * 512
                        psr, B_psr = PSG.next()
                        tr.op("pe", lambda e: e.matmul(psr[:], lhsT=lw[:, 2 * d, c, :], rhs=xcb[:, t0 + c0:t0 + c0 + 512], start=True, stop=True),
                              reads=[B_lw, B_xcbp[pi]], writes=[B_psr])
                        psi, B_psi = PSG.next()
                        tr.op("pe", lambda e: e.matmul(psi[:], lhsT=lw[:, 2 * d + 1, c, :], rhs=xcb[:, t0 + c0:t0 + c0 + 512], start=True, stop=True),
                              reads=[B_lw, B_xcbp[pi]], writes=[B_psi])
                        tr.op("act", lambda e: e.activation(out=r_t[:, c0:c0 + 512], in_=psr[:], func=AF.Tanh, scale=0.5, bias=hbias[:, col + c:col + c + 1]),
                              reads=[B_psr, B_hb], writes=[B_r])
                        tr.op("act", lambda e: e.activation(out=i_t[:, c0:c0 + 512], in_=psi[:], func=AF.Tanh, scale=0.5, bias=hbias[:, colx + c:colx + c + 1]),
                              reads=[B_psi, B_hb], writes=[B_i])
                    sc = sc_a[:, d * 8 + c:d * 8 + c + 1]
                    hs = hsc[:, d * 8 + c:d * 8 + c + 1]
                    tr.op("act", lambda e: e.activation(out=a_t[:, 0:n], in_=r_t[:, 0:n], func=AF.Exp, scale=hs, bias=hs), reads=[B_r, B_hsc], writes=[B_a])
                    tr.op("act", lambda e: e.activation(out=m_t[:, 0:n], in_=r_t[:, 0:n], func=AF.Exp, scale=sc, bias=sc), reads=[B_r, B_sca], writes=[B_m])
                    tr.op("dve", lambda e: e.tensor_scalar(out=m_t[:, 0:n], in0=m_t[:, 0:n], scalar1=0.9999999, scalar2=-1.0, op0=ALU.min, op1=ALU.mult), reads=[B_m], writes=[B_m])
                    return (job, i_t, B_i, a_t, B_a, m_t, B_m)

                def stage_s(st):
                    job, i_t, B_i, a_t, B_a, m_t, B_m = st
                    tr.op("act", lambda e: e.activation(out=m_t[:, 0:n], in_=m_t[:, 0:n], func=AF.Sqrt, scale=0.25, bias=0.25), reads=[B_m], writes=[B_m])

                def stage_r(st):
                    job, i_t, B_i, a_t, B_a, m_t, B_m = st
                    d, t0, first, gi = job
                    pi = t0 // cp
                    mx, B_mx = MX_.next()
                    tr.op("dve", lambda e: e.tensor_tensor(out=mx[:, 0:n], in0=m_t[:, 0:n], in1=xc[:, t0:t0 + n], op=ALU.mult), reads=[B_m, B_xcp[pi]], writes=[B_mx])
                    if first:
                        k0 = n - 1 if d == 1 else 0
                        tr.op("dve", lambda e: e.tensor_scalar(out=mx[:, k0:k0 + 1], in0=xc[:, t0 + k0:t0 + k0 + 1], scalar1=0.5, scalar2=None, op0=ALU.mult), reads=[B_xcp[pi], B_mx], writes=[B_mx])
                    tr.op("dve", lambda e: e.scalar_tensor_tensor(out=mx[:, 0:n], in0=i_t[:, 0:n], scalar=1.0, in1=mx[:, 0:n], op0=ALU.add, op1=ALU.mult), reads=[B_i, B_mx], writes=[B_mx])
                    if d == 1:
                        own_g = t0 < T_own
                        if own_g:
                            dst, B_dst = hB[:, t0:t0 + n], B_hB
                        else:
                            hA, B_hA = HA_.next()
                            dst, B_dst = hA[:, 0:n], B_hA
                        init = 0.0 if first else carry[:, 1:2]
                        tr.op("dve", lambda e: e.tensor_tensor_scan(out=dst[:, ::-1], data0=a_t[:, n - 1::-1], data1=mx[:, n - 1::-1], initial=init, op0=ALU.mult, op1=ALU.add),
                              reads=[B_a, B_mx, B_carry], writes=[B_dst])
                        if gi > 0:
                            tr.op("dve", lambda e: e.tensor_copy(out=carry[:, 1:2], in_=dst[:, 0:1]), reads=[B_dst], writes=[B_carry])
                    else:
                        hA, B_hA = HA_.next()
                        gyt, B_gy = GY.next()
                        tr.dma("sync", gyt[:, 0:n], gy_s[c, :, own_off[s] + t0:own_off[s] + t0 + n], writes=[B_gy])
                        init = 0.0 if first else carry[:, 0:1]
                        tr.op("dve", lambda e: e.tensor_tensor_scan(out=hA[:, 0:n], data0=a_t[:, 0:n], data1=mx[:, 0:n], initial=init, op0=ALU.mult, op1=ALU.add),
                              reads=[B_a, B_mx, B_carry], writes=[B_hA])
                        tr.op("dve", lambda e: e.tensor_copy(out=carry[:, 0:1], in_=hA[:, n - 1:n]), reads=[B_hA], writes=[B_carry])
                        tr.op("pool", lambda e: e.tensor_tensor(out=hA[:, 0:n], in0=hA[:, 0:n], in1=hB[:, t0:t0 + n], op=ALU.add), reads=[B_hA, B_hB], writes=[B_hA])
                        lot, B_lo = LO.next()
                        tr.op("pool", lambda e: e.tensor_tensor(out=lot[:, 0:n], in0=hA[:, 0:n], in1=gyt[:, 0:n], op=ALU.mult), reads=[B_hA, B_gy], writes=[B_lo])
                        tr.dma("pool", lo_s[c, :, own_off[s] + t0:own_off[s] + t0 + n], lot[:, 0:n], reads=[B_lo])

                for p0_ in range(0, len(jobs), 2):
                    pair = jobs[p0_:p0_ + 2]
                    sts = [stage_a(j) for j in pair]
                    for st in sts:
                        stage_s(st)
                    for st in sts:
                        stage_r(st)
        tr.barrier()

    if debug == "p2":
        return finish(nc, tr, es, y, N_own)

    es_w5 = ExitStack()
    wq, B_wq = load_w(es_w5, "wq", xa_wq, D, issue=False)
    wo, B_wo = load_w(es_w5, "wo", xa_wo, D, issue=False, own=True)
    es_w4 = ExitStack()
    wpa, B_wpa = load_w(es_w4, "wpa", p_attn, D, issue=False)
    wpl, B_wpl = load_w(es_w4, "wpl", p_lru, D, issue=False)
    wmx, B_wmx = load_w(es_w4, "wmx", w_mix, D, issue=False, own=True)

    with ExitStack() as ph:
        issue_w(wpa, B_wpa, p_attn, D)
        issue_w(wpl, B_wpl, p_lru, D)
        issue_w(wmx, B_wmx, w_mix, D)
        maxT = max(tc for tc, _ in seqs)
        maxO = max(to for _, to in seqs)
        KT = Rot(tr, ph, nc, "kTh", [128, maxT], BF16, 2)
        VH = Rot(tr, ph, nc, "vh", [128, maxT // 128, 129], BF16, 2)
        QT = Rot(tr, ph, nc, "qTh", [128, maxO], BF16, 2)
        PSS = Rot(tr, ph, nc, "pss", [128, 2, 512], F32, 2, psum=True)
        ACCF = [ph.enter_context(nc.psum_tensor(f"acc{i}", [128, 512], F32)) for i in range(3)]
        ACC = [a_[:, 0:387].rearrange("p (s e) -> p s e", s=3) for a_ in ACCF]
        B_ACC = [tr.buf(f"acc{i}", excl=True) for i in range(3)]
        PT3 = Rot(tr, ph, nc, "p3pt", [128, 8, 128], BF16, 1, psum=True)
        PEX = Rot(tr, ph, nc, "pexp", [128, 2, 512], BF16, 3)
        OSB = Rot(tr, ph, nc, "osb", [128, 8, 129], F32, 2)
        EPO = Rot(tr, ph, nc, "epo", [128, 4, 128], F32, 2)
        EPT = Rot(tr, ph, nc, "ept", [128, 4, 128], F32, 2)
        EPS = Rot(tr, ph, nc, "eps", [128, 32], F32, 2)
        AOB = Rot(tr, ph, nc, "aob", [128, 4, 128], BF16, 2)
        AOT = Rot(tr, ph, nc, "aot", [128, 512], BF16, 2)
        for (vh, B_vh) in VH.items:
            tr.op("pool", lambda e, vh=vh: e.memset(vh[:], 1.0), writes=[B_vh])
        accmap = [(0, 0), (0, 1), (0, 2), (1, 0), (1, 1), (1, 2), (2, 0), (2, 1)]
        gsub_b = gsub[:].unsqueeze(1).broadcast_to([128, 4, 128])

        def issue_loads(s_, h_):
            T_c, T_o = seqs[s_]
            nk = T_c // 128
            kT, B_kT = KT.next()
            vh, B_vh = VH.next()
            qT, B_qT = QT.next()
            tr.dma("sync", kT[:, 0:T_c], kT_s[h_, :, ctx_off[s_]:ctx_off[s_] + T_c], writes=[B_kT])
            vsrc = v_s[h_, ctx_off[s_]:ctx_off[s_] + T_c, :].rearrange("(k p) e -> p k e", p=128)
            for k0 in range(0, nk, 8):
                k1 = min(nk, k0 + 8)
                tr.dma("sync", vh[:, k0:k1, 0:128], vsrc[:, k0:k1, :], writes=[B_vh])
            tr.dma("sync", qT[:, 0:T_o], qT_s[h_, :, own_off[s_]:own_off[s_] + T_o], writes=[B_qT])
            return (kT, B_kT, vh, B_vh, qT, B_qT)

        loaded = [issue_loads(0, 0)]
        pending = []

        for s in range(n_seq):
            T_ctx, T_own = seqs[s]
            nkt = T_ctx // 128
            for h in range(NCH):
                kT, B_kT, vh, B_vh, qT, B_qT = loaded.pop(0)
                nidx = s * NCH + h + 1
                if nidx < n_seq * NCH:
                    loaded.append(issue_loads(nidx // NCH, nidx % NCH))
                for qt in range(T_own // 512):
                    q0 = qt * 512

                    def scores(kt):
                        ps, B_ps = PSS.next()
                        for c in range(2):
                            tr.op("pe", lambda e, ps=ps, c=c, kt=kt: e.matmul(ps[:, c, :], lhsT=kT[c * 64:(c + 1) * 64, kt * 128:(kt + 1) * 128], rhs=qT[c * 64:(c + 1) * 64, q0:q0 + 512], start=True, stop=True),
                                  reads=[B_kT, B_qT], writes=[B_ps], flag=(c == 1))
                        return ps, B_ps

                    nxt = scores(0)
                    for kt in range(nkt):
                        ps, B_ps = nxt
                        if kt + 1 < nkt:
                            nxt = scores(kt + 1)
                        if pending and kt == min(10, nkt - 1):
                            pending.pop(0)()
                        pe_, B_pe = PEX.next()
                        tr.op("act", lambda e, pe_=pe_, ps=ps: e.activation(out=pe_[:], in_=ps[:], func=AF.Exp, scale=0.125), reads=[B_ps], writes=[B_pe])
                        last = (kt == nkt - 1)
                        for c in range(2):
                            for j in range(4):
                                bk, sl = accmap[c * 4 + j]
                                tr.op("pe", lambda e, pe_=pe_, c=c, j=j, bk=bk, sl=sl, kt=kt, last=last: e.matmul(ACC[bk][:, sl, :], lhsT=pe_[:, c, j * 128:(j + 1) * 128], rhs=vh[:, kt, :], start=(kt == 0 and sl == 0), stop=last, skip_group_check=True),
                                      reads=[B_pe, B_vh], writes=[B_ACC[bk]], flag=(last and (c * 4 + j) in (2, 5, 7)))
                    osb, B_osb = OSB.next()
                    for bk, (lo_, hi_) in enumerate(((0, 3), (3, 6), (6, 8))):
                        tr.op("dve", lambda e, bk=bk, lo_=lo_, hi_=hi_: e.tensor_copy(out=osb[:, lo_:hi_, :], in_=ACC[bk][:, 0:hi_ - lo_, :]), reads=[B_ACC[bk]], writes=[B_osb])
                    o4 = osb[:].rearrange("p (c j) e -> p c j e", c=2)
                    ep, B_ep = EPS.next()
                    epo, B_epo = EPO.next()
                    ept, B_ept = EPT.next()
                    rc = ep[:, 0:8].rearrange("p (c j) -> p c j", c=2)
                    tr.op("dve", lambda e: e.reciprocal(out=rc, in_=o4[:, :, :, 128]), reads=[B_osb], writes=[B_ep])
                    tr.op("dve", lambda e: e.tensor_scalar(out=ep[:, 8:12], in0=rc[:, 1, :], scalar1=neglam[:, 0:1], scalar2=None, op0=ALU.mult), reads=[B_ep, B_neglam], writes=[B_ep])
                    r0b = rc[:, 0, :].unsqueeze(2).broadcast_to([128, 4, 128])
                    r1b = ep[:, 8:12].unsqueeze(2).broadcast_to([128, 4, 128])
                    tr.op("pool", lambda e: e.tensor_tensor(out=epo[:], in0=o4[:, 0, :, 0:128], in1=r0b, op=ALU.mult), reads=[B_osb, B_ep], writes=[B_epo])
                    tr.op("pool", lambda e: e.tensor_tensor(out=ept[:], in0=o4[:, 1, :, 0:128], in1=r1b, op=ALU.mult), reads=[B_osb, B_ep], writes=[B_ept])
                    tr.op("pool", lambda e: e.tensor_tensor(out=epo[:], in0=epo[:], in1=ept[:], op=ALU.add), reads=[B_epo, B_ept], writes=[B_epo])
                    tr.op("pool", lambda e: e.tensor_tensor(out=ept[:], in0=epo[:], in1=epo[:], op=ALU.mult), reads=[B_epo], writes=[B_ept])
                    tr.op("dve", lambda e: e.reduce_sum(out=ep[:, 12:16], in_=ept[:], axis=AX.X), reads=[B_ept], writes=[B_ep])
                    tr.op("dve", lambda e: e.tensor_scalar(out=ep[:, 16:20], in0=ep[:, 12:16], scalar1=1.0 / 128.0, scalar2=SUBLN_EPS, op0=ALU.mult, op1=ALU.add), reads=[B_ep], writes=[B_ep])
                    tr.op("pool", lambda e: e.tensor_tensor(out=ep[:, 20:24], in0=ep[:, 16:20], in1=nhalf[:, 0:1].broadcast_to([128, 4]), op=ALU.pow), reads=[B_ep, B_nhalf], writes=[B_ep])
                    rsb = ep[:, 20:24].unsqueeze(2).broadcast_to([128, 4, 128])
                    tr.op("pool", lambda e: e.tensor_tensor(out=epo[:], in0=epo[:], in1=rsb, op=ALU.mult), reads=[B_epo, B_ep], writes=[B_epo])
                    aob, B_aob = AOB.next()
                    tr.op("pool", lambda e: e.tensor_tensor(out=aob[:], in0=epo[:], in1=gsub_b, op=ALU.mult), reads=[B_epo, B_gsub], writes=[B_aob])
                    def tail(aob=aob, B_aob=B_aob, dst=ao_s[h, :, own_off[s] + q0:own_off[s] + q0 + 512]):
                        aot, B_aot = AOT.next()
                        pt3, B_pt3 = PT3.next()
                        for j in range(4):
                            tr.op("pe", lambda e, j=j: e.transpose(out=pt3[:, j, :], in_=aob[:, j, :], identity=identB[:]), reads=[B_aob, B_identB], writes=[B_pt3], flag=(j == 3))
                        tr.op("dve", lambda e: e.tensor_copy(out=aot[:], in_=pt3[:, 0:4, :].rearrange("p j t -> p (j t)")), reads=[B_pt3], writes=[B_aot])
                        tr.dma("pool", dst, aot[:], reads=[B_aot])
                    pending.append(tail)
        while pending:
            pending.pop(0)()
        tr.barrier()

    if debug == "p3":
        return finish(nc, tr, es, y, N_own)

    def ln_tm(ST, lnp, B_lnp, gi, src_ps, B_src, resid, B_res, dst, B_dst, pool_tail=False):
        st, B_st = ST.next()
        d2 = dst[:].rearrange("p (a d) -> p a d", a=2)
        r2 = resid[:].rearrange("p (a d) -> p a d", a=2)
        tr.op("dve", lambda e: e.scalar_tensor_tensor(out=d2, in0=r2, scalar=ALPHA, in1=src_ps[:], op0=ALU.mult, op1=ALU.add),
              reads=[B_res, B_src], writes=[B_dst])
        tr.op("dve", lambda e: e.bn_stats(out=st[:, 0:6], in_=dst[:, 0:512]), reads=[B_dst], writes=[B_st])
        tr.op("dve", lambda e: e.bn_stats(out=st[:, 6:12], in_=dst[:, 512:1024]), reads=[B_dst], writes=[B_st])
        tr.op("dve", lambda e: e.bn_aggr(out=st[:, 12:14], in_=st[:, 0:12]), reads=[B_st], writes=[B_st])
        tr.op("dve", lambda e: e.tensor_scalar_add(out=st[:, 15:16], in0=st[:, 13:14], scalar1=LN_EPS), reads=[B_st], writes=[B_st])
        tr.op("pool", lambda e: e.tensor_tensor(out=st[:, 14:15], in0=st[:, 15:16], in1=nhalf[:, 0:1], op=ALU.pow), reads=[B_st, B_nhalf], writes=[B_st])
        if pool_tail:
            tr.op("pool", lambda e: e.tensor_scalar(out=dst[:], in0=dst[:], scalar1=st[:, 12:13], scalar2=st[:, 14:15], op0=ALU.subtract, op1=ALU.mult),
                  reads=[B_st, B_dst], writes=[B_dst])
            tr.op("pool", lambda e: e.tensor_tensor(out=dst[:], in0=dst[:], in1=lnp[:, gi, :], op=ALU.mult), reads=[B_lnp, B_dst], writes=[B_dst])
            tr.op("pool", lambda e: e.tensor_tensor(out=dst[:], in0=dst[:], in1=lnp[:, gi + 1, :], op=ALU.add), reads=[B_lnp, B_dst], writes=[B_dst])
            return
        tr.op("dve", lambda e: e.scalar_tensor_tensor(out=dst[:], in0=dst[:], scalar=st[:, 12:13], in1=lnp[:, gi, :], op0=ALU.subtract, op1=ALU.mult),
              reads=[B_st, B_lnp, B_dst], writes=[B_dst])
        tr.op("dve", lambda e: e.scalar_tensor_tensor(out=dst[:], in0=dst[:], scalar=st[:, 14:15], in1=lnp[:, gi + 1, :], op0=ALU.mult, op1=ALU.add),
              reads=[B_st, B_lnp, B_dst], writes=[B_dst])

    x1_s = dscr("x1_s", [N_own, D], F32)
    TA = 512
    with ExitStack() as ph:
        issue_w(wq, B_wq, xa_wq, D)
        issue_w(wo, B_wo, xa_wo, D)
        lnp, B_lnp = sb(tr, ph, nc, "lnp", [128, 2, D], F32)
        tr.dma("sync", lnp[:].rearrange("p a d -> p (a d)"), bc[:, 0:2 * D], writes=[B_lnp])
        AO = Rot(tr, ph, nc, "aoT", [128, NCH, TA], BF16, 2)
        LOo = Rot(tr, ph, nc, "loT", [128, NCH, TA], BF16, 2)
        GA = Rot(tr, ph, nc, "gaT", [128, NCH, TA], BF16, 2)
        GL = Rot(tr, ph, nc, "glT", [128, NCH, TA], BF16, 2)
        XIN = Rot(tr, ph, nc, "xin", [128, D], F32, 3)
        MG = Rot(tr, ph, nc, "mgT", [128, NCH, TA], BF16, 2)
        X1 = Rot(tr, ph, nc, "x1", [128, D], F32, 3)
        TMP = Rot(tr, ph, nc, "t4", [128, TA], F32, 4)
        ST = Rot(tr, ph, nc, "st4", [128, 16], F32, 4)
        PSA = Rot(tr, ph, nc, "psa", [128, 512], F32, 4, psum=True)
        PSO = Rot(tr, ph, nc, "pso", [128, 2, 512], F32, 2, psum=True)
        tiles = [(s, tt) for s in range(n_seq) for tt in range(seqs[s][1] // TA)]

        def loads(idx):
            s, tt = tiles[idx]
            o0 = own_off[s] + tt * TA
            r = []
            for R_, src in ((AO, ao_s), (LOo, lo_s), (GA, ga_s), (GL, gl_s)):
                t, B = R_.next()
                tr.dma("sync", t[:], src[:, :, o0:o0 + TA].rearrange("c p t -> p c t"), writes=[B])
                r.append((t, B))
            return r

        def s1_chunks(ld, mgT, B_mgT, ms):
            (aoT, B_ao), (loT, B_lo), (gaT, B_ga), (glT, B_gl) = ld
            for m in ms:
                psa, B_psa = PSA.next()
                psl, B_psl = PSA.next()
                for k in range(NCH):
                    tr.op("pe", lambda e, k=k: e.matmul(psa[:, 0:TA], lhsT=wpa[:, k, m * 128:(m + 1) * 128], rhs=aoT[:, k, :], start=(k == 0), stop=(k == NCH - 1)),
                          reads=[B_wpa, B_ao], writes=[B_psa], flag=(k == NCH - 1))
                for k in range(NCH):
                    tr.op("pe", lambda e, k=k: e.matmul(psl[:, 0:TA], lhsT=wpl[:, k, m * 128:(m + 1) * 128], rhs=loT[:, k, :], start=(k == 0), stop=(k == NCH - 1)),
                          reads=[B_wpl, B_lo], writes=[B_psl], flag=(k == NCH - 1))
                t4, B_t4 = TMP.next()
                tr.op("act", lambda e: e.copy(out=t4[:], in_=psa[:, 0:TA]), reads=[B_psa], writes=[B_t4])
                t5, B_t5 = TMP.next()
                tr.op("act", lambda e: e.copy(out=t5[:], in_=psl[:, 0:TA]), reads=[B_psl], writes=[B_t5])
                tr.op("pool", lambda e: e.tensor_tensor(out=t4[:], in0=t4[:], in1=gaT[:, m, :], op=ALU.mult), reads=[B_t4, B_ga], writes=[B_t4])
                tr.op("pool", lambda e: e.tensor_tensor(out=t5[:], in0=t5[:], in1=glT[:, m, :], op=ALU.mult), reads=[B_t5, B_gl], writes=[B_t5])
                tr.op("pool", lambda e: e.tensor_tensor(out=mgT[:, m, :], in0=t4[:], in1=t5[:], op=ALU.add), reads=[B_t4, B_t5], writes=[B_mgT])

        def s2_sub(idx, mgT, B_mgT, j):
            s, tt = tiles[idx]
            o0 = own_off[s] + tt * TA
            c0x = ctx_off[s] + tt * TA
            xin, B_xin = XIN.next()
            tr.dma("sync", xin[:], xs[c0x + j * 128:c0x + (j + 1) * 128, :], writes=[B_xin])
            pso, B_pso = PSO.next()
            for hf in range(2):
                for k in range(NCH):
                    tr.op("pe", lambda e, k=k, hf=hf: e.matmul(pso[:, hf, :], lhsT=mgT[:, k, j * 128:(j + 1) * 128], rhs=wmx[:, k, hf * 512:(hf + 1) * 512], start=(k == 0), stop=(k == NCH - 1)),
                          reads=[B_mgT, B_wmx], writes=[B_pso], flag=(k == NCH - 1))
            x1, B_x1 = X1.next()
            ln_tm(ST, lnp, B_lnp, 0, pso, B_pso, xin, B_xin, x1, B_x1)
            tr.dma("pool", x1_s[o0 + j * 128:o0 + (j + 1) * 128, :], x1[:], reads=[B_x1])

        ld_cur = loads(0)
        mg_cur = MG.next()
        ld_nxt = loads(1) if len(tiles) > 1 else None
        s1_chunks(ld_cur, mg_cur[0], mg_cur[1], range(NCH))
        for idx in range(len(tiles)):
            has_next = idx + 1 < len(tiles)
            if has_next:
                mg_nxt = MG.next()
            for j in range(TA // 128):
                if has_next:
                    s1_chunks(ld_nxt, mg_nxt[0], mg_nxt[1], range(2 * j, 2 * j + 2))
                s2_sub(idx, mg_cur[0], mg_cur[1], j)
            if has_next:
                mg_cur = mg_nxt
                ld_nxt = loads(idx + 2) if idx + 2 < len(tiles) else None
        tr.barrier()

    es_w4.close()

    with ExitStack() as ph:
        lnp, B_lnp = sb(tr, ph, nc, "lnp2", [128, 2, D], F32)
        tr.dma("sync", lnp[:].rearrange("p a d -> p (a d)"), bc[:, 2 * D:4 * D], writes=[B_lnp])
        KM = Rot(tr, ph, nc, "kmT4", [128, NCH, NMEM], BF16, 2)
        VM = Rot(tr, ph, nc, "vm4", [128, 2, 4, 257], BF16, 2)
        NJ = TA // 128
        X1L = Rot(tr, ph, nc, "x1l", [128, D], F32, 3 * NJ)
        XB = Rot(tr, ph, nc, "x1b", [128, D], BF16, 2)
        X1T = Rot(tr, ph, nc, "x1T", [128, NCH, TA], BF16, 2)
        QCT = Rot(tr, ph, nc, "qcT", [128, NCH, TA], BF16, 2)
        PTc = Rot(tr, ph, nc, "ptc", [128, TA], BF16, 4)
        OTM = Rot(tr, ph, nc, "otm", [128, D], BF16, 2 * NJ)
        OT = Rot(tr, ph, nc, "oT", [128, NCH, TA], BF16, 2)
        X2 = Rot(tr, ph, nc, "x2", [128, D], F32, 2)
        ST = Rot(tr, ph, nc, "st5a", [128, 16], F32, 6)
        PSA = Rot(tr, ph, nc, "psa2", [128, 512], F32, 2, psum=True)
        PSV = Rot(tr, ph, nc, "psv", [128, 512], F32, 2, psum=True)
        PTB = Rot(tr, ph, nc, "p4ptb", [128, 8, 128], BF16, 2, psum=True)
        PSO = Rot(tr, ph, nc, "pso2", [128, 2, 512], F32, 1, psum=True)
        tiles = [(s, tt) for s in range(n_seq) for tt in range(seqs[s][1] // TA)]
        memkv = {}
        cpi2 = [0]

        def cp2(out, in_, rb, wb):
            cpi2[0] += 1
            if cpi2[0] % 2 == 0:
                tr.op("act", lambda e: e.copy(out=out, in_=in_), reads=rb, writes=wb)
            else:
                tr.op("dve", lambda e: e.tensor_copy(out=out, in_=in_), reads=rb, writes=wb)

        def front1(idx):
            s, tt = tiles[idx]
            o0 = own_off[s] + tt * TA
            if s not in memkv:
                kmT, B_kmT = KM.next()
                vm, B_vm = VM.next()
                tr.dma("sync", kmT[:].rearrange("p c m -> p (c m)"), kmT_s[s], writes=[B_kmT])
                tr.dma("sync", vm[:].rearrange("p t h d -> p (t h d)"), vm_s[s], writes=[B_vm])
                memkv[s] = (kmT, B_kmT, vm, B_vm)
            x1T, B_x1T = X1T.next()
            x1l = []
            for j in range(NJ):
                x1, B_x1 = X1L.next()
                tr.dma("sync", x1[:], x1_s[o0 + j * 128:o0 + (j + 1) * 128, :], writes=[B_x1])
                x1l.append((x1, B_x1))
                xb, B_xb = XB.next()
                tr.op("act", lambda e: e.copy(out=xb[:], in_=x1[:]), reads=[B_x1], writes=[B_xb])
                ptb, B_ptb = PTB.next()
                for k in range(NCH):
                    tr.op("pe", lambda e, k=k: e.transpose(out=ptb[:, k, :], in_=xb[:, k * 128:(k + 1) * 128], identity=identB[:]), reads=[B_xb, B_identB], writes=[B_ptb], flag=(k == NCH - 1))
                tr.op("act", lambda e: e.copy(out=x1T[:, :, j * 128:(j + 1) * 128], in_=ptb[:]), reads=[B_ptb], writes=[B_x1T])
            return dict(idx=idx, s=s, o0=o0, x1T=(x1T, B_x1T), x1l=x1l)

        def front2(t):
            x1T, B_x1T = t["x1T"]
            qcT, B_qcT = QCT.next()
            for m in range(NCH):
                psa, B_psa = PSA.next()
                for k in range(NCH):
                    tr.op("pe", lambda e, k=k: e.matmul(psa[:, 0:TA], lhsT=wq[:, k, m * 128:(m + 1) * 128], rhs=x1T[:, k, :], start=(k == 0), stop=(k == NCH - 1)),
                          reads=[B_wq, B_x1T], writes=[B_psa], flag=(k == NCH - 1))
                cp2(qcT[:, m, :], psa[:, 0:TA], [B_psa], [B_qcT])
            t["qcT"] = (qcT, B_qcT)

        def back1(t):
            kmT, B_kmT, vm, B_vm = memkv[t["s"]]
            qcT, B_qcT = t["qcT"]
            otms = [OTM.next() for _ in range(NJ)]
            t["otms"] = otms

            def sc(hh):
                pts = []
                for mt in range(2):
                    psa, B_psa = PSA.next()
                    for dc in range(2):
                        tr.op("pe", lambda e, dc=dc: e.matmul(psa[:, 0:TA], lhsT=kmT[:, hh * 2 + dc, mt * 128:(mt + 1) * 128], rhs=qcT[:, hh * 2 + dc, :], start=(dc == 0), stop=(dc == 1)),
                              reads=[B_kmT, B_qcT], writes=[B_psa], flag=(dc == 1))
                    pt, B_pt = PTc.next()
                    tr.op("act", lambda e: e.activation(out=pt[:], in_=psa[:, 0:TA], func=AF.Exp, scale=1.0 / 16.0), reads=[B_psa], writes=[B_pt])
                    pts.append((pt, B_pt))
                return pts

            def pv(hh, pts):
                for j in range(NJ):
                    psv, B_psv = PSV.next()
                    for mt in range(2):
                        pt, B_pt = pts[mt]
                        tr.op("pe", lambda e, mt=mt: e.matmul(psv[:, 0:257], lhsT=pt[:, j * 128:(j + 1) * 128], rhs=vm[:, mt, hh, :], start=(mt == 0), stop=(mt == 1)),
                              reads=[B_pt, B_vm], writes=[B_psv], flag=(mt == 1))
                    st, B_st = ST.next()
                    tr.op("dve", lambda e: e.reciprocal(out=st[:, 0:1], in_=psv[:, 256:257]), reads=[B_psv], writes=[B_st])
                    otm, B_otm = otms[j]
                    tr.op("dve", lambda e: e.tensor_scalar(out=otm[:, hh * 256:(hh + 1) * 256], in0=psv[:, 0:256], scalar1=st[:, 0:1], scalar2=None, op0=ALU.mult),
                          reads=[B_psv, B_st], writes=[B_otm])

            prev = sc(0)
            for hh in range(1, 4):
                cur = sc(hh)
                pv(hh - 1, prev)
                prev = cur
            pv(3, prev)

        def back3(t):
            oT, B_oT = OT.next()
            for j in range(NJ):
                otm, B_otm = t["otms"][j]
                ptb, B_ptb = PTB.next()
                for k in range(NCH):
                    tr.op("pe", lambda e, k=k: e.transpose(out=ptb[:, k, :], in_=otm[:, k * 128:(k + 1) * 128], identity=identB[:]), reads=[B_otm, B_identB], writes=[B_ptb], flag=(k == NCH - 1))
                cp2(oT[:, :, j * 128:(j + 1) * 128], ptb[:], [B_ptb], [B_oT])
            t["oT"] = (oT, B_oT)

        def back4(t):
            oT, B_oT = t["oT"]
            o0 = t["o0"]
            for j in range(NJ):
                pso, B_pso = PSO.next()
                for hf in range(2):
                    for k in range(NCH):
                        tr.op("pe", lambda e, k=k, hf=hf: e.matmul(pso[:, hf, :], lhsT=oT[:, k, j * 128:(j + 1) * 128], rhs=wo[:, k, hf * 512:(hf + 1) * 512], start=(k == 0), stop=(k == NCH - 1)),
                              reads=[B_oT, B_wo], writes=[B_pso], flag=(k == NCH - 1))
                x2, B_x2 = X2.next()
                x1, B_x1 = t["x1l"][j]
                ln_tm(ST, lnp, B_lnp, 0, pso, B_pso, x1, B_x1, x2, B_x2)
                tr.dma("pool", x2_s[o0 + j * 128:o0 + (j + 1) * 128, :], x2[:], reads=[B_x2])

        cur = front1(0)
        front2(cur)
        for idx in range(len(tiles)):
            nx = front1(idx + 1) if idx + 1 < len(tiles) else None
            back1(cur)
            if nx is not None:
                front2(nx)
            back3(cur)
            back4(cur)
            cur = nx
        tr.barrier()

    es_w5.close()
    if debug == "p4a":
        return finish(nc, tr, es, y, N_own)

    TT = 256
    NS = TT // 128
    with ExitStack() as ph:
        wi, B_wi = load_w(ph, "wi", ffn_wi, 2 * DFF, colblk=1408)
        wo2, B_wo2 = load_w(ph, "wo2", ffn_wo, D, kchunks=NFF, own=True)
        lnp, B_lnp = sb(tr, ph, nc, "lnp3", [128, 2, D], F32)
        tr.dma("sync", lnp[:].rearrange("p a d -> p (a d)"), bc[:, 4 * D:6 * D], chan=B_lnp, writes=[B_lnp])
        X2 = Rot(tr, ph, nc, "x2in", [128, D], F32, 3 * NS)
        XB = Rot(tr, ph, nc, "x2b", [128, D], BF16, 2)
        aT, B_aT = sb(tr, ph, nc, "aT", [128, NFF, TT], BF16)
        SG = Rot(tr, ph, nc, "sg", [128, TT], F32, 3)
        YO = Rot(tr, ph, nc, "yo", [128, D], F32, 2)
        ST = Rot(tr, ph, nc, "st5", [128, 16], F32, 4)
        PSA = Rot(tr, ph, nc, "psb", [128, 512], F32, 4, psum=True)
        PSO = Rot(tr, ph, nc, "pso5", [128, 2, 512], F32, 1, psum=True)
        PTB = Rot(tr, ph, nc, "p5ptb", [128, 8, 128], BF16, 2, psum=True)
        X2T = Rot(tr, ph, nc, "x2Tr", [128, NCH, TT], BF16, 2)
        tiles = [(s, tt) for s in range(n_seq) for tt in range(seqs[s][1] // TT)]

        def front(idx):
            s, tt = tiles[idx]
            o0 = own_off[s] + tt * TT
            x2T, B_x2T = X2T.next()
            x2s = []
            for j in range(NS):
                x2, B_x2 = X2.next()
                tr.dma("sync", x2[:], x2_s[o0 + j * 128:o0 + (j + 1) * 128, :], writes=[B_x2])
                x2s.append((x2, B_x2))
                xb, B_xb = XB.next()
                tr.op("act", lambda e: e.copy(out=xb[:], in_=x2[:]), reads=[B_x2], writes=[B_xb])
                ptb, B_ptb = PTB.next()
                for k in range(NCH):
                    tr.op("pe", lambda e, k=k: e.transpose(out=ptb[:, k, :], in_=xb[:, k * 128:(k + 1) * 128], identity=identB[:]), reads=[B_xb, B_identB], writes=[B_ptb], flag=(k == NCH - 1))
                tr.op("act", lambda e: e.copy(out=x2T[:, :, j * 128:(j + 1) * 128], in_=ptb[:]), reads=[B_ptb], writes=[B_x2T])
            return dict(o0=o0, x2T=(x2T, B_x2T), x2s=x2s)

        cur = front(0)
        for idx in range(len(tiles)):
            o0 = cur["o0"]
            x2T, B_x2T = cur["x2T"]
            x2s = cur["x2s"]
            for f in range(NFF):
                psg, B_psg = PSA.next()
                psu, B_psu = PSA.next()
                for k in range(NCH):
                    tr.op("pe", lambda e, k=k: e.matmul(psg[:, 0:TT], lhsT=wi[:, k, f * 128:(f + 1) * 128], rhs=x2T[:, k, :], start=(k == 0), stop=(k == NCH - 1)),
                          reads=[B_wi, B_x2T], writes=[B_psg], flag=(k == NCH - 1))
                for k in range(NCH):
                    tr.op("pe", lambda e, k=k: e.matmul(psu[:, 0:TT], lhsT=wi[:, k, DFF + f * 128:DFF + (f + 1) * 128], rhs=x2T[:, k, :], start=(k == 0), stop=(k == NCH - 1)),
                          reads=[B_wi, B_x2T], writes=[B_psu], flag=(k == NCH - 1))
                sg, B_sg = SG.next()
                tr.op("act", lambda e: e.activation(out=sg[:], in_=psg[:, 0:TT], func=AF.Silu), reads=[B_psg], writes=[B_sg])
                tr.op("dve", lambda e: e.tensor_tensor(out=aT[:, f, :], in0=psu[:, 0:TT], in1=sg[:], op=ALU.mult), reads=[B_psu, B_sg], writes=[B_aT])
            nx = front(idx + 1) if idx + 1 < len(tiles) else None
            for j in range(NS):
                pso, B_pso = PSO.next()
                for hf in range(2):
                    for f in range(NFF):
                        tr.op("pe", lambda e, f=f, hf=hf: e.matmul(pso[:, hf, :], lhsT=aT[:, f, j * 128:(j + 1) * 128], rhs=wo2[:, f, hf * 512:(hf + 1) * 512], start=(f == 0), stop=(f == NFF - 1)),
                              reads=[B_aT, B_wo2], writes=[B_pso], flag=(f == NFF - 1))
                yo, B_yo = YO.next()
                x2, B_x2 = x2s[j]
                ln_tm(ST, lnp, B_lnp, 0, pso, B_pso, x2, B_x2, yo, B_yo)
                tr.dma("pool", y[o0 + j * 128:o0 + (j + 1) * 128, :], yo[:], reads=[B_yo])
            cur = nx
        tr.barrier()
    return finish(nc, tr, es, y, N_own)


def finish(nc, tr, es, y, N_own):
    if tr.barcount == 0 or any(tr.cnt[e] for e in tr.cnt):
        tr.barrier()
    es.close()
    return nc


def _rope_table(T):
    half = 8
    inv = (np.float32(ROPE_THETA) ** (-np.arange(0, 16, 2, dtype=np.float32) / np.float32(16))).astype(np.float32)
    ang = (np.arange(T, dtype=np.float32)[:, None] * inv[None, :]).astype(np.float32)
    return np.concatenate([np.cos(ang), np.sin(ang)], axis=1).astype(np.float32)


def _fm(v):
    return np.ascontiguousarray(np.asarray(v, np.float32).reshape(NCH, 128).T)


def make_core_inputs(inp, assign, shared):
    seq_list, rev = assign
    xs = np.concatenate([x for (x, m, to) in seq_list], axis=0)
    mems = np.stack([m for (x, m, to) in seq_list], axis=0)
    ropes = []
    for (x, m, to) in seq_list:
        tb = _rope_table(x.shape[0])
        ropes.append(tb[::-1] if rev else tb)
    rope = np.concatenate(ropes, axis=0)
    nt = rope.shape[0] // 128
    rope = np.ascontiguousarray(rope.reshape(nt, 128, 16).transpose(1, 0, 2).reshape(128, nt * 16))
    A, B = (1, 0) if rev else (0, 1)
    cw = np.asarray(inp["conv_w"][0], np.float32)
    z = np.zeros_like(cw[0])
    taps = [z, cw[3], cw[2], cw[1], cw[0]] if rev else [cw[0], cw[1], cw[2], cw[3], z]
    pp = np.zeros((128, NPP), np.float32)
    for j in range(5):
        pp[:, PP_TAP + j:PP_TAP + 40:5] = _fm(taps[j])
    pp[:, PP_CB:PP_CB + 8] = _fm(inp["conv_b"][0])
    pp[:, PP_BAA:PP_BAA + 8] = _fm(inp["lru_ba"][0, A])
    pp[:, PP_BXA:PP_BXA + 8] = _fm(inp["lru_bx"][0, A])
    pp[:, PP_BAB:PP_BAB + 8] = _fm(inp["lru_ba"][0, B])
    pp[:, PP_BXB:PP_BXB + 8] = _fm(inp["lru_bx"][0, B])
    pp[:, PP_LAA:PP_LAA + 8] = _fm(inp["lru_a"][0, A])
    pp[:, PP_LAB:PP_LAB + 8] = _fm(inp["lru_a"][0, B])
    lru_w = np.ascontiguousarray(np.stack([inp["lru_wa"][0, A], inp["lru_wx"][0, A], inp["lru_wa"][0, B], inp["lru_wx"][0, B]], axis=0).astype(np.float32))
    d = dict(shared)
    d.update(xs=np.ascontiguousarray(xs), mems=np.ascontiguousarray(mems), rope=rope, pp=pp, lru_w=lru_w)
    return d


def make_shared(inp):
    bcv = np.concatenate([np.asarray(inp[k][0], np.float32).ravel() for k in
                          ("ln1_g", "ln1_b", "ln2_g", "ln2_b", "ln3_g", "ln3_b", "subln_g", "lambda_q1", "lambda_k1", "lambda_q2", "lambda_k2")])
    assert bcv.shape[0] == NBC
    bc = np.ascontiguousarray(np.broadcast_to(bcv[None, :], (128, NBC)))
    f = lambda k: np.ascontiguousarray(np.asarray(inp[k][0], np.float32))
    return dict(w_in=f("w_in"), p_attn=f("p_attn"), p_lru=f("p_lru"), w_mix=f("w_mix_out"), xa_wq=f("xa_wq"),
                xa_wkv=f("xa_wkv"), xa_wo=f("xa_wo"), ffn_wi=f("ffn_w_in"), ffn_wo=f("ffn_w_out"), bc=bc,
                ident=np.eye(128, dtype=np.float32))


def kernel(**inp):
    return _run(inp, 8)


def _run(inp, n, debug=False):
    xp = np.asarray(inp["x_prompt"], np.float32)
    xsm = np.asarray(inp["x_sample"], np.float32)
    mp = np.asarray(inp["mem_prompt"], np.float32)
    ms = np.asarray(inp["mem_sample"], np.float32)
    Bp, Tp, _ = xp.shape
    Bs, Ts, _ = xsm.shape
    per = Bs // n
    seqs = [(Ts, Ts)] * per + [(Tp, Tp // 2)]
    shared = make_shared(inp)
    in_maps = []
    for c in range(n):
        rev = (c % 2 == 1)
        sl = []
        for i in range(per):
            x = xsm[c * per + i]
            sl.append((x[::-1] if rev else x, ms[c * per + i], Ts))
        x = xp[c // 2]
        sl.append((x[::-1] if rev else x, mp[c // 2], Tp // 2))
        in_maps.append(make_core_inputs(inp, (sl, rev), shared))
    nc = build_program(seqs, debug=debug)
    res = run_bass_kernel_spmd(nc, in_maps, core_ids=list(range(n)))
    if debug:
        return res.results, seqs
    y_p = np.empty((Bp, Tp, D), np.float32)
    y_s = np.empty((Bs, Ts, D), np.float32)
    for c in range(n):
        yc = res.results[c]["y"]
        rev = (c % 2 == 1)
        for i in range(per):
            blk = yc[i * Ts:(i + 1) * Ts]
            y_s[c * per + i] = blk[::-1] if rev else blk
        blk = yc[per * Ts:per * Ts + Tp // 2]
        if rev:
            y_p[c // 2, Tp // 2:] = blk[::-1]
        else:
            y_p[c // 2, :Tp // 2] = blk
    return (y_p, y_s)
```

```python
import math
from contextlib import ExitStack

import numpy as np
import concourse.bass as bass
import concourse.mybir as mybir
from concourse.bass_utils import run_bass_kernel_spmd

F32 = mybir.dt.float32
BF16 = mybir.dt.bfloat16
AF = mybir.ActivationFunctionType
ALU = mybir.AluOpType
AX = mybir.AxisListType

D = 1024
NCH = 8
IN_W = 7168
DFF = 2816
NFF = 22
NMEM = 256
LAMBDA_INIT = 0.8 - 0.6 * math.exp(-0.3 * 0)
ALPHA = 2.0 ** 0.25
LN_EPS = 1e-5
SUBLN_EPS = 1e-5
ROPE_THETA = 500000.0

PP_TAP, PP_CB, PP_BAA, PP_BXA, PP_BAB, PP_BXB, PP_LAA, PP_LAB, NPP = 0, 40, 48, 56, 64, 72, 80, 88, 96
BC_LN1G, BC_LN1B, BC_LN2G, BC_LN2B, BC_LN3G, BC_LN3B, BC_SUBG, BC_LAM, NBC = (
    0, 1024, 2048, 3072, 4096, 5120, 6144, 6272, 6528)


class Chan:
    __slots__ = ("name", "sem", "dcount")

    def __init__(self, name):
        self.name = name
        self.sem = None
        self.dcount = 0


class Buf:
    __slots__ = ("name", "w", "r", "chan", "excl", "span")

    def __init__(self, name, excl=False, own=False, span=False):
        self.name = name
        self.span = span
        self.w = None
        self.r = {}
        self.chan = Chan(name) if own else None
        self.excl = excl


class TR:
    def __init__(self, nc, es):
        self.nc = nc
        self.es = es
        self.eng = {"sync": nc.sync, "act": nc.scalar, "dve": nc.vector, "pool": nc.gpsimd,
                    "pe": nc.tensor}
        self.bufs = []
        self.chans = []
        self.bar = es.enter_context(nc.semaphore("bar"))
        self.barcount = 0
        self.phase = 0
        self.sem = {e: es.enter_context(nc.semaphore(f"s_{e}")) for e in self.eng}
        self.cnt = {e: 0 for e in self.eng}
        self.seen = {e: {f: 0 for f in self.eng} for e in self.eng}
        self.seen_d = {e: {} for e in self.eng}
        self._new_phase_chans()

    def _new_phase_chans(self):
        self.st_chan = {q: Chan(f"st{self.phase}{q}") for q in self.eng}
        self.ld_chan = {q: Chan(f"ld{self.phase}{q}") for q in self.eng}
        self.phase += 1

    def buf(self, name, excl=False, own=False, span=False):
        b = Buf(name, excl, own, span)
        self.bufs.append(b)
        return b

    def _waits(self, eng, reads, writes):
        toks = []
        for b in reads:
            if b.w is not None:
                toks.append((b.w, True))
            if b.excl:
                toks.extend((t, False) for t in b.r.values())
        for b in writes:
            if b.w is not None:
                toks.append((b.w, False))
            toks.extend((t, False) for t in b.r.values())
        e = self.eng[eng]
        for t, raw in toks:
            if t[0] == "e":
                _, f, val = t
                if f == eng and (not raw or eng in ("pe", "sync")):
                    continue
                if self.seen[eng][f] >= val:
                    continue
                e.wait_ge(self.sem[f], val)
                self.seen[eng][f] = val
            else:
                _, ch, val = t
                if self.seen_d[eng].get(id(ch), 0) >= val:
                    continue
                e.wait_ge(ch.sem, ch.dcount)
                self.seen_d[eng][id(ch)] = ch.dcount

    def op(self, eng, fn, reads=(), writes=(), flag=True):
        self._waits(eng, reads, writes)
        ins = fn(self.eng[eng])
        if flag:
            self.cnt[eng] += 1
            ins.then_inc(self.sem[eng], 1)
            val = self.cnt[eng]
        else:
            val = self.cnt[eng] + 1
        tok = ("e", eng, val)
        for b in reads:
            b.r[eng] = tok
        for b in writes:
            b.w = tok
            b.r = {}
        return tok

    def dma(self, q, out, in_, chan=None, reads=(), writes=()):
        self._waits(q, reads, writes)
        if not writes:
            ch = self.st_chan[q]
        else:
            ch = writes[0].chan if writes[0].chan is not None else self.ld_chan[q]
        if ch.sem is None:
            ch.sem = self.es.enter_context(self.nc.semaphore(f"d_{ch.name}"))
            self.chans.append(ch)
        ch.dcount += 16
        self.eng[q].dma_start(out=out, in_=in_).then_inc(ch.sem, 16)
        tok = ("d", ch, ch.dcount)
        for b in reads:
            b.r[("d", id(ch))] = tok
        for b in writes:
            b.w = tok
            b.r = {}
        return tok

    def barrier(self):
        s = self.nc.sync
        for f in self.eng:
            if f != "sync" and self.cnt[f] > self.seen["sync"][f]:
                s.wait_ge(self.sem[f], self.cnt[f])
                self.seen["sync"][f] = self.cnt[f]
        span_ch = {id(b.chan) for b in self.bufs if b.span and b.chan is not None}
        for ch in self.chans:
            if id(ch) in span_ch:
                continue
            if ch.dcount > self.seen_d["sync"].get(id(ch), 0):
                s.wait_ge(ch.sem, ch.dcount)
        self.barcount += 1
        s.sem_inc(self.bar, 1)
        for f in self.eng:
            if f != "sync":
                self.eng[f].wait_ge(self.bar, self.barcount)
        for b in self.bufs:
            if b.span:
                continue
            b.w = None
            b.r = {}
        for e in self.eng:
            for f in self.eng:
                self.seen[e][f] = self.cnt[f]
            for ch in self.chans:
                if id(ch) not in span_ch:
                    self.seen_d[e][id(ch)] = ch.dcount
        self._new_phase_chans()


class Rot:
    def __init__(self, tr, ph, nc, name, shape, dt, n, psum=False):
        self.items = []
        for i in range(n):
            if psum:
                t = ph.enter_context(nc.psum_tensor(f"{name}{i}", shape, dt))
            else:
                t = ph.enter_context(nc.sbuf_tensor(f"{name}{i}", shape, dt))
            self.items.append((t, tr.buf(f"{name}{i}", excl=psum, own=not psum)))
        self.i = 0

    def next(self):
        it = self.items[self.i % len(self.items)]
        self.i += 1
        return it


def sb(tr, ph, nc, name, shape, dt):
    t = ph.enter_context(nc.sbuf_tensor(name, shape, dt))
    return t, tr.buf(name)


def build_program(seqs, debug=False):
    n_seq = len(seqs)
    ctx_off, own_off = [], []
    a = b = 0
    for (tc, to) in seqs:
        ctx_off.append(a)
        own_off.append(b)
        a += tc
        b += to
    N_ctx, N_own = a, b

    nc = bass.Bass("TRN2", target_bir_lowering=False)

    def din(name, shape, dt=F32):
        return nc.dram_tensor(name, list(shape), dt, kind="ExternalInput").ap()

    def dscr(name, shape, dt):
        kind = "ExternalOutput" if debug else "Internal"
        return nc.dram_tensor(name, list(shape), dt, kind=kind).ap()

    xs = din("xs", [N_ctx, D])
    mems = din("mems", [n_seq, NMEM, D])
    rope = din("rope", [128, (N_ctx // 128) * 16])
    w_in = din("w_in", [D, IN_W])
    p_attn = din("p_attn", [D, D])
    p_lru = din("p_lru", [D, D])
    w_mix = din("w_mix", [D, D])
    xa_wq = din("xa_wq", [D, D])
    xa_wkv = din("xa_wkv", [D, 2 * D])
    xa_wo = din("xa_wo", [D, D])
    ffn_wi = din("ffn_wi", [D, 2 * DFF])
    ffn_wo = din("ffn_wo", [DFF, D])
    lru_w = din("lru_w", [4, NCH, 128, 128])
    pp = din("pp", [128, NPP])
    bc = din("bc", [128, NBC])
    ident = din("ident", [128, 128])
    y = nc.dram_tensor("y", [N_own, D], F32, kind="ExternalOutput").ap()

    qT_s = dscr("qT_s", [NCH, 128, N_own], BF16)
    kT_s = dscr("kT_s", [NCH, 128, N_ctx], BF16)
    v_s = dscr("v_s", [NCH, N_ctx, 128], BF16)
    xr_s = dscr("xr_s", [NCH, 128, N_ctx], F32)
    gy_s = dscr("gy_s", [NCH, 128, N_own], BF16)
    ga_s = dscr("ga_s", [NCH, 128, N_own], BF16)
    gl_s = dscr("gl_s", [NCH, 128, N_own], BF16)
    lo_s = dscr("lo_s", [NCH, 128, N_own], BF16)
    ao_s = dscr("ao_s", [NCH, 128, N_own], BF16)
    x2_s = dscr("x2_s", [N_own, D], F32)
    kmT_s = dscr("kmT_s", [n_seq, 128, NCH * NMEM], BF16)
    vm_s = dscr("vm_s", [n_seq, 128, 2 * 4 * 257], BF16)

    es = ExitStack()
    tr = TR(nc, es)
    dbg = dscr("dbg", [8, 128, 512], F32) if debug else None
    dbgc = dscr("dbgc", [128, 128], F32) if debug else None
    dbgl = dscr("dbgl", [128, 1], F32) if debug else None
    def dump(slot, ap, B):
        if debug:
            tr.dma("sync", dbg[slot], ap, chan=B, reads=[B])

    identF, B_identF = sb(tr, es, nc, "identF", [128, 128], F32)
    identB, B_identB = sb(tr, es, nc, "identB", [128, 128], BF16)
    pp_sb, B_pp = sb(tr, es, nc, "pp_sb", [128, NPP], F32)
    neglam, B_neglam = sb(tr, es, nc, "neglam", [128, 1], F32)
    gsub, B_gsub = sb(tr, es, nc, "gsub", [128, 128], F32)
    sc_a, B_sca = sb(tr, es, nc, "sc_a", [128, 16], F32)
    sc_2a, B_sc2a = sb(tr, es, nc, "sc_2a", [128, 16], F32)
    hbias, B_hb = sb(tr, es, nc, "hbias", [128, 32], F32)
    hsc, B_hsc = sb(tr, es, nc, "hsc", [128, 16], F32)
    nhalf, B_nhalf = sb(tr, es, nc, "nhalf", [128, 1], F32)

    def issue_w(t, B, w_ap, ncols, kchunks=NCH, colblk=1024):
        wv = w_ap.rearrange("(kc p) n -> p kc n", p=128)
        for kc in range(kchunks):
            for c0 in range(0, ncols, colblk):
                c1 = min(ncols, c0 + colblk)
                tr.dma("pool", t[:, kc, c0:c1], wv[:, kc, c0:c1], chan=B, writes=[B])

    def load_w(ph, name, w_ap, ncols, kchunks=NCH, colblk=1024, issue=True, own=False):
        t = ph.enter_context(nc.sbuf_tensor(name, [128, kchunks, ncols], BF16))
        B = tr.buf(name, own=own)
        if issue:
            issue_w(t, B, w_ap, ncols, kchunks, colblk)
        return t, B

    es_win = ExitStack()
    win = es_win.enter_context(nc.sbuf_tensor("win", [128, NCH, IN_W], BF16))
    B_winb = [tr.buf(f"win{b_}", own=True, span=True) for b_ in range(IN_W // 1024)]

    with ExitStack() as ph:
        tr.op("pool", lambda e: e.memset(nhalf[:], -0.5), writes=[B_nhalf])
        tr.dma("sync", identF[:], ident, chan=B_identF, writes=[B_identF])
        tr.dma("pool", identB[:], ident, chan=B_identB, writes=[B_identB])
        tr.dma("sync", pp_sb[:], pp, chan=B_pp, writes=[B_pp])
        lamv, B_lamv = sb(tr, ph, nc, "lamv", [128, 384], F32)
        tr.dma("sync", lamv[:], bc[:, BC_SUBG:BC_SUBG + 384], chan=B_lamv, writes=[B_lamv])
        tmp, B_tmp = sb(tr, ph, nc, "p0tmp", [128, 256], F32)
        sm, B_sm = sb(tr, ph, nc, "p0sm", [128, 64], F32)
        tr.op("dve", lambda e: e.tensor_tensor(out=tmp[:, 0:64], in0=lamv[:, 128:192], in1=lamv[:, 192:256], op=ALU.mult),
              reads=[B_lamv], writes=[B_tmp])
        tr.op("dve", lambda e: e.tensor_tensor(out=tmp[:, 64:128], in0=lamv[:, 256:320], in1=lamv[:, 320:384], op=ALU.mult),
              reads=[B_lamv], writes=[B_tmp])
        tr.op("dve", lambda e: e.reduce_sum(out=sm[:, 0:1], in_=tmp[:, 0:64], axis=AX.X), reads=[B_tmp], writes=[B_sm])
        tr.op("dve", lambda e: e.reduce_sum(out=sm[:, 1:2], in_=tmp[:, 64:128], axis=AX.X), reads=[B_tmp], writes=[B_sm])
        tr.op("act", lambda e: e.activation(out=sm[:, 2:4], in_=sm[:, 0:2], func=AF.Exp), reads=[B_sm], writes=[B_sm])
        tr.op("dve", lambda e: e.tensor_tensor(out=sm[:, 4:5], in0=sm[:, 3:4], in1=sm[:, 2:3], op=ALU.subtract),
              reads=[B_sm], writes=[B_sm])
        tr.op("dve", lambda e: e.tensor_scalar_add(out=neglam[:], in0=sm[:, 4:5], scalar1=-LAMBDA_INIT),
              reads=[B_sm], writes=[B_neglam])
        tr.op("dve", lambda e: e.tensor_scalar_mul(out=gsub[:], in0=lamv[:, 0:128], scalar1=1.0 - LAMBDA_INIT),
              reads=[B_lamv], writes=[B_gsub])
        la = pp_sb[:, PP_LAA:PP_LAA + 16]
        t_ay, t_z, t_w, t_ln, t_d, t_rd, t_l1p, t_relu = (sm[:, 8 + 0:8 + 16], sm[:, 24:40], sm[:, 40:56],
                                                         tmp[:, 128:144], tmp[:, 144:160], tmp[:, 160:176],
                                                         tmp[:, 176:192], tmp[:, 192:208])
        tr.op("dve", lambda e: e.tensor_scalar_mul(out=t_ay, in0=la, scalar1=-1.0), reads=[B_pp], writes=[B_sm])
        tr.op("dve", lambda e: e.tensor_tensor(out=t_ay, in0=t_ay, in1=la, op=ALU.max), reads=[B_pp, B_sm], writes=[B_sm])
        tr.op("act", lambda e: e.activation(out=t_z, in_=t_ay, func=AF.Exp, scale=-1.0), reads=[B_sm], writes=[B_sm])
        tr.op("dve", lambda e: e.tensor_scalar_add(out=t_w, in0=t_z, scalar1=1.0), reads=[B_sm], writes=[B_sm])
        tr.op("act", lambda e: e.activation(out=t_ln, in_=t_w, func=AF.Ln), reads=[B_sm], writes=[B_tmp])
        tr.op("dve", lambda e: e.tensor_scalar(out=t_d, in0=t_w, scalar1=-1.0, scalar2=1e-30, op0=ALU.add, op1=ALU.max),
              reads=[B_sm], writes=[B_tmp])
        tr.op("dve", lambda e: e.reciprocal(out=t_rd, in_=t_d), reads=[B_tmp], writes=[B_tmp])
        tr.op("dve", lambda e: e.tensor_tensor(out=t_l1p, in0=t_ln, in1=t_z, op=ALU.mult), reads=[B_tmp, B_sm], writes=[B_tmp])
        tr.op("dve", lambda e: e.tensor_tensor(out=t_l1p, in0=t_l1p, in1=t_rd, op=ALU.mult), reads=[B_tmp], writes=[B_tmp])
        tr.op("dve", lambda e: e.tensor_scalar(out=t_relu, in0=la, scalar1=-1.0, scalar2=0.0, op0=ALU.mult, op1=ALU.max),
              reads=[B_pp], writes=[B_tmp])
        tr.op("dve", lambda e: e.tensor_tensor(out=t_l1p, in0=t_l1p, in1=t_relu, op=ALU.add), reads=[B_tmp], writes=[B_tmp])
        tr.op("dve", lambda e: e.tensor_scalar_mul(out=sc_a[:], in0=t_l1p, scalar1=-8.0), reads=[B_tmp], writes=[B_sca])
        tr.op("dve", lambda e: e.tensor_scalar_mul(out=sc_2a[:], in0=t_l1p, scalar1=-16.0), reads=[B_tmp], writes=[B_sc2a])
        tr.op("dve", lambda e: e.tensor_scalar_mul(out=hsc[:], in0=t_l1p, scalar1=-4.0), reads=[B_tmp], writes=[B_hsc])
        tr.op("dve", lambda e: e.tensor_scalar_mul(out=hbias[:], in0=pp_sb[:, PP_BAA:PP_BAA + 32], scalar1=0.5), reads=[B_pp], writes=[B_hb])

        wkv, B_wkv = load_w(ph, "wkv", xa_wkv, 2 * D, colblk=2048)
        wv_in = w_in.rearrange("(kc p) n -> p kc n", p=128)
        for b_ in (3, 0, 1, 2, 4, 5, 6):
            for kc in range(NCH):
                tr.dma("pool", win[:, kc, b_ * 1024:(b_ + 1) * 1024], wv_in[:, kc, b_ * 1024:(b_ + 1) * 1024], writes=[B_winb[b_]])
        MF = Rot(tr, ph, nc, "mf", [128, D], F32, 2)
        PST = Rot(tr, ph, nc, "p0pst", [128, 4, 128], F32, 2, psum=True)
        PSK = Rot(tr, ph, nc, "p0psk", [128, 512], F32, 3, psum=True)
        memT, B_memT = sb(tr, ph, nc, "memT", [128, NCH, NMEM], BF16)
        kmT, B_kmT = sb(tr, ph, nc, "kmT", [128, NCH, NMEM], BF16)
        vm, B_vm = sb(tr, ph, nc, "vm", [128, 2, 4, 257], BF16)
        tr.op("dve", lambda e: e.memset(vm[:], 1.0), writes=[B_vm])
        cp = 0
        for s in range(n_seq):
            for t in range(2):
                mf, B_mf = MF.next()
                tr.dma("sync", mf[:], mems[s, t * 128:(t + 1) * 128, :], chan=B_mf, writes=[B_mf])
                for hf in range(2):
                    pst, B_pst = PST.next()
                    for i in range(4):
                        kk = hf * 4 + i
                        tr.op("pe", lambda e, pst=pst, i=i, kk=kk, mf=mf: e.transpose(out=pst[:, i, :], in_=mf[:, kk * 128:(kk + 1) * 128], identity=identF[:]),
                              reads=[B_mf, B_identF], writes=[B_pst], flag=(i == 3))
                    eng = "act" if cp % 2 == 0 else "dve"
                    cp += 1
                    if eng == "act":
                        tr.op("act", lambda e, pst=pst, hf=hf, t=t: e.copy(out=memT[:, hf * 4:hf * 4 + 4, t * 128:(t + 1) * 128], in_=pst[:]),
                              reads=[B_pst], writes=[B_memT])
                    else:
                        tr.op("dve", lambda e, pst=pst, hf=hf, t=t: e.tensor_copy(out=memT[:, hf * 4:hf * 4 + 4, t * 128:(t + 1) * 128], in_=pst[:]),
                              reads=[B_pst], writes=[B_memT])
            for m in range(NCH):
                ps, B_ps = PSK.next()
                for k in range(NCH):
                    tr.op("pe", lambda e, ps=ps, k=k, m=m: e.matmul(ps[:, 0:NMEM], lhsT=wkv[:, k, m * 128:(m + 1) * 128], rhs=memT[:, k, :], start=(k == 0), stop=(k == NCH - 1)),
                          reads=[B_wkv, B_memT], writes=[B_ps], flag=(k == NCH - 1))
                if m % 2 == 0:
                    tr.op("act", lambda e, ps=ps, m=m: e.copy(out=kmT[:, m, :], in_=ps[:, 0:NMEM]), reads=[B_ps], writes=[B_kmT])
                else:
                    tr.op("dve", lambda e, ps=ps, m=m: e.tensor_copy(out=kmT[:, m, :], in_=ps[:, 0:NMEM]), reads=[B_ps], writes=[B_kmT])
            tr.dma("sync", kmT_s[s], kmT[:].rearrange("p c m -> p (c m)"), chan=B_kmT, reads=[B_kmT])
            for t in range(2):
                for hf in range(2):
                    ps, B_ps = PSK.next()
                    for k in range(NCH):
                        tr.op("pe", lambda e, ps=ps, k=k, t=t, hf=hf: e.matmul(ps[:], lhsT=memT[:, k, t * 128:(t + 1) * 128], rhs=wkv[:, k, D + hf * 512:D + (hf + 1) * 512], start=(k == 0), stop=(k == NCH - 1)),
                              reads=[B_wkv, B_memT], writes=[B_ps], flag=(k == NCH - 1))
                    src = ps[:].rearrange("p (h d) -> p h d", h=2)
                    if (t + hf) % 2 == 0:
                        tr.op("act", lambda e, src=src, t=t, hf=hf: e.copy(out=vm[:, t, hf * 2:hf * 2 + 2, 0:256], in_=src), reads=[B_ps], writes=[B_vm])
                    else:
                        tr.op("dve", lambda e, src=src, t=t, hf=hf: e.tensor_copy(out=vm[:, t, hf * 2:hf * 2 + 2, 0:256], in_=src), reads=[B_ps], writes=[B_vm])
            tr.dma("sync", vm_s[s], vm[:].rearrange("p t h d -> p (t h d)"), chan=B_vm, reads=[B_vm])
        if debug:
            tr.dma("sync", dbgc[:, 0:16], sc_a[:], chan=B_sca, reads=[B_sca])
            tr.dma("sync", dbgc[:, 16:32], hsc[:], chan=B_hsc, reads=[B_hsc])
            tr.dma("sync", dbgc[:, 32:64], hbias[:], chan=B_hb, reads=[B_hb])
            tr.dma("sync", dbgl, neglam[:], chan=B_neglam, reads=[B_neglam])
            tr.dma("sync", dbgc[:, 64:128], sm[:], chan=B_sm, reads=[B_sm])
            tr.dma("sync", dbg[7, :, 0:256], tmp[:], chan=B_tmp, reads=[B_tmp])
            tr.dma("sync", dbg[6, :, 0:384], lamv[:], chan=B_lamv, reads=[B_lamv])
        tr.barrier()

    if debug == "p0":
        return finish(nc, tr, es, y, N_own)

    with ExitStack() as ph:
        n_tiles128 = N_ctx // 128
        rope_sb, B_rope = sb(tr, ph, nc, "rope_sb", [128, n_tiles128, 16], F32)
        tr.dma("sync", rope_sb[:].rearrange("p t c -> p (t c)"), rope, chan=B_rope, writes=[B_rope])
        XF = Rot(tr, ph, nc, "xf", [128, D], F32, 3)
        XT = Rot(tr, ph, nc, "xT", [128, NCH, 512], BF16, 2)
        PST = Rot(tr, ph, nc, "p1pst", [128, 4, 128], F32, 2, psum=True)
        PSM = Rot(tr, ph, nc, "p1psm", [128, 512], F32, 4, psum=True)
        PTB = Rot(tr, ph, nc, "p1ptb", [128, 8, 128], BF16, 2, psum=True)
        QTM = Rot(tr, ph, nc, "qtm", [128, 512], BF16, 3)
        RT = Rot(tr, ph, nc, "ropet", [128, 4, 64], F32, 2)
        QTS = Rot(tr, ph, nc, "qts", [128, 4, 512], BF16, 2)
        VTM = Rot(tr, ph, nc, "vtm", [128, D], BF16, 2)
        STF = Rot(tr, ph, nc, "stf", [128, 512], F32, 3)
        STB = Rot(tr, ph, nc, "stb", [128, 512], BF16, 4)
        cpi = [0]

        def evac_copy(out, in_, rb, wb):
            cpi[0] += 1
            if cpi[0] % 2 == 0:
                tr.op("act", lambda e: e.copy(out=out, in_=in_), reads=rb, writes=wb)
            else:
                tr.op("dve", lambda e: e.tensor_copy(out=out, in_=in_), reads=rb, writes=wb)

        def tm_rope_group(xT, B_xT, col0, dst, tok_own0, tile128_0):
            its = [(hf, j) for hf in range(2) for j in range(4)]
            state = {}
            qts_of = {}

            def stA(i):
                hf, j = its[i]
                ps, B_ps = PSM.next()
                for k in range(NCH):
                    tr.op("pe", lambda e, k=k: e.matmul(ps[:], lhsT=xT[:, k, j * 128:(j + 1) * 128], rhs=win[:, k, col0 + hf * 512:col0 + (hf + 1) * 512], start=(k == 0), stop=(k == NCH - 1)),
                          reads=[B_xT, B_winb[col0 // 1024]], writes=[B_ps], flag=(k == NCH - 1))
                qtm, B_qtm = QTM.next()
                tr.op("act", lambda e: e.copy(out=qtm[:], in_=ps[:]), reads=[B_ps], writes=[B_qtm])
                ps3 = ps[:].rearrange("p (g d) -> p g d", g=8)
                q3 = qtm[:].rearrange("p (g d) -> p g d", g=8)
                x1, x2 = ps3[:, :, 0:8], ps3[:, :, 8:16]
                ti = tile128_0 + j
                cos = rope_sb[:, ti, 0:8].unsqueeze(1).broadcast_to([128, 8, 8])
                sin = rope_sb[:, ti, 8:16].unsqueeze(1).broadcast_to([128, 8, 8])
                rt, B_rt = RT.next()
                r3 = rt[:].rearrange("p a (g d) -> p a g d", g=8)
                tr.op("dve", lambda e: e.tensor_tensor(out=r3[:, 0], in0=x1, in1=cos, op=ALU.mult), reads=[B_ps, B_rope], writes=[B_rt])
                tr.op("dve", lambda e: e.tensor_tensor(out=r3[:, 1], in0=x2, in1=sin, op=ALU.mult), reads=[B_ps, B_rope], writes=[B_rt])
                tr.op("dve", lambda e: e.tensor_tensor(out=r3[:, 2], in0=x2, in1=cos, op=ALU.mult), reads=[B_ps, B_rope], writes=[B_rt])
                tr.op("dve", lambda e: e.tensor_tensor(out=r3[:, 3], in0=x1, in1=sin, op=ALU.mult), reads=[B_ps, B_rope], writes=[B_rt])
                tr.op("dve", lambda e: e.tensor_tensor(out=q3[:, :, 0:8], in0=r3[:, 0], in1=r3[:, 1], op=ALU.subtract), reads=[B_rt], writes=[B_qtm])
                tr.op("dve", lambda e: e.tensor_tensor(out=q3[:, :, 8:16], in0=r3[:, 2], in1=r3[:, 3], op=ALU.add), reads=[B_rt], writes=[B_qtm])
                state[i] = (qtm, B_qtm)

            def stC(i):
                hf, j = its[i]
                if j == 0:
                    qts_of[hf] = QTS.next()
                qts, B_qts = qts_of[hf]
                qtm, B_qtm = state.pop(i)
                ptb, B_ptb = PTB.next()
                for t in range(4):
                    tr.op("pe", lambda e, t=t: e.transpose(out=ptb[:, t, :], in_=qtm[:, t * 128:(t + 1) * 128], identity=identB[:]),
                          reads=[B_qtm, B_identB], writes=[B_ptb], flag=(t == 3))
                evac_copy(qts[:, :, j * 128:(j + 1) * 128], ptb[:, 0:4, :], [B_ptb], [B_qts])
                if j == 3:
                    tr.dma("sync", dst[hf * 4:hf * 4 + 4, :, tok_own0:tok_own0 + 512].rearrange("h p t -> p h t"), qts[:], reads=[B_qts])

            n = len(its)
            for i in range(n + 2):
                if i < n:
                    stA(i)
                if i >= 2:
                    stC(i - 2)

        def fm_group(xT, B_xT, col0, dst, tok0, kind):
            for m in range(NCH):
                ps, B_ps = PSM.next()
                for k in range(NCH):
                    tr.op("pe", lambda e, ps=ps, k=k, m=m: e.matmul(ps[:], lhsT=win[:, k, col0 + m * 128:col0 + (m + 1) * 128], rhs=xT[:, k, :], start=(k == 0), stop=(k == NCH - 1)),
                          reads=[B_xT, B_winb[col0 // 1024]], writes=[B_ps], flag=(k == NCH - 1))
                if kind == "f32":
                    st, B_st = STF.next()
                    evac_copy(st[:], ps[:], [B_ps], [B_st])
                else:
                    st, B_st = STB.next()
                    fn = AF.Gelu_apprx_tanh if kind == "gelu" else AF.Sigmoid
                    tr.op("act", lambda e, st=st, ps=ps, fn=fn: e.activation(out=st[:], in_=ps[:], func=fn), reads=[B_ps], writes=[B_st])
                tr.dma("sync", dst[m, :, tok0:tok0 + 512], st[:], chan=B_st, reads=[B_st])

        for s in range(n_seq):
            T_ctx, T_own = seqs[s]
            for tt in range(T_ctx // 512):
                own = tt * 512 < T_own
                tok0 = ctx_off[s] + tt * 512
                tokown0 = own_off[s] + tt * 512
                xT, B_xT = XT.next()
                for j in range(4):
                    xf, B_xf = XF.next()
                    tr.dma("sync", xf[:], xs[tok0 + j * 128:tok0 + (j + 1) * 128, :], chan=B_xf, writes=[B_xf])
                    for hf in range(2):
                        pst, B_pst = PST.next()
                        for i in range(4):
                            kk = hf * 4 + i
                            tr.op("pe", lambda e, pst=pst, i=i, kk=kk, xf=xf: e.transpose(out=pst[:, i, :], in_=xf[:, kk * 128:(kk + 1) * 128], identity=identF[:]),
                                  reads=[B_xf, B_identF], writes=[B_pst], flag=(i == 3))
                        evac_copy(xT[:, hf * 4:hf * 4 + 4, j * 128:(j + 1) * 128], pst[:], [B_pst], [B_xT])
                ONLY = ("xr", "q", "k", "v", "gy", "ga", "gl")
                if "xr" in ONLY:
                    fm_group(xT, B_xT, 3 * D, xr_s, tok0, "f32")
                if own and "q" in ONLY:
                    tm_rope_group(xT, B_xT, 0, qT_s, tokown0, tok0 // 128)
                if "k" in ONLY:
                    tm_rope_group(xT, B_xT, D, kT_s, tok0, tok0 // 128)
                for j in (range(4) if "v" in ONLY else []):
                    vtm, B_vtm = VTM.next()
                    for hf in range(2):
                        ps, B_ps = PSM.next()
                        for k in range(NCH):
                            tr.op("pe", lambda e, ps=ps, k=k, j=j, hf=hf: e.matmul(ps[:], lhsT=xT[:, k, j * 128:(j + 1) * 128], rhs=win[:, k, 2 * D + hf * 512:2 * D + (hf + 1) * 512], start=(k == 0), stop=(k == NCH - 1)),
                                  reads=[B_xT, B_winb[2]], writes=[B_ps], flag=(k == NCH - 1))
                        evac_copy(vtm[:, hf * 512:(hf + 1) * 512], ps[:], [B_ps], [B_vtm])
                    t0 = tok0 + j * 128
                    tr.dma("sync", v_s[:, t0:t0 + 128, :].rearrange("h p e -> p h e"), vtm[:].rearrange("p (h e) -> p h e", h=8), chan=B_vtm, reads=[B_vtm])
                if own and "gy" in ONLY:
                    fm_group(xT, B_xT, 4 * D, gy_s, tokown0, "gelu")
                if own and "ga" in ONLY:
                    fm_group(xT, B_xT, 5 * D, ga_s, tokown0, "sig")
                if own and "gl" in ONLY:
                    fm_group(xT, B_xT, 6 * D, gl_s, tokown0, "sig")
        tr.barrier()

    for b_ in B_winb:
        b_.span = False
    es_win.close()
    if debug == "p1":
        return finish(nc, tr, es, y, N_own)

    with ExitStack() as ph:
        GRP = 1024
        CP = 2048
        maxT = max(tc for tc, _ in seqs)
        maxO = max(to for _, to in seqs)
        lw, B_lw = sb(tr, ph, nc, "lw", [128, 4, NCH, 128], BF16)
        for a_ in range(4):
            tr.dma("pool", lw[:, a_], lru_w[a_].rearrange("c d e -> d c e"), writes=[B_lw])
        XR = Rot(tr, ph, nc, "xr", [128, maxT + 4], F32, 1)
        xc = ph.enter_context(nc.sbuf_tensor("xc", [128, maxT], F32))
        xcb = ph.enter_context(nc.sbuf_tensor("xcb", [128, maxT], BF16))
        npc = (maxT + CP - 1) // CP
        B_xcp = [tr.buf(f"xc{i}") for i in range(npc)]
        B_xcbp = [tr.buf(f"xcb{i}") for i in range(npc)]
        hB, B_hB = sb(tr, ph, nc, "hB", [128, maxO], F32)
        carry, B_carry = sb(tr, ph, nc, "carry", [128, 2], F32)
        PSG = Rot(tr, ph, nc, "psg", [128, 512], F32, 6, psum=True)
        g = max(min(GRP, to) for _, to in seqs)
        RT_ = Rot(tr, ph, nc, "r_t", [128, g], F32, 2)
        IT_ = Rot(tr, ph, nc, "i_t", [128, g], F32, 4)
        AT_ = Rot(tr, ph, nc, "a_t", [128, g], F32, 4)
        MT_ = Rot(tr, ph, nc, "m_t", [128, g], F32, 4)
        MX_ = Rot(tr, ph, nc, "mx_t", [128, g], F32, 4)
        HA_ = Rot(tr, ph, nc, "hA", [128, g], F32, 2)
        GY = Rot(tr, ph, nc, "gyt", [128, g], BF16, 2)
        LO = Rot(tr, ph, nc, "lot", [128, g], BF16, 2)
        for (xr, B_xr) in XR.items:
            tr.op("pool", lambda e, xr=xr: e.memset(xr[:], 0.0), writes=[B_xr])

        for s in range(n_seq):
            T_ctx, T_own = seqs[s]
            gs = min(GRP, T_own)
            cp = min(CP, T_ctx)
            for c in range(NCH):
                xr, B_xr = XR.next()
                tr.dma("sync", xr[:, 2:2 + T_ctx], xr_s[c, :, ctx_off[s]:ctx_off[s] + T_ctx], writes=[B_xr])
                if T_ctx < maxT:
                    tr.op("pool", lambda e: e.memset(xr[:, 2 + T_ctx:4 + T_ctx], 0.0), writes=[B_xr])
                tp = PP_TAP + c * 5
                for pi in range(T_ctx // cp - 1, -1, -1):
                    p0 = pi * cp
                    tr.op("pool", lambda e: e.tensor_scalar(out=xc[:, p0:p0 + cp], in0=xr[:, p0:p0 + cp], scalar1=pp_sb[:, tp:tp + 1], scalar2=pp_sb[:, PP_CB + c:PP_CB + c + 1], op0=ALU.mult, op1=ALU.add),
                          reads=[B_xr, B_pp], writes=[B_xcp[pi]])
                    for jj in range(1, 5):
                        tr.op("dve", lambda e: e.scalar_tensor_tensor(out=xc[:, p0:p0 + cp], in0=xr[:, p0 + jj:p0 + jj + cp], scalar=pp_sb[:, tp + jj:tp + jj + 1], in1=xc[:, p0:p0 + cp], op0=ALU.mult, op1=ALU.add),
                              reads=[B_xr, B_pp, B_xcp[pi]], writes=[B_xcp[pi]])
                    tr.op("act", lambda e: e.copy(out=xcb[:, p0:p0 + cp], in_=xc[:, p0:p0 + cp]), reads=[B_xcp[pi]], writes=[B_xcbp[pi]])

                ngB = T_ctx // gs
                jobs = [(1, gi * gs, gi == ngB - 1, gi) for gi in range(ngB - 1, -1, -1)]
                jobs += [(0, gi * gs, gi == 0, gi) for gi in range(T_own // gs)]
                n = gs

                def stage_a(job):
                    d, t0, first, gi = job
                    pi = t0 // cp
                    col = 0 if d == 0 else 16
                    colx = 8 if d == 0 else 24
                    r_t, B_r = RT_.next()
                    i_t, B_i = IT_.next()
                    a_t, B_a = AT_.next()
                    m_t, B_m = MT_.next()
                    for sub in range(n // 512):
                        c0 = sub * 512
                        psr, B_psr = PSG.next()
                        tr.op("pe", lambda e: e.matmul(psr[:], lhsT=lw[:, 2 * d, c, :], rhs=xcb[:, t0 + c0:t0 + c0 + 512], start=True, stop=True),
                              reads=[B_lw, B_xcbp[pi]], writes=[B_psr])
                        psi, B_psi = PSG.next()
                        tr.op("pe", lambda e: e.matmul(psi[:], lhsT=lw[:, 2 * d + 1, c, :], rhs=xcb[:, t0 + c0:t0 + c0 + 512], start=True, stop=True),
                              reads=[B_lw, B_xcbp[pi]], writes=[B_psi])
                        tr.op("act", lambda e: e.activation(out=r_t[:, c0:c0 + 512], in_=psr[:], func=AF.Tanh, scale=0.5, bias=hbias[:, col + c:col + c + 1]),
                              reads=[B_psr, B_hb], writes=[B_r])
                        tr.op("act", lambda e: e.activation(out=i_t[:, c0:c0 + 512], in_=psi[:], func=AF.Tanh, scale=0.5, bias=hbias[:, colx + c:colx + c + 1]),
                              reads=[B_psi, B_hb], writes=[B_i])
                    sc = sc_a[:, d * 8 + c:d * 8 + c + 1]
                    hs = hsc[:, d * 8 + c:d * 8 + c + 1]
                    tr.op("act", lambda e: e.activation(out=a_t[:, 0:n], in_=r_t[:, 0:n], func=AF.Exp, scale=hs, bias=hs), reads=[B_r, B_hsc], writes=[B_a])
                    tr.op("act", lambda e: e.activation(out=m_t[:, 0:n], in_=r_t[:, 0:n], func=AF.Exp, scale=sc, bias=sc), reads=[B_r, B_sca], writes=[B_m])
                    tr.op("dve", lambda e: e.tensor_scalar(out=m_t[:, 0:n], in0=m_t[:, 0:n], scalar1=0.9999999, scalar2=-1.0, op0=ALU.min, op1=ALU.mult), reads=[B_m], writes=[B_m])
                    return (job, i_t, B_i, a_t, B_a, m_t, B_m)

                def stage_s(st):
                    job, i_t, B_i, a_t, B_a, m_t, B_m = st
                    tr.op("act", lambda e: e.activation(out=m_t[:, 0:n], in_=m_t[:, 0:n], func=AF.Sqrt, scale=0.25, bias=0.25), reads=[B_m], writes=[B_m])

                def stage_r(st):
                    job, i_t, B_i, a_t, B_a, m_t, B_m = st
                    d, t0, first, gi = job
                    pi = t0 // cp
                    mx, B_mx = MX_.next()
                    tr.op("dve", lambda e: e.tensor_tensor(out=mx[:, 0:n], in0=m_t[:, 0:n], in1=xc[:, t0:t0 + n], op=ALU.mult), reads=[B_m, B_xcp[pi]], writes=[B_mx])
                    if first:
                        k0 = n - 1 if d == 1 else 0
                        tr.op("dve", lambda e: e.tensor_scalar(out=mx[:, k0:k0 + 1], in0=xc[:, t0 + k0:t0 + k0 + 1], scalar1=0.5, scalar2=None, op0=ALU.mult), reads=[B_xcp[pi], B_mx], writes=[B_mx])
                    tr.op("dve", lambda e: e.scalar_tensor_tensor(out=mx[:, 0:n], in0=i_t[:, 0:n], scalar=1.0, in1=mx[:, 0:n], op0=ALU.add, op1=ALU.mult), reads=[B_i, B_mx], writes=[B_mx])
                    if d == 1:
                        own_g = t0 < T_own
                        if own_g:
                            dst, B_dst = hB[:, t0:t0 + n], B_hB
                        else:
                            hA, B_hA = HA_.next()
                            dst, B_dst = hA[:, 0:n], B_hA
                        init = 0.0 if first else carry[:, 1:2]
                        tr.op("dve", lambda e: e.tensor_tensor_scan(out=dst[:, ::-1], data0=a_t[:, n - 1::-1], data1=mx[:, n - 1::-1], initial=init, op0=ALU.mult, op1=ALU.add),
                              reads=[B_a, B_mx, B_carry], writes=[B_dst])
                        if gi > 0:
                            tr.op("dve", lambda e: e.tensor_copy(out=carry[:, 1:2], in_=dst[:, 0:1]), reads=[B_dst], writes=[B_carry])
                    else:
                        hA, B_hA = HA_.next()
                        gyt, B_gy = GY.next()
                        tr.dma("sync", gyt[:, 0:n], gy_s[c, :, own_off[s] + t0:own_off[s] + t0 + n], writes=[B_gy])
                        init = 0.0 if first else carry[:, 0:1]
                        tr.op("dve", lambda e: e.tensor_tensor_scan(out=hA[:, 0:n], data0=a_t[:, 0:n], data1=mx[:, 0:n], initial=init, op0=ALU.mult, op1=ALU.add),
                              reads=[B_a, B_mx, B_carry], writes=[B_hA])
                        tr.op("dve", lambda e: e.tensor_copy(out=carry[:, 0:1], in_=hA[:, n - 1:n]), reads=[B_hA], writes=[B_carry])
                        tr.op("pool", lambda e: e.tensor_tensor(out=hA[:, 0:n], in0=hA[:, 0:n], in1=hB[:, t0:t0 + n], op=ALU.add), reads=[B_hA, B_hB], writes=[B_hA])
                        lot, B_lo = LO.next()
                        tr.op("pool", lambda e: e.tensor_tensor(out=lot[:, 0:n], in0=hA[:, 0:n], in1=gyt[:, 0:n], op=ALU.mult), reads=[B_hA, B_gy], writes=[B_lo])
                        tr.dma("pool", lo_s[c, :, own_off[s] + t0:own_off[s] + t0 + n], lot[:, 0:n], reads=[B_lo])

                for p0_ in range(0, len(jobs), 2):
                    pair = jobs[p0_:p0_ + 2]
                    sts = [stage_a(j) for j in pair]
                    for st in sts:
                        stage_s(st)
                    for st in sts:
                        stage_r(st)
        tr.barrier()

    if debug == "p2":
        return finish(nc, tr, es, y, N_own)

    es_w5 = ExitStack()
    wq, B_wq = load_w(es_w5, "wq", xa_wq, D, issue=False)
    wo, B_wo = load_w(es_w5, "wo", xa_wo, D, issue=False, own=True)
    es_w4 = ExitStack()
    wpa, B_wpa = load_w(es_w4, "wpa", p_attn, D, issue=False)
    wpl, B_wpl = load_w(es_w4, "wpl", p_lru, D, issue=False)
    wmx, B_wmx = load_w(es_w4, "wmx", w_mix, D, issue=False, own=True)

    with ExitStack() as ph:
        issue_w(wpa, B_wpa, p_attn, D)
        issue_w(wpl, B_wpl, p_lru, D)
        issue_w(wmx, B_wmx, w_mix, D)
        maxT = max(tc for tc, _ in seqs)
        maxO = max(to for _, to in seqs)
        KT = Rot(tr, ph, nc, "kTh", [128, maxT], BF16, 2)
        VH = Rot(tr, ph, nc, "vh", [128, maxT // 128, 129], BF16, 2)
        QT = Rot(tr, ph, nc, "qTh", [128, maxO], BF16, 2)
        PSS = Rot(tr, ph, nc, "pss", [128, 2, 512], F32, 2, psum=True)
        ACCF = [ph.enter_context(nc.psum_tensor(f"acc{i}", [128, 512], F32)) for i in range(3)]
        ACC = [a_[:, 0:387].rearrange("p (s e) -> p s e", s=3) for a_ in ACCF]
        B_ACC = [tr.buf(f"acc{i}", excl=True) for i in range(3)]
        PT3 = Rot(tr, ph, nc, "p3pt", [128, 8, 128], BF16, 1, psum=True)
        PEX = Rot(tr, ph, nc, "pexp", [128, 2, 512], BF16, 3)
        OSB = Rot(tr, ph, nc, "osb", [128, 8, 129], F32, 2)
        EPO = Rot(tr, ph, nc, "epo", [128, 4, 128], F32, 2)
        EPT = Rot(tr, ph, nc, "ept", [128, 4, 128], F32, 2)
        EPS = Rot(tr, ph, nc, "eps", [128, 32], F32, 2)
        AOB = Rot(tr, ph, nc, "aob", [128, 4, 128], BF16, 2)
        AOT = Rot(tr, ph, nc, "aot", [128, 512], BF16, 2)
        for (vh, B_vh) in VH.items:
            tr.op("pool", lambda e, vh=vh: e.memset(vh[:], 1.0), writes=[B_vh])
        accmap = [(0, 0), (0, 1), (0, 2), (1, 0), (1, 1), (1, 2), (2, 0), (2, 1)]
        gsub_b = gsub[:].unsqueeze(1).broadcast_to([128, 4, 128])

        def issue_loads(s_, h_):
            T_c, T_o = seqs[s_]
            nk = T_c // 128
            kT, B_kT = KT.next()
            vh, B_vh = VH.next()
            qT, B_qT = QT.next()
            tr.dma("sync", kT[:, 0:T_c], kT_s[h_, :, ctx_off[s_]:ctx_off[s_] + T_c], writes=[B_kT])
            vsrc = v_s[h_, ctx_off[s_]:ctx_off[s_] + T_c, :].rearrange("(k p) e -> p k e", p=128)
            for k0 in range(0, nk, 8):
                k1 = min(nk, k0 + 8)
                tr.dma("sync", vh[:, k0:k1, 0:128], vsrc[:, k0:k1, :], writes=[B_vh])
            tr.dma("sync", qT[:, 0:T_o], qT_s[h_, :, own_off[s_]:own_off[s_] + T_o], writes=[B_qT])
            return (kT, B_kT, vh, B_vh, qT, B_qT)

        loaded = [issue_loads(0, 0)]
        pending = []

        for s in range(n_seq):
            T_ctx, T_own = seqs[s]
            nkt = T_ctx // 128
            for h in range(NCH):
                kT, B_kT, vh, B_vh, qT, B_qT = loaded.pop(0)
                nidx = s * NCH + h + 1
                if nidx < n_seq * NCH:
                    loaded.append(issue_loads(nidx // NCH, nidx % NCH))
                for qt in range(T_own // 512):
                    q0 = qt * 512

                    def scores(kt):
                        ps, B_ps = PSS.next()
                        for c in range(2):
                            tr.op("pe", lambda e, ps=ps, c=c, kt=kt: e.matmul(ps[:, c, :], lhsT=kT[c * 64:(c + 1) * 64, kt * 128:(kt + 1) * 128], rhs=qT[c * 64:(c + 1) * 64, q0:q0 + 512], start=True, stop=True),
                                  reads=[B_kT, B_qT], writes=[B_ps], flag=(c == 1))
                        return ps, B_ps

                    nxt = scores(0)
                    for kt in range(nkt):
                        ps, B_ps = nxt
                        if kt + 1 < nkt:
                            nxt = scores(kt + 1)
                        if pending and kt == min(10, nkt - 1):
                            pending.pop(0)()
                        pe_, B_pe = PEX.next()
                        tr.op("act", lambda e, pe_=pe_, ps=ps: e.activation(out=pe_[:], in_=ps[:], func=AF.Exp, scale=0.125), reads=[B_ps], writes=[B_pe])
                        last = (kt == nkt - 1)
                        for c in range(2):
                            for j in range(4):
                                bk, sl = accmap[c * 4 + j]
                                tr.op("pe", lambda e, pe_=pe_, c=c, j=j, bk=bk, sl=sl, kt=kt, last=last: e.matmul(ACC[bk][:, sl, :], lhsT=pe_[:, c, j * 128:(j + 1) * 128], rhs=vh[:, kt, :], start=(kt == 0 and sl == 0), stop=last, skip_group_check=True),
                                      reads=[B_pe, B_vh], writes=[B_ACC[bk]], flag=(last and (c * 4 + j) in (2, 5, 7)))
                    osb, B_osb = OSB.next()
                    for bk, (lo_, hi_) in enumerate(((0, 3), (3, 6), (6, 8))):
                        tr.op("dve", lambda e, bk=bk, lo_=lo_, hi_=hi_: e.tensor_copy(out=osb[:, lo_:hi_, :], in_=ACC[bk][:, 0:hi_ - lo_, :]), reads=[B_ACC[bk]], writes=[B_osb])
                    o4 = osb[:].rearrange("p (c j) e -> p c j e", c=2)
                    ep, B_ep = EPS.next()
                    epo, B_epo = EPO.next()
                    ept, B_ept = EPT.next()
                    rc = ep[:, 0:8].rearrange("p (c j) -> p c j", c=2)
                    tr.op("dve", lambda e: e.reciprocal(out=rc, in_=o4[:, :, :, 128]), reads=[B_osb], writes=[B_ep])
                    tr.op("dve", lambda e: e.tensor_scalar(out=ep[:, 8:12], in0=rc[:, 1, :], scalar1=neglam[:, 0:1], scalar2=None, op0=ALU.mult), reads=[B_ep, B_neglam], writes=[B_ep])
                    r0b = rc[:, 0, :].unsqueeze(2).broadcast_to([128, 4, 128])
                    r1b = ep[:, 8:12].unsqueeze(2).broadcast_to([128, 4, 128])
                    tr.op("pool", lambda e: e.tensor_tensor(out=epo[:], in0=o4[:, 0, :, 0:128], in1=r0b, op=ALU.mult), reads=[B_osb, B_ep], writes=[B_epo])
                    tr.op("pool", lambda e: e.tensor_tensor(out=ept[:], in0=o4[:, 1, :, 0:128], in1=r1b, op=ALU.mult), reads=[B_osb, B_ep], writes=[B_ept])
                    tr.op("pool", lambda e: e.tensor_tensor(out=epo[:], in0=epo[:], in1=ept[:], op=ALU.add), reads=[B_epo, B_ept], writes=[B_epo])
                    tr.op("pool", lambda e: e.tensor_tensor(out=ept[:], in0=epo[:], in1=epo[:], op=ALU.mult), reads=[B_epo], writes=[B_ept])
                    tr.op("dve", lambda e: e.reduce_sum(out=ep[:, 12:16], in_=ept[:], axis=AX.X), reads=[B_ept], writes=[B_ep])
                    tr.op("dve", lambda e: e.tensor_scalar(out=ep[:, 16:20], in0=ep[:, 12:16], scalar1=1.0 / 128.0, scalar2=SUBLN_EPS, op0=ALU.mult, op1=ALU.add), reads=[B_ep], writes=[B_ep])
                    tr.op("pool", lambda e: e.tensor_tensor(out=ep[:, 20:24], in0=ep[:, 16:20], in1=nhalf[:, 0:1].broadcast_to([128, 4]), op=ALU.pow), reads=[B_ep, B_nhalf], writes=[B_ep])
                    rsb = ep[:, 20:24].unsqueeze(2).broadcast_to([128, 4, 128])
                    tr.op("pool", lambda e: e.tensor_tensor(out=epo[:], in0=epo[:], in1=rsb, op=ALU.mult), reads=[B_epo, B_ep], writes=[B_epo])
                    aob, B_aob = AOB.next()
                    tr.op("pool", lambda e: e.tensor_tensor(out=aob[:], in0=epo[:], in1=gsub_b, op=ALU.mult), reads=[B_epo, B_gsub], writes=[B_aob])
                    def tail(aob=aob, B_aob=B_aob, dst=ao_s[h, :, own_off[s] + q0:own_off[s] + q0 + 512]):
                        aot, B_aot = AOT.next()
                        pt3, B_pt3 = PT3.next()
                        for j in range(4):
                            tr.op("pe", lambda e, j=j: e.transpose(out=pt3[:, j, :], in_=aob[:, j, :], identity=identB[:]), reads=[B_aob, B_identB], writes=[B_pt3], flag=(j == 3))
                        tr.op("dve", lambda e: e.tensor_copy(out=aot[:], in_=pt3[:, 0:4, :].rearrange("p j t -> p (j t)")), reads=[B_pt3], writes=[B_aot])
                        tr.dma("pool", dst, aot[:], reads=[B_aot])
                    pending.append(tail)
        while pending:
            pending.pop(0)()
        tr.barrier()

    if debug == "p3":
        return finish(nc, tr, es, y, N_own)

    def ln_tm(ST, lnp, B_lnp, gi, src_ps, B_src, resid, B_res, dst, B_dst, pool_tail=False):
        st, B_st = ST.next()
        d2 = dst[:].rearrange("p (a d) -> p a d", a=2)
        r2 = resid[:].rearrange("p (a d) -> p a d", a=2)
        tr.op("dve", lambda e: e.scalar_tensor_tensor(out=d2, in0=r2, scalar=ALPHA, in1=src_ps[:], op0=ALU.mult, op1=ALU.add),
              reads=[B_res, B_src], writes=[B_dst])
        tr.op("dve", lambda e: e.bn_stats(out=st[:, 0:6], in_=dst[:, 0:512]), reads=[B_dst], writes=[B_st])
        tr.op("dve", lambda e: e.bn_stats(out=st[:, 6:12], in_=dst[:, 512:1024]), reads=[B_dst], writes=[B_st])
        tr.op("dve", lambda e: e.bn_aggr(out=st[:, 12:14], in_=st[:, 0:12]), reads=[B_st], writes=[B_st])
        tr.op("dve", lambda e: e.tensor_scalar_add(out=st[:, 15:16], in0=st[:, 13:14], scalar1=LN_EPS), reads=[B_st], writes=[B_st])
        tr.op("pool", lambda e: e.tensor_tensor(out=st[:, 14:15], in0=st[:, 15:16], in1=nhalf[:, 0:1], op=ALU.pow), reads=[B_st, B_nhalf], writes=[B_st])
        if pool_tail:
            tr.op("pool", lambda e: e.tensor_scalar(out=dst[:], in0=dst[:], scalar1=st[:, 12:13], scalar2=st[:, 14:15], op0=ALU.subtract, op1=ALU.mult),
                  reads=[B_st, B_dst], writes=[B_dst])
            tr.op("pool", lambda e: e.tensor_tensor(out=dst[:], in0=dst[:], in1=lnp[:, gi, :], op=ALU.mult), reads=[B_lnp, B_dst], writes=[B_dst])
            tr.op("pool", lambda e: e.tensor_tensor(out=dst[:], in0=dst[:], in1=lnp[:, gi + 1, :], op=ALU.add), reads=[B_lnp, B_dst], writes=[B_dst])
            return
        tr.op("dve", lambda e: e.scalar_tensor_tensor(out=dst[:], in0=dst[:], scalar=st[:, 12:13], in1=lnp[:, gi, :], op0=ALU.subtract, op1=ALU.mult),
              reads=[B_st, B_lnp, B_dst], writes=[B_dst])
        tr.op("dve", lambda e: e.scalar_tensor_tensor(out=dst[:], in0=dst[:], scalar=st[:, 14:15], in1=lnp[:, gi + 1, :], op0=ALU.mult, op1=ALU.add),
              reads=[B_st, B_lnp, B_dst], writes=[B_dst])

    x1_s = dscr("x1_s", [N_own, D], F32)
    TA = 512
    with ExitStack() as ph:
        issue_w(wq, B_wq, xa_wq, D)
        issue_w(wo, B_wo, xa_wo, D)
        lnp, B_lnp = sb(tr, ph, nc, "lnp", [128, 2, D], F32)
        tr.dma("sync", lnp[:].rearrange("p a d -> p (a d)"), bc[:, 0:2 * D], writes=[B_lnp])
        AO = Rot(tr, ph, nc, "aoT", [128, NCH, TA], BF16, 2)
        LOo = Rot(tr, ph, nc, "loT", [128, NCH, TA], BF16, 2)
        GA = Rot(tr, ph, nc, "gaT", [128, NCH, TA], BF16, 2)
        GL = Rot(tr, ph, nc, "glT", [128, NCH, TA], BF16, 2)
        XIN = Rot(tr, ph, nc, "xin", [128, D], F32, 3)
        MG = Rot(tr, ph, nc, "mgT", [128, NCH, TA], BF16, 2)
        X1 = Rot(tr, ph, nc, "x1", [128, D], F32, 3)
        TMP = Rot(tr, ph, nc, "t4", [128, TA], F32, 4)
        ST = Rot(tr, ph, nc, "st4", [128, 16], F32, 4)
        PSA = Rot(tr, ph, nc, "psa", [128, 512], F32, 4, psum=True)
        PSO = Rot(tr, ph, nc, "pso", [128, 2, 512], F32, 2, psum=True)
        tiles = [(s, tt) for s in range(n_seq) for tt in range(seqs[s][1] // TA)]

        def loads(idx):
            s, tt = tiles[idx]
            o0 = own_off[s] + tt * TA
            r = []
            for R_, src in ((AO, ao_s), (LOo, lo_s), (GA, ga_s), (GL, gl_s)):
                t, B = R_.next()
                tr.dma("sync", t[:], src[:, :, o0:o0 + TA].rearrange("c p t -> p c t"), writes=[B])
                r.append((t, B))
            return r

        def s1_chunks(ld, mgT, B_mgT, ms):
            (aoT, B_ao), (loT, B_lo), (gaT, B_ga), (glT, B_gl) = ld
            for m in ms:
                psa, B_psa = PSA.next()
                psl, B_psl = PSA.next()
                for k in range(NCH):
                    tr.op("pe", lambda e, k=k: e.matmul(psa[:, 0:TA], lhsT=wpa[:, k, m * 128:(m + 1) * 128], rhs=aoT[:, k, :], start=(k == 0), stop=(k == NCH - 1)),
                          reads=[B_wpa, B_ao], writes=[B_psa], flag=(k == NCH - 1))
                for k in range(NCH):
                    tr.op("pe", lambda e, k=k: e.matmul(psl[:, 0:TA], lhsT=wpl[:, k, m * 128:(m + 1) * 128], rhs=loT[:, k, :], start=(k == 0), stop=(k == NCH - 1)),
                          reads=[B_wpl, B_lo], writes=[B_psl], flag=(k == NCH - 1))
                t4, B_t4 = TMP.next()
                tr.op("act", lambda e: e.copy(out=t4[:], in_=psa[:, 0:TA]), reads=[B_psa], writes=[B_t4])
                t5, B_t5 = TMP.next()
                tr.op("act", lambda e: e.copy(out=t5[:], in_=psl[:, 0:TA]), reads=[B_psl], writes=[B_t5])
                tr.op("pool", lambda e: e.tensor_tensor(out=t4[:], in0=t4[:], in1=gaT[:, m, :], op=ALU.mult), reads=[B_t4, B_ga], writes=[B_t4])
                tr.op("pool", lambda e: e.tensor_tensor(out=t5[:], in0=t5[:], in1=glT[:, m, :], op=ALU.mult), reads=[B_t5, B_gl], writes=[B_t5])
                tr.op("pool", lambda e: e.tensor_tensor(out=mgT[:, m, :], in0=t4[:], in1=t5[:], op=ALU.add), reads=[B_t4, B_t5], writes=[B_mgT])

        def s2_sub(idx, mgT, B_mgT, j):
            s, tt = tiles[idx]
            o0 = own_off[s] + tt * TA
            c0x = ctx_off[s] + tt * TA
            xin, B_xin = XIN.next()
            tr.dma("sync", xin[:], xs[c0x + j * 128:c0x + (j + 1) * 128, :], writes=[B_xin])
            pso, B_pso = PSO.next()
            for hf in range(2):
                for k in range(NCH):
                    tr.op("pe", lambda e, k=k, hf=hf: e.matmul(pso[:, hf, :], lhsT=mgT[:, k, j * 128:(j + 1) * 128], rhs=wmx[:, k, hf * 512:(hf + 1) * 512], start=(k == 0), stop=(k == NCH - 1)),
                          reads=[B_mgT, B_wmx], writes=[B_pso], flag=(k == NCH - 1))
            x1, B_x1 = X1.next()
            ln_tm(ST, lnp, B_lnp, 0, pso, B_pso, xin, B_xin, x1, B_x1)
            tr.dma("pool", x1_s[o0 + j * 128:o0 + (j + 1) * 128, :], x1[:], reads=[B_x1])

        ld_cur = loads(0)
        mg_cur = MG.next()
        ld_nxt = loads(1) if len(tiles) > 1 else None
        s1_chunks(ld_cur, mg_cur[0], mg_cur[1], range(NCH))
        for idx in range(len(tiles)):
            has_next = idx + 1 < len(tiles)
            if has_next:
                mg_nxt = MG.next()
            for j in range(TA // 128):
                if has_next:
                    s1_chunks(ld_nxt, mg_nxt[0], mg_nxt[1], range(2 * j, 2 * j + 2))
                s2_sub(idx, mg_cur[0], mg_cur[1], j)
            if has_next:
                mg_cur = mg_nxt
                ld_nxt = loads(idx + 2) if idx + 2 < len(tiles) else None
        tr.barrier()

    es_w4.close()

    with ExitStack() as ph:
        lnp, B_lnp = sb(tr, ph, nc, "lnp2", [128, 2, D], F32)
        tr.dma("sync", lnp[:].rearrange("p a d -> p (a d)"), bc[:, 2 * D:4 * D], writes=[B_lnp])
        KM = Rot(tr, ph, nc, "kmT4", [128, NCH, NMEM], BF16, 2)
        VM = Rot(tr, ph, nc, "vm4", [128, 2, 4, 257], BF16, 2)
        NJ = TA // 128
        X1L = Rot(tr, ph, nc, "x1l", [128, D], F32, 3 * NJ)
        XB = Rot(tr, ph, nc, "x1b", [128, D], BF16, 2)
        X1T = Rot(tr, ph, nc, "x1T", [128, NCH, TA], BF16, 2)
        QCT = Rot(tr, ph, nc, "qcT", [128, NCH, TA], BF16, 2)
        PTc = Rot(tr, ph, nc, "ptc", [128, TA], BF16, 4)
        OTM = Rot(tr, ph, nc, "otm", [128, D], BF16, 2 * NJ)
        OT = Rot(tr, ph, nc, "oT", [128, NCH, TA], BF16, 2)
        X2 = Rot(tr, ph, nc, "x2", [128, D], F32, 2)
        ST = Rot(tr, ph, nc, "st5a", [128, 16], F32, 6)
        PSA = Rot(tr, ph, nc, "psa2", [128, 512], F32, 2, psum=True)
        PSV = Rot(tr, ph, nc, "psv", [128, 512], F32, 2, psum=True)
        PTB = Rot(tr, ph, nc, "p4ptb", [128, 8, 128], BF16, 2, psum=True)
        PSO = Rot(tr, ph, nc, "pso2", [128, 2, 512], F32, 1, psum=True)
        tiles = [(s, tt) for s in range(n_seq) for tt in range(seqs[s][1] // TA)]
        memkv = {}
        cpi2 = [0]

        def cp2(out, in_, rb, wb):
            cpi2[0] += 1
            if cpi2[0] % 2 == 0:
                tr.op("act", lambda e: e.copy(out=out, in_=in_), reads=rb, writes=wb)
            else:
                tr.op("dve", lambda e: e.tensor_copy(out=out, in_=in_), reads=rb, writes=wb)

        def front1(idx):
            s, tt = tiles[idx]
            o0 = own_off[s] + tt * TA
            if s not in memkv:
                kmT, B_kmT = KM.next()
                vm, B_vm = VM.next()
                tr.dma("sync", kmT[:].rearrange("p c m -> p (c m)"), kmT_s[s], writes=[B_kmT])
                tr.dma("sync", vm[:].rearrange("p t h d -> p (t h d)"), vm_s[s], writes=[B_vm])
                memkv[s] = (kmT, B_kmT, vm, B_vm)
            x1T, B_x1T = X1T.next()
            x1l = []
            for j in range(NJ):
                x1, B_x1 = X1L.next()
                tr.dma("sync", x1[:], x1_s[o0 + j * 128:o0 + (j + 1) * 128, :], writes=[B_x1])
                x1l.append((x1, B_x1))
                xb, B_xb = XB.next()
                tr.op("act", lambda e: e.copy(out=xb[:], in_=x1[:]), reads=[B_x1], writes=[B_xb])
                ptb, B_ptb = PTB.next()
                for k in range(NCH):
                    tr.op("pe", lambda e, k=k: e.transpose(out=ptb[:, k, :], in_=xb[:, k * 128:(k + 1) * 128], identity=identB[:]), reads=[B_xb, B_identB], writes=[B_ptb], flag=(k == NCH - 1))
                tr.op("act", lambda e: e.copy(out=x1T[:, :, j * 128:(j + 1) * 128], in_=ptb[:]), reads=[B_ptb], writes=[B_x1T])
            return dict(idx=idx, s=s, o0=o0, x1T=(x1T, B_x1T), x1l=x1l)

        def front2(t):
            x1T, B_x1T = t["x1T"]
            qcT, B_qcT = QCT.next()
            for m in range(NCH):
                psa, B_psa = PSA.next()
                for k in range(NCH):
                    tr.op("pe", lambda e, k=k: e.matmul(psa[:, 0:TA], lhsT=wq[:, k, m * 128:(m + 1) * 128], rhs=x1T[:, k, :], start=(k == 0), stop=(k == NCH - 1)),
                          reads=[B_wq, B_x1T], writes=[B_psa], flag=(k == NCH - 1))
                cp2(qcT[:, m, :], psa[:, 0:TA], [B_psa], [B_qcT])
            t["qcT"] = (qcT, B_qcT)

        def back1(t):
            kmT, B_kmT, vm, B_vm = memkv[t["s"]]
            qcT, B_qcT = t["qcT"]
            otms = [OTM.next() for _ in range(NJ)]
            t["otms"] = otms

            def sc(hh):
                pts = []
                for mt in range(2):
                    psa, B_psa = PSA.next()
                    for dc in range(2):
                        tr.op("pe", lambda e, dc=dc: e.matmul(psa[:, 0:TA], lhsT=kmT[:, hh * 2 + dc, mt * 128:(mt + 1) * 128], rhs=qcT[:, hh * 2 + dc, :], start=(dc == 0), stop=(dc == 1)),
                              reads=[B_kmT, B_qcT], writes=[B_psa], flag=(dc == 1))
                    pt, B_pt = PTc.next()
                    tr.op("act", lambda e: e.activation(out=pt[:], in_=psa[:, 0:TA], func=AF.Exp, scale=1.0 / 16.0), reads=[B_psa], writes=[B_pt])
                    pts.append((pt, B_pt))
                return pts

            def pv(hh, pts):
                for j in range(NJ):
                    psv, B_psv = PSV.next()
                    for mt in range(2):
                        pt, B_pt = pts[mt]
                        tr.op("pe", lambda e, mt=mt: e.matmul(psv[:, 0:257], lhsT=pt[:, j * 128:(j + 1) * 128], rhs=vm[:, mt, hh, :], start=(mt == 0), stop=(mt == 1)),
                              reads=[B_pt, B_vm], writes=[B_psv], flag=(mt == 1))
                    st, B_st = ST.next()
                    tr.op("dve", lambda e: e.reciprocal(out=st[:, 0:1], in_=psv[:, 256:257]), reads=[B_psv], writes=[B_st])
                    otm, B_otm = otms[j]
                    tr.op("dve", lambda e: e.tensor_scalar(out=otm[:, hh * 256:(hh + 1) * 256], in0=psv[:, 0:256], scalar1=st[:, 0:1], scalar2=None, op0=ALU.mult),
                          reads=[B_psv, B_st], writes=[B_otm])

            prev = sc(0)
            for hh in range(1, 4):
                cur = sc(hh)
                pv(hh - 1, prev)
                prev = cur
            pv(3, prev)

        def back3(t):
            oT, B_oT = OT.next()
            for j in range(NJ):
                otm, B_otm = t["otms"][j]
                ptb, B_ptb = PTB.next()
                for k in range(NCH):
                    tr.op("pe", lambda e, k=k: e.transpose(out=ptb[:, k, :], in_=otm[:, k * 128:(k + 1) * 128], identity=identB[:]), reads=[B_otm, B_identB], writes=[B_ptb], flag=(k == NCH - 1))
                cp2(oT[:, :, j * 128:(j + 1) * 128], ptb[:], [B_ptb], [B_oT])
            t["oT"] = (oT, B_oT)

        def back4(t):
            oT, B_oT = t["oT"]
            o0 = t["o0"]
            for j in range(NJ):
                pso, B_pso = PSO.next()
                for hf in range(2):
                    for k in range(NCH):
                        tr.op("pe", lambda e, k=k, hf=hf: e.matmul(pso[:, hf, :], lhsT=oT[:, k, j * 128:(j + 1) * 128], rhs=wo[:, k, hf * 512:(hf + 1) * 512], start=(k == 0), stop=(k == NCH - 1)),
                              reads=[B_oT, B_wo], writes=[B_pso], flag=(k == NCH - 1))
                x2, B_x2 = X2.next()
                x1, B_x1 = t["x1l"][j]
                ln_tm(ST, lnp, B_lnp, 0, pso, B_pso, x1, B_x1, x2, B_x2)
                tr.dma("pool", x2_s[o0 + j * 128:o0 + (j + 1) * 128, :], x2[:], reads=[B_x2])

        cur = front1(0)
        front2(cur)
        for idx in range(len(tiles)):
            nx = front1(idx + 1) if idx + 1 < len(tiles) else None
            back1(cur)
            if nx is not None:
                front2(nx)
            back3(cur)
            back4(cur)
            cur = nx
        tr.barrier()

    es_w5.close()
    if debug == "p4a":
        return finish(nc, tr, es, y, N_own)

    TT = 256
    NS = TT // 128
    with ExitStack() as ph:
        wi, B_wi = load_w(ph, "wi", ffn_wi, 2 * DFF, colblk=1408)
        wo2, B_wo2 = load_w(ph, "wo2", ffn_wo, D, kchunks=NFF, own=True)
        lnp, B_lnp = sb(tr, ph, nc, "lnp3", [128, 2, D], F32)
        tr.dma("sync", lnp[:].rearrange("p a d -> p (a d)"), bc[:, 4 * D:6 * D], chan=B_lnp, writes=[B_lnp])
        X2 = Rot(tr, ph, nc, "x2in", [128, D], F32, 3 * NS)
        XB = Rot(tr, ph, nc, "x2b", [128, D], BF16, 2)
        aT, B_aT = sb(tr, ph, nc, "aT", [128, NFF, TT], BF16)
        SG = Rot(tr, ph, nc, "sg", [128, TT], F32, 3)
        YO = Rot(tr, ph, nc, "yo", [128, D], F32, 2)
        ST = Rot(tr, ph, nc, "st5", [128, 16], F32, 4)
        PSA = Rot(tr, ph, nc, "psb", [128, 512], F32, 4, psum=True)
        PSO = Rot(tr, ph, nc, "pso5", [128, 2, 512], F32, 1, psum=True)
        PTB = Rot(tr, ph, nc, "p5ptb", [128, 8, 128], BF16, 2, psum=True)
        X2T = Rot(tr, ph, nc, "x2Tr", [128, NCH, TT], BF16, 2)
        tiles = [(s, tt) for s in range(n_seq) for tt in range(seqs[s][1] // TT)]

        def front(idx):
            s, tt = tiles[idx]
            o0 = own_off[s] + tt * TT
            x2T, B_x2T = X2T.next()
            x2s = []
            for j in range(NS):
                x2, B_x2 = X2.next()
                tr.dma("sync", x2[:], x2_s[o0 + j * 128:o0 + (j + 1) * 128, :], writes=[B_x2])
                x2s.append((x2, B_x2))
                xb, B_xb = XB.next()
                tr.op("act", lambda e: e.copy(out=xb[:], in_=x2[:]), reads=[B_x2], writes=[B_xb])
                ptb, B_ptb = PTB.next()
                for k in range(NCH):
                    tr.op("pe", lambda e, k=k: e.transpose(out=ptb[:, k, :], in_=xb[:, k * 128:(k + 1) * 128], identity=identB[:]), reads=[B_xb, B_identB], writes=[B_ptb], flag=(k == NCH - 1))
                tr.op("act", lambda e: e.copy(out=x2T[:, :, j * 128:(j + 1) * 128], in_=ptb[:]), reads=[B_ptb], writes=[B_x2T])
            return dict(o0=o0, x2T=(x2T, B_x2T), x2s=x2s)

        cur = front(0)
        for idx in range(len(tiles)):
            o0 = cur["o0"]
            x2T, B_x2T = cur["x2T"]
            x2s = cur["x2s"]
            for f in range(NFF):
                psg, B_psg = PSA.next()
                psu, B_psu = PSA.next()
                for k in range(NCH):
                    tr.op("pe", lambda e, k=k: e.matmul(psg[:, 0:TT], lhsT=wi[:, k, f * 128:(f + 1) * 128], rhs=x2T[:, k, :], start=(k == 0), stop=(k == NCH - 1)),
                          reads=[B_wi, B_x2T], writes=[B_psg], flag=(k == NCH - 1))
                for k in range(NCH):
                    tr.op("pe", lambda e, k=k: e.matmul(psu[:, 0:TT], lhsT=wi[:, k, DFF + f * 128:DFF + (f + 1) * 128], rhs=x2T[:, k, :], start=(k == 0), stop=(k == NCH - 1)),
                          reads=[B_wi, B_x2T], writes=[B_psu], flag=(k == NCH - 1))
                sg, B_sg = SG.next()
                tr.op("act", lambda e: e.activation(out=sg[:], in_=psg[:, 0:TT], func=AF.Silu), reads=[B_psg], writes=[B_sg])
                tr.op("dve", lambda e: e.tensor_tensor(out=aT[:, f, :], in0=psu[:, 0:TT], in1=sg[:], op=ALU.mult), reads=[B_psu, B_sg], writes=[B_aT])
            nx = front(idx + 1) if idx + 1 < len(tiles) else None
            for j in range(NS):
                pso, B_pso = PSO.next()
                for hf in range(2):
                    for f in range(NFF):
                        tr.op("pe", lambda e, f=f, hf=hf: e.matmul(pso[:, hf, :], lhsT=aT[:, f, j * 128:(j + 1) * 128], rhs=wo2[:, f, hf * 512:(hf + 1) * 512], start=(f == 0), stop=(f == NFF - 1)),
                              reads=[B_aT, B_wo2], writes=[B_pso], flag=(f == NFF - 1))
                yo, B_yo = YO.next()
                x2, B_x2 = x2s[j]
                ln_tm(ST, lnp, B_lnp, 0, pso, B_pso, x2, B_x2, yo, B_yo)
                tr.dma("pool", y[o0 + j * 128:o0 + (j + 1) * 128, :], yo[:], reads=[B_yo])
            cur = nx
        tr.barrier()
    return finish(nc, tr, es, y, N_own)


def finish(nc, tr, es, y, N_own):
    if tr.barcount == 0 or any(tr.cnt[e] for e in tr.cnt):
        tr.barrier()
    es.close()
    return nc


def _rope_table(T):
    half = 8
    inv = (np.float32(ROPE_THETA) ** (-np.arange(0, 16, 2, dtype=np.float32) / np.float32(16))).astype(np.float32)
    ang = (np.arange(T, dtype=np.float32)[:, None] * inv[None, :]).astype(np.float32)
    return np.concatenate([np.cos(ang), np.sin(ang)], axis=1).astype(np.float32)


def _fm(v):
    return np.ascontiguousarray(np.asarray(v, np.float32).reshape(NCH, 128).T)


def make_core_inputs(inp, assign, shared):
    seq_list, rev = assign
    xs = np.concatenate([x for (x, m, to) in seq_list], axis=0)
    mems = np.stack([m for (x, m, to) in seq_list], axis=0)
    ropes = []
    for (x, m, to) in seq_list:
        tb = _rope_table(x.shape[0])
        ropes.append(tb[::-1] if rev else tb)
    rope = np.concatenate(ropes, axis=0)
    nt = rope.shape[0] // 128
    rope = np.ascontiguousarray(rope.reshape(nt, 128, 16).transpose(1, 0, 2).reshape(128, nt * 16))
    A, B = (1, 0) if rev else (0, 1)
    cw = np.asarray(inp["conv_w"][0], np.float32)
    z = np.zeros_like(cw[0])
    taps = [z, cw[3], cw[2], cw[1], cw[0]] if rev else [cw[0], cw[1], cw[2], cw[3], z]
    pp = np.zeros((128, NPP), np.float32)
    for j in range(5):
        pp[:, PP_TAP + j:PP_TAP + 40:5] = _fm(taps[j])
    pp[:, PP_CB:PP_CB + 8] = _fm(inp["conv_b"][0])
    pp[:, PP_BAA:PP_BAA + 8] = _fm(inp["lru_ba"][0, A])
    pp[:, PP_BXA:PP_BXA + 8] = _fm(inp["lru_bx"][0, A])
    pp[:, PP_BAB:PP_BAB + 8] = _fm(inp["lru_ba"][0, B])
    pp[:, PP_BXB:PP_BXB + 8] = _fm(inp["lru_bx"][0, B])
    pp[:, PP_LAA:PP_LAA + 8] = _fm(inp["lru_a"][0, A])
    pp[:, PP_LAB:PP_LAB + 8] = _fm(inp["lru_a"][0, B])
    lru_w = np.ascontiguousarray(np.stack([inp["lru_wa"][0, A], inp["lru_wx"][0, A], inp["lru_wa"][0, B], inp["lru_wx"][0, B]], axis=0).astype(np.float32))
    d = dict(shared)
    d.update(xs=np.ascontiguousarray(xs), mems=np.ascontiguousarray(mems), rope=rope, pp=pp, lru_w=lru_w)
    return d


def make_shared(inp):
    bcv = np.concatenate([np.asarray(inp[k][0], np.float32).ravel() for k in
                          ("ln1_g", "ln1_b", "ln2_g", "ln2_b", "ln3_g", "ln3_b", "subln_g", "lambda_q1", "lambda_k1", "lambda_q2", "lambda_k2")])
    assert bcv.shape[0] == NBC
    bc = np.ascontiguousarray(np.broadcast_to(bcv[None, :], (128, NBC)))
    f = lambda k: np.ascontiguousarray(np.asarray(inp[k][0], np.float32))
    return dict(w_in=f("w_in"), p_attn=f("p_attn"), p_lru=f("p_lru"), w_mix=f("w_mix_out"), xa_wq=f("xa_wq"),
                xa_wkv=f("xa_wkv"), xa_wo=f("xa_wo"), ffn_wi=f("ffn_w_in"), ffn_wo=f("ffn_w_out"), bc=bc,
                ident=np.eye(128, dtype=np.float32))


def kernel(**inp):
    return _run(inp, 8)


def _run(inp, n, debug=False):
    xp = np.asarray(inp["x_prompt"], np.float32)
    xsm = np.asarray(inp["x_sample"], np.float32)
    mp = np.asarray(inp["mem_prompt"], np.float32)
    ms = np.asarray(inp["mem_sample"], np.float32)
    Bp, Tp, _ = xp.shape
    Bs, Ts, _ = xsm.shape
    per = Bs // n
    seqs = [(Ts, Ts)] * per + [(Tp, Tp // 2)]
    shared = make_shared(inp)
    in_maps = []
    for c in range(n):
        rev = (c % 2 == 1)
        sl = []
        for i in range(per):
            x = xsm[c * per + i]
            sl.append((x[::-1] if rev else x, ms[c * per + i], Ts))
        x = xp[c // 2]
        sl.append((x[::-1] if rev else x, mp[c // 2], Tp // 2))
        in_maps.append(make_core_inputs(inp, (sl, rev), shared))
    nc = build_program(seqs, debug=debug)
    res = run_bass_kernel_spmd(nc, in_maps, core_ids=list(range(n)))
    if debug:
        return res.results, seqs
    y_p = np.empty((Bp, Tp, D), np.float32)
    y_s = np.empty((Bs, Ts, D), np.float32)
    for c in range(n):
        yc = res.results[c]["y"]
        rev = (c % 2 == 1)
        for i in range(per):
            blk = yc[i * Ts:(i + 1) * Ts]
            y_s[c * per + i] = blk[::-1] if rev else blk
        blk = yc[per * Ts:per * Ts + Tp // 2]
        if rev:
            y_p[c // 2, Tp // 2:] = blk[::-1]
        else:
            y_p[c // 2, :Tp // 2] = blk
    return (y_p, y_s)
```

```python
import math
from contextlib import ExitStack

import numpy as np
import concourse.bass as bass
import concourse.mybir as mybir
from concourse.bass_utils import run_bass_kernel_spmd

F32 = mybir.dt.float32
BF16 = mybir.dt.bfloat16
AF = mybir.ActivationFunctionType
ALU = mybir.AluOpType
AX = mybir.AxisListType

D = 1024
NCH = 8
IN_W = 7168
DFF = 2816
NFF = 22
NMEM = 256
LAMBDA_INIT = 0.8 - 0.6 * math.exp(-0.3 * 0)
ALPHA = 2.0 ** 0.25
LN_EPS = 1e-5
SUBLN_EPS = 1e-5
ROPE_THETA = 500000.0

PP_TAP, PP_CB, PP_BAA, PP_BXA, PP_BAB, PP_BXB, PP_LAA, PP_LAB, NPP = 0, 40, 48, 56, 64, 72, 80, 88, 96
BC_LN1G, BC_LN1B, BC_LN2G, BC_LN2B, BC_LN3G, BC_LN3B, BC_SUBG, BC_LAM, NBC = (
    0, 1024, 2048, 3072, 4096, 5120, 6144, 6272, 6528)


class Chan:
    __slots__ = ("name", "sem", "dcount")

    def __init__(self, name):
        self.name = name
        self.sem = None
        self.dcount = 0


class Buf:
    __slots__ = ("name", "w", "r", "chan", "excl", "span")

    def __init__(self, name, excl=False, own=False, span=False):
        self.name = name
        self.span = span
        self.w = None
        self.r = {}
        self.chan = Chan(name) if own else None
        self.excl = excl


class TR:
    def __init__(self, nc, es):
        self.nc = nc
        self.es = es
        self.eng = {"sync": nc.sync, "act": nc.scalar, "dve": nc.vector, "pool": nc.gpsimd,
                    "pe": nc.tensor}
        self.bufs = []
        self.chans = []
        self.bar = es.enter_context(nc.semaphore("bar"))
        self.barcount = 0
        self.phase = 0
        self.sem = {e: es.enter_context(nc.semaphore(f"s_{e}")) for e in self.eng}
        self.cnt = {e: 0 for e in self.eng}
        self.seen = {e: {f: 0 for f in self.eng} for e in self.eng}
        self.seen_d = {e: {} for e in self.eng}
        self._new_phase_chans()

    def _new_phase_chans(self):
        self.st_chan = {q: Chan(f"st{self.phase}{q}") for q in self.eng}
        self.ld_chan = {q: Chan(f"ld{self.phase}{q}") for q in self.eng}
        self.phase += 1

    def buf(self, name, excl=False, own=False, span=False):
        b = Buf(name, excl, own, span)
        self.bufs.append(b)
        return b

    def _waits(self, eng, reads, writes):
        toks = []
        for b in reads:
            if b.w is not None:
                toks.append((b.w, True))
            if b.excl:
                toks.extend((t, False) for t in b.r.values())
        for b in writes:
            if b.w is not None:
                toks.append((b.w, False))
            toks.extend((t, False) for t in b.r.values())
        e = self.eng[eng]
        for t, raw in toks:
            if t[0] == "e":
                _, f, val = t
                if f == eng and (not raw or eng in ("pe", "sync")):
                    continue
                if self.seen[eng][f] >= val:
                    continue
                e.wait_ge(self.sem[f], val)
                self.seen[eng][f] = val
            else:
                _, ch, val = t
                if self.seen_d[eng].get(id(ch), 0) >= val:
                    continue
                e.wait_ge(ch.sem, ch.dcount)
                self.seen_d[eng][id(ch)] = ch.dcount

    def op(self, eng, fn, reads=(), writes=(), flag=True):
        self._waits(eng, reads, writes)
        ins = fn(self.eng[eng])
        if flag:
            self.cnt[eng] += 1
            ins.then_inc(self.sem[eng], 1)
            val = self.cnt[eng]
        else:
            val = self.cnt[eng] + 1
        tok = ("e", eng, val)
        for b in reads:
            b.r[eng] = tok
        for b in writes:
            b.w = tok
            b.r = {}
        return tok

    def dma(self, q, out, in_, chan=None, reads=(), writes=()):
        self._waits(q, reads, writes)
        if not writes:
            ch = self.st_chan[q]
        else:
            ch = writes[0].chan if writes[0].chan is not None else self.ld_chan[q]
        if ch.sem is None:
            ch.sem = self.es.enter_context(self.nc.semaphore(f"d_{ch.name}"))
            self.chans.append(ch)
        ch.dcount += 16
        self.eng[q].dma_start(out=out, in_=in_).then_inc(ch.sem, 16)
        tok = ("d", ch, ch.dcount)
        for b in reads:
            b.r[("d", id(ch))] = tok
        for b in writes:
            b.w = tok
            b.r = {}
        return tok

    def barrier(self):
        s = self.nc.sync
        for f in self.eng:
            if f != "sync" and self.cnt[f] > self.seen["sync"][f]:
                s.wait_ge(self.sem[f], self.cnt[f])
                self.seen["sync"][f] = self.cnt[f]
        span_ch = {id(b.chan) for b in self.bufs if b.span and b.chan is not None}
        for ch in self.chans:
            if id(ch) in span_ch:
                continue
            if ch.dcount > self.seen_d["sync"].get(id(ch), 0):
                s.wait_ge(ch.sem, ch.dcount)
        self.barcount += 1
        s.sem_inc(self.bar, 1)
        for f in self.eng:
            if f != "sync":
                self.eng[f].wait_ge(self.bar, self.barcount)
        for b in self.bufs:
            if b.span:
                continue
            b.w = None
            b.r = {}
        for e in self.eng:
            for f in self.eng:
                self.seen[e][f] = self.cnt[f]
            for ch in self.chans:
                if id(ch) not in span_ch:
                    self.seen_d[e][id(ch)] = ch.dcount
        self._new_phase_chans()


class Rot:
    def __init__(self, tr, ph, nc, name, shape, dt, n, psum=False):
        self.items = []
        for i in range(n):
            if psum:
                t = ph.enter_context(nc.psum_tensor(f"{name}{i}", shape, dt))
            else:
                t = ph.enter_context(nc.sbuf_tensor(f"{name}{i}", shape, dt))
            self.items.append((t, tr.buf(f"{name}{i}", excl=psum, own=not psum)))
        self.i = 0

    def next(self):
        it = self.items[self.i % len(self.items)]
        self.i += 1
        return it


def sb(tr, ph, nc, name, shape, dt):
    t = ph.enter_context(nc.sbuf_tensor(name, shape, dt))
    return t, tr.buf(name)


def build_program(seqs, debug=False):
    n_seq = len(seqs)
    ctx_off, own_off = [], []
    a = b = 0
    for (tc, to) in seqs:
        ctx_off.append(a)
        own_off.append(b)
        a += tc
        b += to
    N_ctx, N_own = a, b

    nc = bass.Bass("TRN2", target_bir_lowering=False)

    def din(name, shape, dt=F32):
        return nc.dram_tensor(name, list(shape), dt, kind="ExternalInput").ap()

    def dscr(name, shape, dt):
        kind = "ExternalOutput" if debug else "Internal"
        return nc.dram_tensor(name, list(shape), dt, kind=kind).ap()

    xs = din("xs", [N_ctx, D])
    mems = din("mems", [n_seq, NMEM, D])
    rope = din("rope", [128, (N_ctx // 128) * 16])
    w_in = din("w_in", [D, IN_W])
    p_attn = din("p_attn", [D, D])
    p_lru = din("p_lru", [D, D])
    w_mix = din("w_mix", [D, D])
    xa_wq = din("xa_wq", [D, D])
    xa_wkv = din("xa_wkv", [D, 2 * D])
    xa_wo = din("xa_wo", [D, D])
    ffn_wi = din("ffn_wi", [D, 2 * DFF])
    ffn_wo = din("ffn_wo", [DFF, D])
    lru_w = din("lru_w", [4, NCH, 128, 128])
    pp = din("pp", [128, NPP])
    bc = din("bc", [128, NBC])
    ident = din("ident", [128, 128])
    y = nc.dram_tensor("y", [N_own, D], F32, kind="ExternalOutput").ap()

    qT_s = dscr("qT_s", [NCH, 128, N_own], BF16)
    kT_s = dscr("kT_s", [NCH, 128, N_ctx], BF16)
    v_s = dscr("v_s", [NCH, N_ctx, 128], BF16)
    xr_s = dscr("xr_s", [NCH, 128, N_ctx], F32)
    gy_s = dscr("gy_s", [NCH, 128, N_own], BF16)
    ga_s = dscr("ga_s", [NCH, 128, N_own], BF16)
    gl_s = dscr("gl_s", [NCH, 128, N_own], BF16)
    lo_s = dscr("lo_s", [NCH, 128, N_own], BF16)
    ao_s = dscr("ao_s", [NCH, 128, N_own], BF16)
    x2_s = dscr("x2_s", [N_own, D], F32)
    kmT_s = dscr("kmT_s", [n_seq, 128, NCH * NMEM], BF16)
    vm_s = dscr("vm_s", [n_seq, 128, 2 * 4 * 257], BF16)

    es = ExitStack()
    tr = TR(nc, es)
    dbg = dscr("dbg", [8, 128, 512], F32) if debug else None
    dbgc = dscr("dbgc", [128, 128], F32) if debug else None
    dbgl = dscr("dbgl", [128, 1], F32) if debug else None
    def dump(slot, ap, B):
        if debug:
            tr.dma("sync", dbg[slot], ap, chan=B, reads=[B])

    identF, B_identF = sb(tr, es, nc, "identF", [128, 128], F32)
    identB, B_identB = sb(tr, es, nc, "identB", [128, 128], BF16)
    pp_sb, B_pp = sb(tr, es, nc, "pp_sb", [128, NPP], F32)
    neglam, B_neglam = sb(tr, es, nc, "neglam", [128, 1], F32)
    gsub, B_gsub = sb(tr, es, nc, "gsub", [128, 128], F32)
    sc_a, B_sca = sb(tr, es, nc, "sc_a", [128, 16], F32)
    sc_2a, B_sc2a = sb(tr, es, nc, "sc_2a", [128, 16], F32)
    hbias, B_hb = sb(tr, es, nc, "hbias", [128, 32], F32)
    hsc, B_hsc = sb(tr, es, nc, "hsc", [128, 16], F32)
    nhalf, B_nhalf = sb(tr, es, nc, "nhalf", [128, 1], F32)

    def issue_w(t, B, w_ap, ncols, kchunks=NCH, colblk=1024):
        wv = w_ap.rearrange("(kc p) n -> p kc n", p=128)
        for kc in range(kchunks):
            for c0 in range(0, ncols, colblk):
                c1 = min(ncols, c0 + colblk)
                tr.dma("pool", t[:, kc, c0:c1], wv[:, kc, c0:c1], chan=B, writes=[B])

    def load_w(ph, name, w_ap, ncols, kchunks=NCH, colblk=1024, issue=True, own=False):
        t = ph.enter_context(nc.sbuf_tensor(name, [128, kchunks, ncols], BF16))
        B = tr.buf(name, own=own)
        if issue:
            issue_w(t, B, w_ap, ncols, kchunks, colblk)
        return t, B

    es_win = ExitStack()
    win = es_win.enter_context(nc.sbuf_tensor("win", [128, NCH, IN_W], BF16))
    B_winb = [tr.buf(f"win{b_}", own=True, span=True) for b_ in range(IN_W // 1024)]

    with ExitStack() as ph:
        tr.op("pool", lambda e: e.memset(nhalf[:], -0.5), writes=[B_nhalf])
        tr.dma("sync", identF[:], ident, chan=B_identF, writes=[B_identF])
        tr.dma("pool", identB[:], ident, chan=B_identB, writes=[B_identB])
        tr.dma("sync", pp_sb[:], pp, chan=B_pp, writes=[B_pp])
        lamv, B_lamv = sb(tr, ph, nc, "lamv", [128, 384], F32)
        tr.dma("sync", lamv[:], bc[:, BC_SUBG:BC_SUBG + 384], chan=B_lamv, writes=[B_lamv])
        tmp, B_tmp = sb(tr, ph, nc, "p0tmp", [128, 256], F32)
        sm, B_sm = sb(tr, ph, nc, "p0sm", [128, 64], F32)
        tr.op("dve", lambda e: e.tensor_tensor(out=tmp[:, 0:64], in0=lamv[:, 128:192], in1=lamv[:, 192:256], op=ALU.mult),
              reads=[B_lamv], writes=[B_tmp])
        tr.op("dve", lambda e: e.tensor_tensor(out=tmp[:, 64:128], in0=lamv[:, 256:320], in1=lamv[:, 320:384], op=ALU.mult),
              reads=[B_lamv], writes=[B_tmp])
        tr.op("dve", lambda e: e.reduce_sum(out=sm[:, 0:1], in_=tmp[:, 0:64], axis=AX.X), reads=[B_tmp], writes=[B_sm])
        tr.op("dve", lambda e: e.reduce_sum(out=sm[:, 1:2], in_=tmp[:, 64:128], axis=AX.X), reads=[B_tmp], writes=[B_sm])
        tr.op("act", lambda e: e.activation(out=sm[:, 2:4], in_=sm[:, 0:2], func=AF.Exp), reads=[B_sm], writes=[B_sm])
        tr.op("dve", lambda e: e.tensor_tensor(out=sm[:, 4:5], in0=sm[:, 3:4], in1=sm[:, 2:3], op=ALU.subtract),
              reads=[B_sm], writes=[B_sm])
        tr.op("dve", lambda e: e.tensor_scalar_add(out=neglam[:], in0=sm[:, 4:5], scalar1=-LAMBDA_INIT),
              reads=[B_sm], writes=[B_neglam])
        tr.op("dve", lambda e: e.tensor_scalar_mul(out=gsub[:], in0=lamv[:, 0:128], scalar1=1.0 - LAMBDA_INIT),
              reads=[B_lamv], writes=[B_gsub])
        la = pp_sb[:, PP_LAA:PP_LAA + 16]
        t_ay, t_z, t_w, t_ln, t_d, t_rd, t_l1p, t_relu = (sm[:, 8 + 0:8 + 16], sm[:, 24:40], sm[:, 40:56],
                                                         tmp[:, 128:144], tmp[:, 144:160], tmp[:, 160:176],
                                                         tmp[:, 176:192], tmp[:, 192:208])
        tr.op("dve", lambda e: e.tensor_scalar_mul(out=t_ay, in0=la, scalar1=-1.0), reads=[B_pp], writes=[B_sm])
        tr.op("dve", lambda e: e.tensor_tensor(out=t_ay, in0=t_ay, in1=la, op=ALU.max), reads=[B_pp, B_sm], writes=[B_sm])
        tr.op("act", lambda e: e.activation(out=t_z, in_=t_ay, func=AF.Exp, scale=-1.0), reads=[B_sm], writes=[B_sm])
        tr.op("dve", lambda e: e.tensor_scalar_add(out=t_w, in0=t_z, scalar1=1.0), reads=[B_sm], writes=[B_sm])
        tr.op("act", lambda e: e.activation(out=t_ln, in_=t_w, func=AF.Ln), reads=[B_sm], writes=[B_tmp])
        tr.op("dve", lambda e: e.tensor_scalar(out=t_d, in0=t_w, scalar1=-1.0, scalar2=1e-30, op0=ALU.add, op1=ALU.max),
              reads=[B_sm], writes=[B_tmp])
        tr.op("dve", lambda e: e.reciprocal(out=t_rd, in_=t_d), reads=[B_tmp], writes=[B_tmp])
        tr.op("dve", lambda e: e.tensor_tensor(out=t_l1p, in0=t_ln, in1=t_z, op=ALU.mult), reads=[B_tmp, B_sm], writes=[B_tmp])
        tr.op("dve", lambda e: e.tensor_tensor(out=t_l1p, in0=t_l1p, in1=t_rd, op=ALU.mult), reads=[B_tmp], writes=[B_tmp])
        tr.op("dve", lambda e: e.tensor_scalar(out=t_relu, in0=la, scalar1=-1.0, scalar2=0.0, op0=ALU.mult, op1=ALU.max),
              reads=[B_pp], writes=[B_tmp])
        tr.op("dve", lambda e: e.tensor_tensor(out=t_l1p, in0=t_l1p, in1=t_relu, op=ALU.add), reads=[B_tmp], writes=[B_tmp])
        tr.op("dve", lambda e: e.tensor_scalar_mul(out=sc_a[:], in0=t_l1p, scalar1=-8.0), reads=[B_tmp], writes=[B_sca])
        tr.op("dve", lambda e: e.tensor_scalar_mul(out=sc_2a[:], in0=t_l1p, scalar1=-16.0), reads=[B_tmp], writes=[B_sc2a])
        tr.op("dve", lambda e: e.tensor_scalar_mul(out=hsc[:], in0=t_l1p, scalar1=-4.0), reads=[B_tmp], writes=[B_hsc])
        tr.op("dve", lambda e: e.tensor_scalar_mul(out=hbias[:], in0=pp_sb[:, PP_BAA:PP_BAA + 32], scalar1=0.5), reads=[B_pp], writes=[B_hb])

        wkv, B_wkv = load_w(ph, "wkv", xa_wkv, 2 * D, colblk=2048)
        wv_in = w_in.rearrange("(kc p) n -> p kc n", p=128)
        for b_ in (3, 0, 1, 2, 4, 5, 6):
            for kc in range(NCH):
                tr.dma("pool", win[:, kc, b_ * 1024:(b_ + 1) * 1024], wv_in[:, kc, b_ * 1024:(b_ + 1) * 1024], writes=[B_winb[b_]])
        MF = Rot(tr, ph, nc, "mf", [128, D], F32, 2)
        PST = Rot(tr, ph, nc, "p0pst", [128, 4, 128], F32, 2, psum=True)
        PSK = Rot(tr, ph, nc, "p0psk", [128, 512], F32, 3, psum=True)
        memT, B_memT = sb(tr, ph, nc, "memT", [128, NCH, NMEM], BF16)
        kmT, B_kmT = sb(tr, ph, nc, "kmT", [128, NCH, NMEM], BF16)
        vm, B_vm = sb(tr, ph, nc, "vm", [128, 2, 4, 257], BF16)
        tr.op("dve", lambda e: e.memset(vm[:], 1.0), writes=[B_vm])
        cp = 0
        for s in range(n_seq):
            for t in range(2):
                mf, B_mf = MF.next()
                tr.dma("sync", mf[:], mems[s, t * 128:(t + 1) * 128, :], chan=B_mf, writes=[B_mf])
                for hf in range(2):
                    pst, B_pst = PST.next()
                    for i in range(4):
                        kk = hf * 4 + i
                        tr.op("pe", lambda e, pst=pst, i=i, kk=kk, mf=mf: e.transpose(out=pst[:, i, :], in_=mf[:, kk * 128:(kk + 1) * 128], identity=identF[:]),
                              reads=[B_mf, B_identF], writes=[B_pst], flag=(i == 3))
                    eng = "act" if cp % 2 == 0 else "dve"
                    cp += 1
                    if eng == "act":
                        tr.op("act", lambda e, pst=pst, hf=hf, t=t: e.copy(out=memT[:, hf * 4:hf * 4 + 4, t * 128:(t + 1) * 128], in_=pst[:]),
                              reads=[B_pst], writes=[B_memT])
                    else:
                        tr.op("dve", lambda e, pst=pst, hf=hf, t=t: e.tensor_copy(out=memT[:, hf * 4:hf * 4 + 4, t * 128:(t + 1) * 128], in_=pst[:]),
                              reads=[B_pst], writes=[B_memT])
            for m in range(NCH):
                ps, B_ps = PSK.next()
                for k in range(NCH):
                    tr.op("pe", lambda e, ps=ps, k=k, m=m: e.matmul(ps[:, 0:NMEM], lhsT=wkv[:, k, m * 128:(m + 1) * 128], rhs=memT[:, k, :], start=(k == 0), stop=(k == NCH - 1)),
                          reads=[B_wkv, B_memT], writes=[B_ps], flag=(k == NCH - 1))
                if m % 2 == 0:
                    tr.op("act", lambda e, ps=ps, m=m: e.copy(out=kmT[:, m, :], in_=ps[:, 0:NMEM]), reads=[B_ps], writes=[B_kmT])
                else:
                    tr.op("dve", lambda e, ps=ps, m=m: e.tensor_copy(out=kmT[:, m, :], in_=ps[:, 0:NMEM]), reads=[B_ps], writes=[B_kmT])
            tr.dma("sync", kmT_s[s], kmT[:].rearrange("p c m -> p (c m)"), chan=B_kmT, reads=[B_kmT])
            for t in range(2):
                for hf in range(2):
                    ps, B_ps = PSK.next()
                    for k in range(NCH):
                        tr.op("pe", lambda e, ps=ps, k=k, t=t, hf=hf: e.matmul(ps[:], lhsT=memT[:, k, t * 128:(t + 1) * 128], rhs=wkv[:, k, D + hf * 512:D + (hf + 1) * 512], start=(k == 0), stop=(k == NCH - 1)),
                              reads=[B_wkv, B_memT], writes=[B_ps], flag=(k == NCH - 1))
                    src = ps[:].rearrange("p (h d) -> p h d", h=2)
                    if (t + hf) % 2 == 0:
                        tr.op("act", lambda e, src=src, t=t, hf=hf: e.copy(out=vm[:, t, hf * 2:hf * 2 + 2, 0:256], in_=src), reads=[B_ps], writes=[B_vm])
                    else:
                        tr.op("dve", lambda e, src=src, t=t, hf=hf: e.tensor_copy(out=vm[:, t, hf * 2:hf * 2 + 2, 0:256], in_=src), reads=[B_ps], writes=[B_vm])
            tr.dma("sync", vm_s[s], vm[:].rearrange("p t h d -> p (t h d)"), chan=B_vm, reads=[B_vm])
        if debug:
            tr.dma("sync", dbgc[:, 0:16], sc_a[:], chan=B_sca, reads=[B_sca])
            tr.dma("sync", dbgc[:, 16:32], hsc[:], chan=B_hsc, reads=[B_hsc])
            tr.dma("sync", dbgc[:, 32:64], hbias[:], chan=B_hb, reads=[B_hb])
            tr.dma("sync", dbgl, neglam[:], chan=B_neglam, reads=[B_neglam])
            tr.dma("sync", dbgc[:, 64:128], sm[:], chan=B_sm, reads=[B_sm])
            tr.dma("sync", dbg[7, :, 0:256], tmp[:], chan=B_tmp, reads=[B_tmp])
            tr.dma("sync", dbg[6, :, 0:384], lamv[:], chan=B_lamv, reads=[B_lamv])
        tr.barrier()

    if debug == "p0":
        return finish(nc, tr, es, y, N_own)

    with ExitStack() as ph:
        n_tiles128 = N_ctx // 128
        rope_sb, B_rope = sb(tr, ph, nc, "rope_sb", [128, n_tiles128, 16], F32)
        tr.dma("sync", rope_sb[:].rearrange("p t c -> p (t c)"), rope, chan=B_rope, writes=[B_rope])
        XF = Rot(tr, ph, nc, "xf", [128, D], F32, 3)
        XT = Rot(tr, ph, nc, "xT", [128, NCH, 512], BF16, 2)
        PST = Rot(tr, ph, nc, "p1pst", [128, 4, 128], F32, 2, psum=True)
        PSM = Rot(tr, ph, nc, "p1psm", [128, 512], F32, 4, psum=True)
        PTB = Rot(tr, ph, nc, "p1ptb", [128, 8, 128], BF16, 2, psum=True)
        QTM = Rot(tr, ph, nc, "qtm", [128, 512], BF16, 3)
        RT = Rot(tr, ph, nc, "ropet", [128, 4, 64], F32, 2)
        QTS = Rot(tr, ph, nc, "qts", [128, 4, 512], BF16, 2)
        VTM = Rot(tr, ph, nc, "vtm", [128, D], BF16, 2)
        STF = Rot(tr, ph, nc, "stf", [128, 512], F32, 3)
        STB = Rot(tr, ph, nc, "stb", [128, 512], BF16, 4)
        cpi = [0]

        def evac_copy(out, in_, rb, wb):
            cpi[0] += 1
            if cpi[0] % 2 == 0:
                tr.op("act", lambda e: e.copy(out=out, in_=in_), reads=rb, writes=wb)
            else:
                tr.op("dve", lambda e: e.tensor_copy(out=out, in_=in_), reads=rb, writes=wb)

        def tm_rope_group(xT, B_xT, col0, dst, tok_own0, tile128_0):
            its = [(hf, j) for hf in range(2) for j in range(4)]
            state = {}
            qts_of = {}

            def stA(i):
                hf, j = its[i]
                ps, B_ps = PSM.next()
                for k in range(NCH):
                    tr.op("pe", lambda e, k=k: e.matmul(ps[:], lhsT=xT[:, k, j * 128:(j + 1) * 128], rhs=win[:, k, col0 + hf * 512:col0 + (hf + 1) * 512], start=(k == 0), stop=(k == NCH - 1)),
                          reads=[B_xT, B_winb[col0 // 1024]], writes=[B_ps], flag=(k == NCH - 1))
                qtm, B_qtm = QTM.next()
                tr.op("act", lambda e: e.copy(out=qtm[:], in_=ps[:]), reads=[B_ps], writes=[B_qtm])
                ps3 = ps[:].rearrange("p (g d) -> p g d", g=8)
                q3 = qtm[:].rearrange("p (g d) -> p g d", g=8)
                x1, x2 = ps3[:, :, 0:8], ps3[:, :, 8:16]
                ti = tile128_0 + j
                cos = rope_sb[:, ti, 0:8].unsqueeze(1).broadcast_to([128, 8, 8])
                sin = rope_sb[:, ti, 8:16].unsqueeze(1).broadcast_to([128, 8, 8])
                rt, B_rt = RT.next()
                r3 = rt[:].rearrange("p a (g d) -> p a g d", g=8)
                tr.op("dve", lambda e: e.tensor_tensor(out=r3[:, 0], in0=x1, in1=cos, op=ALU.mult), reads=[B_ps, B_rope], writes=[B_rt])
                tr.op("dve", lambda e: e.tensor_tensor(out=r3[:, 1], in0=x2, in1=sin, op=ALU.mult), reads=[B_ps, B_rope], writes=[B_rt])
                tr.op("dve", lambda e: e.tensor_tensor(out=r3[:, 2], in0=x2, in1=cos, op=ALU.mult), reads=[B_ps, B_rope], writes=[B_rt])
                tr.op("dve", lambda e: e.tensor_tensor(out=r3[:, 3], in0=x1, in1=sin, op=ALU.mult), reads=[B_ps, B_rope], writes=[B_rt])
                tr.op("dve", lambda e: e.tensor_tensor(out=q3[:, :, 0:8], in0=r3[:, 0], in1=r3[:, 1], op=ALU.subtract), reads=[B_rt], writes=[B_qtm])
                tr.op("dve", lambda e: e.tensor_tensor(out=q3[:, :, 8:16], in0=r3[:, 2], in1=r3[:, 3], op=ALU.add), reads=[B_rt], writes=[B_qtm])
                state[i] = (qtm, B_qtm)

            def stC(i):
                hf, j = its[i]
                if j == 0:
                    qts_of[hf] = QTS.next()
                qts, B_qts = qts_of[hf]
                qtm, B_qtm = state.pop(i)
                ptb, B_ptb = PTB.next()
                for t in range(4):
                    tr.op("pe", lambda e, t=t: e.transpose(out=ptb[:, t, :], in_=qtm[:, t * 128:(t + 1) * 128], identity=identB[:]),
                          reads=[B_qtm, B_identB], writes=[B_ptb], flag=(t == 3))
                evac_copy(qts[:, :, j * 128:(j + 1) * 128], ptb[:, 0:4, :], [B_ptb], [B_qts])
                if j == 3:
                    tr.dma("sync", dst[hf * 4:hf * 4 + 4, :, tok_own0:tok_own0 + 512].rearrange("h p t -> p h t"), qts[:], reads=[B_qts])

            n = len(its)
            for i in range(n + 2):
                if i < n:
                    stA(i)
                if i >= 2:
                    stC(i - 2)

        def fm_group(xT, B_xT, col0, dst, tok0, kind):
            for m in range(NCH):
                ps, B_ps = PSM.next()
                for k in range(NCH):
                    tr.op("pe", lambda e, ps=ps, k=k, m=m: e.matmul(ps[:], lhsT=win[:, k, col0 + m * 128:col0 + (m + 1) * 128], rhs=xT[:, k, :], start=(k == 0), stop=(k == NCH - 1)),
                          reads=[B_xT, B_winb[col0 // 1024]], writes=[B_ps], flag=(k == NCH - 1))
                if kind == "f32":
                    st, B_st = STF.next()
                    evac_copy(st[:], ps[:], [B_ps], [B_st])
                else:
                    st, B_st = STB.next()
                    fn = AF.Gelu_apprx_tanh if kind == "gelu" else AF.Sigmoid
                    tr.op("act", lambda e, st=st, ps=ps, fn=fn: e.activation(out=st[:], in_=ps[:], func=fn), reads=[B_ps], writes=[B_st])
                tr.dma("sync", dst[m, :, tok0:tok0 + 512], st[:], chan=B_st, reads=[B_st])

        for s in range(n_seq):
            T_ctx, T_own = seqs[s]
            for tt in range(T_ctx // 512):
                own = tt * 512 < T_own
                tok0 = ctx_off[s] + tt * 512
                tokown0 = own_off[s] + tt * 512
                xT, B_xT = XT.next()
                for j in range(4):
                    xf, B_xf = XF.next()
                    tr.dma("sync", xf[:], xs[tok0 + j * 128:tok0 + (j + 1) * 128, :], chan=B_xf, writes=[B_xf])
                    for hf in range(2):
                        pst, B_pst = PST.next()
                        for i in range(4):
                            kk = hf * 4 + i
                            tr.op("pe", lambda e, pst=pst, i=i, kk=kk, xf=xf: e.transpose(out=pst[:, i, :], in_=xf[:, kk * 128:(kk + 1) * 128], identity=identF[:]),
                                  reads=[B_xf, B_identF], writes=[B_pst], flag=(i == 3))
                        evac_copy(xT[:, hf * 4:hf * 4 + 4, j * 128:(j + 1) * 128], pst[:], [B_pst], [B_xT])
                ONLY = ("xr", "q", "k", "v", "gy", "ga", "gl")
                if "xr" in ONLY:
                    fm_group(xT, B_xT, 3 * D, xr_s, tok0, "f32")
                if own and "q" in ONLY:
                    tm_rope_group(xT, B_xT, 0, qT_s, tokown0, tok0 // 128)
                if "k" in ONLY:
                    tm_rope_group(xT, B_xT, D, kT_s, tok0, tok0 // 128)
                for j in (range(4) if "v" in ONLY else []):
                    vtm, B_vtm = VTM.next()
                    for hf in range(2):
                        ps, B_ps = PSM.next()
                        for k in range(NCH):
                            tr.op("pe", lambda e, ps=ps, k=k, j=j, hf=hf: e.matmul(ps[:], lhsT=xT[:, k, j * 128:(j + 1) * 128], rhs=win[:, k, 2 * D + hf * 512:2 * D + (hf + 1) * 512], start=(k == 0), stop=(k == NCH - 1)),
                                  reads=[B_xT, B_winb[2]], writes=[B_ps], flag=(k == NCH - 1))
                        evac_copy(vtm[:, hf * 512:(hf + 1) * 512], ps[:], [B_ps], [B_vtm])
                    t0 = tok0 + j * 128
                    tr.dma("sync", v_s[:, t0:t0 + 128, :].rearrange("h p e -> p h e"), vtm[:].rearrange("p (h e) -> p h e", h=8), chan=B_vtm, reads=[B_vtm])
                if own and "gy" in ONLY:
                    fm_group(xT, B_xT, 4 * D, gy_s, tokown0, "gelu")
                if own and "ga" in ONLY:
                    fm_group(xT, B_xT, 5 * D, ga_s, tokown0, "sig")
                if own and "gl" in ONLY:
                    fm_group(xT, B_xT, 6 * D, gl_s, tokown0, "sig")
        tr.barrier()

    for b_ in B_winb:
        b_.span = False
    es_win.close()
    if debug == "p1":
        return finish(nc, tr, es, y, N_own)

    with ExitStack() as ph:
        GRP = 1024
        CP = 2048
        maxT = max(tc for tc, _ in seqs)
        maxO = max(to for _, to in seqs)
        lw, B_lw = sb(tr, ph, nc, "lw", [128, 4, NCH, 128], BF16)
        for a_ in range(4):
            tr.dma("pool", lw[:, a_], lru_w[a_].rearrange("c d e -> d c e"), writes=[B_lw])
        XR = Rot(tr, ph, nc, "xr", [128, maxT + 4], F32, 1)
        xc = ph.enter_context(nc.sbuf_tensor("xc", [128, maxT], F32))
        xcb = ph.enter_context(nc.sbuf_tensor("xcb", [128, maxT], BF16))
        npc = (maxT + CP - 1) // CP
        B_xcp = [tr.buf(f"xc{i}") for i in range(npc)]
        B_xcbp = [tr.buf(f"xcb{i}") for i in range(npc)]
        hB, B_hB = sb(tr, ph, nc, "hB", [128, maxO], F32)
        carry, B_carry = sb(tr, ph, nc, "carry", [128, 2], F32)
        PSG = Rot(tr, ph, nc, "psg", [128, 512], F32, 6, psum=True)
        g = max(min(GRP, to) for _, to in seqs)
        RT_ = Rot(tr, ph, nc, "r_t", [128, g], F32, 2)
        IT_ = Rot(tr, ph, nc, "i_t", [128, g], F32, 4)
        AT_ = Rot(tr, ph, nc, "a_t", [128, g], F32, 4)
        MT_ = Rot(tr, ph, nc, "m_t", [128, g], F32, 4)
        MX_ = Rot(tr, ph, nc, "mx_t", [128, g], F32, 4)
        HA_ = Rot(tr, ph, nc, "hA", [128, g], F32, 2)
        GY = Rot(tr, ph, nc, "gyt", [128, g], BF16, 2)
        LO = Rot(tr, ph, nc, "lot", [128, g], BF16, 2)
        for (xr, B_xr) in XR.items:
            tr.op("pool", lambda e, xr=xr: e.memset(xr[:], 0.0), writes=[B_xr])

        for s in range(n_seq):
            T_ctx, T_own = seqs[s]
            gs = min(GRP, T_own)
            cp = min(CP, T_ctx)
            for c in range(NCH):
                xr, B_xr = XR.next()
                tr.dma("sync", xr[:, 2:2 + T_ctx], xr_s[c, :, ctx_off[s]:ctx_off[s] + T_ctx], writes=[B_xr])
                if T_ctx < maxT:
                    tr.op("pool", lambda e: e.memset(xr[:, 2 + T_ctx:4 + T_ctx], 0.0), writes=[B_xr])
                tp = PP_TAP + c * 5
                for pi in range(T_ctx // cp - 1, -1, -1):
                    p0 = pi * cp
                    tr.op("pool", lambda e: e.tensor_scalar(out=xc[:, p0:p0 + cp], in0=xr[:, p0:p0 + cp], scalar1=pp_sb[:, tp:tp + 1], scalar2=pp_sb[:, PP_CB + c:PP_CB + c + 1], op0=ALU.mult, op1=ALU.add),
                          reads=[B_xr, B_pp], writes=[B_xcp[pi]])
                    for jj in range(1, 5):
                        tr.op("dve", lambda e: e.scalar_tensor_tensor(out=xc[:, p0:p0 + cp], in0=xr[:, p0 + jj:p0 + jj + cp], scalar=pp_sb[:, tp + jj:tp + jj + 1], in1=xc[:, p0:p0 + cp], op0=ALU.mult, op1=ALU.add),
                              reads=[B_xr, B_pp, B_xcp[pi]], writes=[B_xcp[pi]])
                    tr.op("act", lambda e: e.copy(out=xcb[:, p0:p0 + cp], in_=xc[:, p0:p0 + cp]), reads=[B_xcp[pi]], writes=[B_xcbp[pi]])

                ngB = T_ctx // gs
                jobs = [(1, gi * gs, gi == ngB - 1, gi) for gi in range(ngB - 1, -1, -1)]
                jobs += [(0, gi * gs, gi == 0, gi) for gi in range(T_own // gs)]
                n = gs

                def stage_a(job):
                    d, t0, first, gi = job
                    pi = t0 // cp
                    col = 0 if d == 0 else 16
                    colx = 8 if d == 0 else 24
                    r_t, B_r = RT_.next()
                    i_t, B_i = IT_.next()
                    a_t, B_a = AT_.next()
                    m_t, B_m = MT_.next()
                    for sub in range(n // 512):
                        c0 = sub * 512
                        psr, B_psr = PSG.next()
                        tr.op("pe", lambda e: e.matmul(psr[:], lhsT=lw[:, 2 * d, c, :], rhs=xcb[:, t0 + c0:t0 + c0 + 512], start=True, stop=True),
                              reads=[B_lw, B_xcbp[pi]], writes=[B_psr])
                        psi, B_psi = PSG.next()
                        tr.op("pe", lambda e: e.matmul(psi[:], lhsT=lw[:, 2 * d + 1, c, :], rhs=xcb[:, t0 + c0:t0 + c0 + 512], start=True, stop=True),
                              reads=[B_lw, B_xcbp[pi]], writes=[B_psi])
                        tr.op("act", lambda e: e.activation(out=r_t[:, c0:c0 + 512], in_=psr[:], func=AF.Tanh, scale=0.5, bias=hbias[:, col + c:col + c + 1]),
                              reads=[B_psr, B_hb], writes=[B_r])
                        tr.op("act", lambda e: e.activation(out=i_t[:, c0:c0 + 512], in_=psi[:], func=AF.Tanh, scale=0.5, bias=hbias[:, colx + c:colx + c + 1]),
                              reads=[B_psi, B_hb], writes=[B_i])
                    sc = sc_a[:, d * 8 + c:d * 8 + c + 1]
                    hs = hsc[:, d * 8 + c:d * 8 + c + 1]
                    tr.op("act", lambda e: e.activation(out=a_t[:, 0:n], in_=r_t[:, 0:n], func=AF.Exp, scale=hs, bias=hs), reads=[B_r, B_hsc], writes=[B_a])
                    tr.op("act", lambda e: e.activation(out=m_t[:, 0:n], in_=r_t[:, 0:n], func=AF.Exp, scale=sc, bias=sc), reads=[B_r, B_sca], writes=[B_m])
                    tr.op("dve", lambda e: e.tensor_scalar(out=m_t[:, 0:n], in0=m_t[:, 0:n], scalar1=0.9999999, scalar2=-1.0, op0=ALU.min, op1=ALU.mult), reads=[B_m], writes=[B_m])
                    return (job, i_t, B_i, a_t, B_a, m_t, B_m)

                def stage_s(st):
                    job, i_t, B_i, a_t, B_a, m_t, B_m = st
                    tr.op("act", lambda e: e.activation(out=m_t[:, 0:n], in_=m_t[:, 0:n], func=AF.Sqrt, scale=0.25, bias=0.25), reads=[B_m], writes=[B_m])

                def stage_r(st):
                    job, i_t, B_i, a_t, B_a, m_t, B_m = st
                    d, t0, first, gi = job
                    pi = t0 // cp
                    mx, B_mx = MX_.next()
                    tr.op("dve", lambda e: e.tensor_tensor(out=mx[:, 0:n], in0=m_t[:, 0:n], in1=xc[:, t0:t0 + n], op=ALU.mult), reads=[B_m, B_xcp[pi]], writes=[B_mx])
                    if first:
                        k0 = n - 1 if d == 1 else 0
                        tr.op("dve", lambda e: e.tensor_scalar(out=mx[:, k0:k0 + 1], in0=xc[:, t0 + k0:t0 + k0 + 1], scalar1=0.5, scalar2=None, op0=ALU.mult), reads=[B_xcp[pi], B_mx], writes=[B_mx])
                    tr.op("dve", lambda e: e.scalar_tensor_tensor(out=mx[:, 0:n], in0=i_t[:, 0:n], scalar=1.0, in1=mx[:, 0:n], op0=ALU.add, op1=ALU.mult), reads=[B_i, B_mx], writes=[B_mx])
                    if d == 1:
                        own_g = t0 < T_own
                        if own_g:
                            dst, B_dst = hB[:, t0:t0 + n], B_hB
                        else:
                            hA, B_hA = HA_.next()
                            dst, B_dst = hA[:, 0:n], B_hA
                        init = 0.0 if first else carry[:, 1:2]
                        tr.op("dve", lambda e: e.tensor_tensor_scan(out=dst[:, ::-1], data0=a_t[:, n - 1::-1], data1=mx[:, n - 1::-1], initial=init, op0=ALU.mult, op1=ALU.add),
                              reads=[B_a, B_mx, B_carry], writes=[B_dst])
                        if gi > 0:
                            tr.op("dve", lambda e: e.tensor_copy(out=carry[:, 1:2], in_=dst[:, 0:1]), reads=[B_dst], writes=[B_carry])
                    else:
                        hA, B_hA = HA_.next()
                        gyt, B_gy = GY.next()
                        tr.dma("sync", gyt[:, 0:n], gy_s[c, :, own_off[s] + t0:own_off[s] + t0 + n], writes=[B_gy])
                        init = 0.0 if first else carry[:, 0:1]
                        tr.op("dve", lambda e: e.tensor_tensor_scan(out=hA[:, 0:n], data0=a_t[:, 0:n], data1=mx[:, 0:n], initial=init, op0=ALU.mult, op1=ALU.add),
                              reads=[B_a, B_mx, B_carry], writes=[B_hA])
                        tr.op("dve", lambda e: e.tensor_copy(out=carry[:, 0:1], in_=hA[:, n - 1:n]), reads=[B_hA], writes=[B_carry])
                        tr.op("pool", lambda e: e.tensor_tensor(out=hA[:, 0:n], in0=hA[:, 0:n], in1=hB[:, t0:t0 + n], op=ALU.add), reads=[B_hA, B_hB], writes=[B_hA])
                        lot, B_lo = LO.next()
                        tr.op("pool", lambda e: e.tensor_tensor(out=lot[:, 0:n], in0=hA[:, 0:n], in1=gyt[:, 0:n], op=ALU.mult), reads=[B_hA, B_gy], writes=[B_lo])
                        tr.dma("pool", lo_s[c, :, own_off[s] + t0:own_off[s] + t0 + n], lot[:, 0:n], reads=[B_lo])

                for p0_ in range(0, len(jobs), 2):
                    pair = jobs[p0_:p0_ + 2]
                    sts = [stage_a(j) for j in pair]
                    for st in sts:
                        stage_s(st)
                    for st in sts:
                        stage_r(st)
        tr.barrier()

    if debug == "p2":
        return finish(nc, tr, es, y, N_own)

    es_w5 = ExitStack()
    wq, B_wq = load_w(es_w5, "wq", xa_wq, D, issue=False)
    wo, B_wo = load_w(es_w5, "wo", xa_wo, D, issue=False, own=True)
    es_w4 = ExitStack()
    wpa, B_wpa = load_w(es_w4, "wpa", p_attn, D, issue=False)
    wpl, B_wpl = load_w(es_w4, "wpl", p_lru, D, issue=False)
    wmx, B_wmx = load_w(es_w4, "wmx", w_mix, D, issue=False, own=True)

    with ExitStack() as ph:
        issue_w(wpa, B_wpa, p_attn, D)
        issue_w(wpl, B_wpl, p_lru, D)
        issue_w(wmx, B_wmx, w_mix, D)
        maxT = max(tc for tc, _ in seqs)
        maxO = max(to for _, to in seqs)
        KT = Rot(tr, ph, nc, "kTh", [128, maxT], BF16, 2)
        VH = Rot(tr, ph, nc, "vh", [128, maxT // 128, 129], BF16, 2)
        QT = Rot(tr, ph, nc, "qTh", [128, maxO], BF16, 2)
        PSS = Rot(tr, ph, nc, "pss", [128, 2, 512], F32, 2, psum=True)
        ACCF = [ph.enter_context(nc.psum_tensor(f"acc{i}", [128, 512], F32)) for i in range(3)]
        ACC = [a_[:, 0:387].rearrange("p (s e) -> p s e", s=3) for a_ in ACCF]
        B_ACC = [tr.buf(f"acc{i}", excl=True) for i in range(3)]
        PT3 = Rot(tr, ph, nc, "p3pt", [128, 8, 128], BF16, 1, psum=True)
        PEX = Rot(tr, ph, nc, "pexp", [128, 2, 512], BF16, 3)
        OSB = Rot(tr, ph, nc, "osb", [128, 8, 129], F32, 2)
        EPO = Rot(tr, ph, nc, "epo", [128, 4, 128], F32, 2)
        EPT = Rot(tr, ph, nc, "ept", [128, 4, 128], F32, 2)
        EPS = Rot(tr, ph, nc, "eps", [128, 32], F32, 2)
        AOB = Rot(tr, ph, nc, "aob", [128, 4, 128], BF16, 2)
        AOT = Rot(tr, ph, nc, "aot", [128, 512], BF16, 2)
        for (vh, B_vh) in VH.items:
            tr.op("pool", lambda e, vh=vh: e.memset(vh[:], 1.0), writes=[B_vh])
        accmap = [(0, 0), (0, 1), (0, 2), (1, 0), (1, 1), (1, 2), (2, 0), (2, 1)]
        gsub_b = gsub[:].unsqueeze(1).broadcast_to([128, 4, 128])

        def issue_loads(s_, h_):
            T_c, T_o = seqs[s_]
            nk = T_c // 128
            kT, B_kT = KT.next()
            vh, B_vh = VH.next()
            qT, B_qT = QT.next()
            tr.dma("sync", kT[:, 0:T_c], kT_s[h_, :, ctx_off[s_]:ctx_off[s_] + T_c], writes=[B_kT])
            vsrc = v_s[h_, ctx_off[s_]:ctx_off[s_] + T_c, :].rearrange("(k p) e -> p k e", p=128)
            for k0 in range(0, nk, 8):
                k1 = min(nk, k0 + 8)
                tr.dma("sync", vh[:, k0:k1, 0:128], vsrc[:, k0:k1, :], writes=[B_vh])
            tr.dma("sync", qT[:, 0:T_o], qT_s[h_, :, own_off[s_]:own_off[s_] + T_o], writes=[B_qT])
            return (kT, B_kT, vh, B_vh, qT, B_qT)

        loaded = [issue_loads(0, 0)]
        pending = []

        for s in range(n_seq):
            T_ctx, T_own = seqs[s]
            nkt = T_ctx // 128
            for h in range(NCH):
                kT, B_kT, vh, B_vh, qT, B_qT = loaded.pop(0)
                nidx = s * NCH + h + 1
                if nidx < n_seq * NCH:
                    loaded.append(issue_loads(nidx // NCH, nidx % NCH))
                for qt in range(T_own // 512):
                    q0 = qt * 512

                    def scores(kt):
                        ps, B_ps = PSS.next()
                        for c in range(2):
                            tr.op("pe", lambda e, ps=ps, c=c, kt=kt: e.matmul(ps[:, c, :], lhsT=kT[c * 64:(c + 1) * 64, kt * 128:(kt + 1) * 128], rhs=qT[c * 64:(c + 1) * 64, q0:q0 + 512], start=True, stop=True),
                                  reads=[B_kT, B_qT], writes=[B_ps], flag=(c == 1))
                        return ps, B_ps

                    nxt = scores(0)
                    for kt in range(nkt):
                        ps, B_ps = nxt
                        if kt + 1 < nkt:
                            nxt = scores(kt + 1)
                        if pending and kt == min(10, nkt - 1):
                            pending.pop(0)()
                        pe_, B_pe = PEX.next()
                        tr.op("act", lambda e, pe_=pe_, ps=ps: e.activation(out=pe_[:], in_=ps[:], func=AF.Exp, scale=0.125), reads=[B_ps], writes=[B_pe])
                        last = (kt == nkt - 1)
                        for c in range(2):
                            for j in range(4):
                                bk, sl = accmap[c * 4 + j]
                                tr.op("pe", lambda e, pe_=pe_, c=c, j=j, bk=bk, sl=sl, kt=kt, last=last: e.matmul(ACC[bk][:, sl, :], lhsT=pe_[:, c, j * 128:(j + 1) * 128], rhs=vh[:, kt, :], start=(kt == 0 and sl == 0), stop=last, skip_group_check=True),
                                      reads=[B_pe, B_vh], writes=[B_ACC[bk]], flag=(last and (c * 4 + j) in (2, 5, 7)))
                    osb, B_osb = OSB.next()
                    for bk, (lo_, hi_) in enumerate(((0, 3), (3, 6), (6, 8))):
                        tr.op("dve", lambda e, bk=bk, lo_=lo_, hi_=hi_: e.tensor_copy(out=osb[:, lo_:hi_, :], in_=ACC[bk][:, 0:hi_ - lo_, :]), reads=[B_ACC[bk]], writes=[B_osb])
                    o4 = osb[:].rearrange("p (c j) e -> p c j e", c=2)
                    ep, B_ep = EPS.next()
                    epo, B_epo = EPO.next()
                    ept, B_ept = EPT.next()
                    rc = ep[:, 0:8].rearrange("p (c j) -> p c j", c=2)
                    tr.op("dve", lambda e: e.reciprocal(out=rc, in_=o4[:, :, :, 128]), reads=[B_osb], writes=[B_ep])
                    tr.op("dve", lambda e: e.tensor_scalar(out=ep[:, 8:12], in0=rc[:, 1, :], scalar1=neglam[:, 0:1], scalar2=None, op0=ALU.mult), reads=[B_ep, B_neglam], writes=[B_ep])
                    r0b = rc[:, 0, :].unsqueeze(2).broadcast_to([128, 4, 128])
                    r1b = ep[:, 8:12].unsqueeze(2).broadcast_to([128, 4, 128])
                    tr.op("pool", lambda e: e.tensor_tensor(out=epo[:], in0=o4[:, 0, :, 0:128], in1=r0b, op=ALU.mult), reads=[B_osb, B_ep], writes=[B_epo])
                    tr.op("pool", lambda e: e.tensor_tensor(out=ept[:], in0=o4[:, 1, :, 0:128], in1=r1b, op=ALU.mult), reads=[B_osb, B_ep], writes=[B_ept])
                    tr.op("pool", lambda e: e.tensor_tensor(out=epo[:], in0=epo[:], in1=ept[:], op=ALU.add), reads=[B_epo, B_ept], writes=[B_epo])
                    tr.op("pool", lambda e: e.tensor_tensor(out=ept[:], in0=epo[:], in1=epo[:], op=ALU.mult), reads=[B_epo], writes=[B_ept])
                    tr.op("dve", lambda e: e.reduce_sum(out=ep[:, 12:16], in_=ept[:], axis=AX.X), reads=[B_ept], writes=[B_ep])
                    tr.op("dve", lambda e: e.tensor_scalar(out=ep[:, 16:20], in0=ep[:, 12:16], scalar1=1.0 / 128.0, scalar2=SUBLN_EPS, op0=ALU.mult, op1=ALU.add), reads=[B_ep], writes=[B_ep])
                    tr.op("pool", lambda e: e.tensor_tensor(out=ep[:, 20:24], in0=ep[:, 16:20], in1=nhalf[:, 0:1].broadcast_to([128, 4]), op=ALU.pow), reads=[B_ep, B_nhalf], writes=[B_ep])
                    rsb = ep[:, 20:24].unsqueeze(2).broadcast_to([128, 4, 128])
                    tr.op("pool", lambda e: e.tensor_tensor(out=epo[:], in0=epo[:], in1=rsb, op=ALU.mult), reads=[B_epo, B_ep], writes=[B_epo])
                    aob, B_aob = AOB.next()
                    tr.op("pool", lambda e: e.tensor_tensor(out=aob[:], in0=epo[:], in1=gsub_b, op=ALU.mult), reads=[B_epo, B_gsub], writes=[B_aob])
                    def tail(aob=aob, B_aob=B_aob, dst=ao_s[h, :, own_off[s] + q0:own_off[s] + q0 + 512]):
                        aot, B_aot = AOT.next()
                        pt3, B_pt3 = PT3.next()
                        for j in range(4):
                            tr.op("pe", lambda e, j=j: e.transpose(out=pt3[:, j, :], in_=aob[:, j, :], identity=identB[:]), reads=[B_aob, B_identB], writes=[B_pt3], flag=(j == 3))
                        tr.op("dve", lambda e: e.tensor_copy(out=aot[:], in_=pt3[:, 0:4, :].rearrange("p j t -> p (j t)")), reads=[B_pt3], writes=[B_aot])
                        tr.dma("pool", dst, aot[:], reads=[B_aot])
                    pending.append(tail)
        while pending:
            pending.pop(0)()
        tr.barrier()

    if debug == "p3":
        return finish(nc, tr, es, y, N_own)

    def ln_tm(ST, lnp, B_lnp, gi, src_ps, B_src, resid, B_res, dst, B_dst, pool_tail=False, evac=None):
        st, B_st = ST.next()
        d2 = dst[:].rearrange("p (a d) -> p a d", a=2)
        r2 = resid[:].rearrange("p (a d) -> p a d", a=2)
        src_ap = src_ps[:]
        if evac is not None:
            ev, B_ev = evac.next()
            e2 = ev[:].rearrange("p (a d) -> p a d", a=2)
            tr.op("act", lambda e: e.copy(out=e2, in_=src_ps[:]), reads=[B_src], writes=[B_ev])
            src_ap, B_src = e2, B_ev
        tr.op("dve", lambda e: e.scalar_tensor_tensor(out=d2, in0=r2, scalar=ALPHA, in1=src_ap, op0=ALU.mult, op1=ALU.add),
              reads=[B_res, B_src], writes=[B_dst])
        tr.op("dve", lambda e: e.bn_stats(out=st[:, 0:6], in_=dst[:, 0:512]), reads=[B_dst], writes=[B_st])
        tr.op("dve", lambda e: e.bn_stats(out=st[:, 6:12], in_=dst[:, 512:1024]), reads=[B_dst], writes=[B_st])
        tr.op("dve", lambda e: e.bn_aggr(out=st[:, 12:14], in_=st[:, 0:12]), reads=[B_st], writes=[B_st])
        tr.op("dve", lambda e: e.tensor_scalar_add(out=st[:, 15:16], in0=st[:, 13:14], scalar1=LN_EPS), reads=[B_st], writes=[B_st])
        tr.op("pool", lambda e: e.tensor_tensor(out=st[:, 14:15], in0=st[:, 15:16], in1=nhalf[:, 0:1], op=ALU.pow), reads=[B_st, B_nhalf], writes=[B_st])
        if pool_tail:
            tr.op("pool", lambda e: e.tensor_scalar(out=dst[:], in0=dst[:], scalar1=st[:, 12:13], scalar2=st[:, 14:15], op0=ALU.subtract, op1=ALU.mult),
                  reads=[B_st, B_dst], writes=[B_dst])
            tr.op("pool", lambda e: e.tensor_tensor(out=dst[:], in0=dst[:], in1=lnp[:, gi, :], op=ALU.mult), reads=[B_lnp, B_dst], writes=[B_dst])
            tr.op("pool", lambda e: e.tensor_tensor(out=dst[:], in0=dst[:], in1=lnp[:, gi + 1, :], op=ALU.add), reads=[B_lnp, B_dst], writes=[B_dst])
            return
        tr.op("dve", lambda e: e.scalar_tensor_tensor(out=dst[:], in0=dst[:], scalar=st[:, 12:13], in1=lnp[:, gi, :], op0=ALU.subtract, op1=ALU.mult),
              reads=[B_st, B_lnp, B_dst], writes=[B_dst])
        tr.op("dve", lambda e: e.scalar_tensor_tensor(out=dst[:], in0=dst[:], scalar=st[:, 14:15], in1=lnp[:, gi + 1, :], op0=ALU.mult, op1=ALU.add),
              reads=[B_st, B_lnp, B_dst], writes=[B_dst])

    x1_s = dscr("x1_s", [N_own, D], F32)
    TA = 512
    with ExitStack() as ph:
        issue_w(wq, B_wq, xa_wq, D)
        issue_w(wo, B_wo, xa_wo, D)
        lnp, B_lnp = sb(tr, ph, nc, "lnp", [128, 2, D], F32)
        tr.dma("sync", lnp[:].rearrange("p a d -> p (a d)"), bc[:, 0:2 * D], writes=[B_lnp])
        AO = Rot(tr, ph, nc, "aoT", [128, NCH, TA], BF16, 2)
        LOo = Rot(tr, ph, nc, "loT", [128, NCH, TA], BF16, 2)
        GA = Rot(tr, ph, nc, "gaT", [128, NCH, TA], BF16, 2)
        GL = Rot(tr, ph, nc, "glT", [128, NCH, TA], BF16, 2)
        XIN = Rot(tr, ph, nc, "xin", [128, D], F32, 3)
        MG = Rot(tr, ph, nc, "mgT", [128, NCH, TA], BF16, 2)
        X1 = Rot(tr, ph, nc, "x1", [128, D], F32, 3)
        TMP = Rot(tr, ph, nc, "t4", [128, TA], F32, 4)
        ST = Rot(tr, ph, nc, "st4", [128, 16], F32, 4)
        PSA = Rot(tr, ph, nc, "psa", [128, 512], F32, 4, psum=True)
        PSO = Rot(tr, ph, nc, "pso", [128, 2, 512], F32, 2, psum=True)
        tiles = [(s, tt) for s in range(n_seq) for tt in range(seqs[s][1] // TA)]

        def loads(idx):
            s, tt = tiles[idx]
            o0 = own_off[s] + tt * TA
            r = []
            for R_, src in ((AO, ao_s), (LOo, lo_s), (GA, ga_s), (GL, gl_s)):
                t, B = R_.next()
                tr.dma("sync", t[:], src[:, :, o0:o0 + TA].rearrange("c p t -> p c t"), writes=[B])
                r.append((t, B))
            return r

        def s1_chunks(ld, mgT, B_mgT, ms):
            (aoT, B_ao), (loT, B_lo), (gaT, B_ga), (glT, B_gl) = ld
            for m in ms:
                psa, B_psa = PSA.next()
                psl, B_psl = PSA.next()
                for k in range(NCH):
                    tr.op("pe", lambda e, k=k: e.matmul(psa[:, 0:TA], lhsT=wpa[:, k, m * 128:(m + 1) * 128], rhs=aoT[:, k, :], start=(k == 0), stop=(k == NCH - 1)),
                          reads=[B_wpa, B_ao], writes=[B_psa], flag=(k == NCH - 1))
                for k in range(NCH):
                    tr.op("pe", lambda e, k=k: e.matmul(psl[:, 0:TA], lhsT=wpl[:, k, m * 128:(m + 1) * 128], rhs=loT[:, k, :], start=(k == 0), stop=(k == NCH - 1)),
                          reads=[B_wpl, B_lo], writes=[B_psl], flag=(k == NCH - 1))
                t4, B_t4 = TMP.next()
                tr.op("act", lambda e: e.copy(out=t4[:], in_=psa[:, 0:TA]), reads=[B_psa], writes=[B_t4])
                t5, B_t5 = TMP.next()
                tr.op("act", lambda e: e.copy(out=t5[:], in_=psl[:, 0:TA]), reads=[B_psl], writes=[B_t5])
                tr.op("pool", lambda e: e.tensor_tensor(out=t4[:], in0=t4[:], in1=gaT[:, m, :], op=ALU.mult), reads=[B_t4, B_ga], writes=[B_t4])
                tr.op("pool", lambda e: e.tensor_tensor(out=t5[:], in0=t5[:], in1=glT[:, m, :], op=ALU.mult), reads=[B_t5, B_gl], writes=[B_t5])
                tr.op("pool", lambda e: e.tensor_tensor(out=mgT[:, m, :], in0=t4[:], in1=t5[:], op=ALU.add), reads=[B_t4, B_t5], writes=[B_mgT])

        def s2_sub(idx, mgT, B_mgT, j):
            s, tt = tiles[idx]
            o0 = own_off[s] + tt * TA
            c0x = ctx_off[s] + tt * TA
            xin, B_xin = XIN.next()
            tr.dma("sync", xin[:], xs[c0x + j * 128:c0x + (j + 1) * 128, :], writes=[B_xin])
            pso, B_pso = PSO.next()
            for hf in range(2):
                for k in range(NCH):
                    tr.op("pe", lambda e, k=k, hf=hf: e.matmul(pso[:, hf, :], lhsT=mgT[:, k, j * 128:(j + 1) * 128], rhs=wmx[:, k, hf * 512:(hf + 1) * 512], start=(k == 0), stop=(k == NCH - 1)),
                          reads=[B_mgT, B_wmx], writes=[B_pso], flag=(k == NCH - 1))
            x1, B_x1 = X1.next()
            ln_tm(ST, lnp, B_lnp, 0, pso, B_pso, xin, B_xin, x1, B_x1)
            tr.dma("pool", x1_s[o0 + j * 128:o0 + (j + 1) * 128, :], x1[:], reads=[B_x1])

        ld_cur = loads(0)
        mg_cur = MG.next()
        ld_nxt = loads(1) if len(tiles) > 1 else None
        s1_chunks(ld_cur, mg_cur[0], mg_cur[1], range(NCH))
        for idx in range(len(tiles)):
            has_next = idx + 1 < len(tiles)
            if has_next:
                mg_nxt = MG.next()
            for j in range(TA // 128):
                if has_next:
                    s1_chunks(ld_nxt, mg_nxt[0], mg_nxt[1], range(2 * j, 2 * j + 2))
                s2_sub(idx, mg_cur[0], mg_cur[1], j)
            if has_next:
                mg_cur = mg_nxt
                ld_nxt = loads(idx + 2) if idx + 2 < len(tiles) else None
        tr.barrier()

    es_w4.close()

    with ExitStack() as ph:
        lnp, B_lnp = sb(tr, ph, nc, "lnp2", [128, 2, D], F32)
        tr.dma("sync", lnp[:].rearrange("p a d -> p (a d)"), bc[:, 2 * D:4 * D], writes=[B_lnp])
        KM = Rot(tr, ph, nc, "kmT4", [128, NCH, NMEM], BF16, 2)
        VM = Rot(tr, ph, nc, "vm4", [128, 2, 4, 257], BF16, 2)
        NJ = TA // 128
        X1L = Rot(tr, ph, nc, "x1l", [128, D], F32, 3 * NJ)
        XB = Rot(tr, ph, nc, "x1b", [128, D], BF16, 2)
        X1T = Rot(tr, ph, nc, "x1T", [128, NCH, TA], BF16, 2)
        QCT = Rot(tr, ph, nc, "qcT", [128, NCH, TA], BF16, 2)
        PTc = Rot(tr, ph, nc, "ptc", [128, TA], BF16, 4)
        OTM = Rot(tr, ph, nc, "otm", [128, D], BF16, 2 * NJ)
        OT = Rot(tr, ph, nc, "oT", [128, NCH, TA], BF16, 2)
        X2 = Rot(tr, ph, nc, "x2", [128, D], F32, 2)
        EV2 = Rot(tr, ph, nc, "ev2", [128, D], F32, 2)
        ST = Rot(tr, ph, nc, "st5a", [128, 16], F32, 6)
        PSA = Rot(tr, ph, nc, "psa2", [128, 512], F32, 2, psum=True)
        PSV = Rot(tr, ph, nc, "psv", [128, 512], F32, 2, psum=True)
        PTB = Rot(tr, ph, nc, "p4ptb", [128, 8, 128], BF16, 2, psum=True)
        PSO = Rot(tr, ph, nc, "pso2", [128, 2, 512], F32, 1, psum=True)
        tiles = [(s, tt) for s in range(n_seq) for tt in range(seqs[s][1] // TA)]
        memkv = {}
        cpi2 = [0]

        def cp2(out, in_, rb, wb):
            cpi2[0] += 1
            if cpi2[0] % 2 == 0:
                tr.op("act", lambda e: e.copy(out=out, in_=in_), reads=rb, writes=wb)
            else:
                tr.op("dve", lambda e: e.tensor_copy(out=out, in_=in_), reads=rb, writes=wb)

        def front1(idx):
            s, tt = tiles[idx]
            o0 = own_off[s] + tt * TA
            if s not in memkv:
                kmT, B_kmT = KM.next()
                vm, B_vm = VM.next()
                tr.dma("sync", kmT[:].rearrange("p c m -> p (c m)"), kmT_s[s], writes=[B_kmT])
                tr.dma("sync", vm[:].rearrange("p t h d -> p (t h d)"), vm_s[s], writes=[B_vm])
                memkv[s] = (kmT, B_kmT, vm, B_vm)
            x1T, B_x1T = X1T.next()
            x1l = []
            for j in range(NJ):
                x1, B_x1 = X1L.next()
                tr.dma("sync", x1[:], x1_s[o0 + j * 128:o0 + (j + 1) * 128, :], writes=[B_x1])
                x1l.append((x1, B_x1))
                xb, B_xb = XB.next()
                tr.op("act", lambda e: e.copy(out=xb[:], in_=x1[:]), reads=[B_x1], writes=[B_xb])
                ptb, B_ptb = PTB.next()
                for k in range(NCH):
                    tr.op("pe", lambda e, k=k: e.transpose(out=ptb[:, k, :], in_=xb[:, k * 128:(k + 1) * 128], identity=identB[:]), reads=[B_xb, B_identB], writes=[B_ptb], flag=(k == NCH - 1))
                tr.op("act", lambda e: e.copy(out=x1T[:, :, j * 128:(j + 1) * 128], in_=ptb[:]), reads=[B_ptb], writes=[B_x1T])
            return dict(idx=idx, s=s, o0=o0, x1T=(x1T, B_x1T), x1l=x1l)

        def front2(t):
            x1T, B_x1T = t["x1T"]
            qcT, B_qcT = QCT.next()
            for m in range(NCH):
                psa, B_psa = PSA.next()
                for k in range(NCH):
                    tr.op("pe", lambda e, k=k: e.matmul(psa[:, 0:TA], lhsT=wq[:, k, m * 128:(m + 1) * 128], rhs=x1T[:, k, :], start=(k == 0), stop=(k == NCH - 1)),
                          reads=[B_wq, B_x1T], writes=[B_psa], flag=(k == NCH - 1))
                cp2(qcT[:, m, :], psa[:, 0:TA], [B_psa], [B_qcT])
            t["qcT"] = (qcT, B_qcT)

        def back1(t):
            kmT, B_kmT, vm, B_vm = memkv[t["s"]]
            qcT, B_qcT = t["qcT"]
            otms = [OTM.next() for _ in range(NJ)]
            t["otms"] = otms

            def sc(hh):
                pts = []
                for mt in range(2):
                    psa, B_psa = PSA.next()
                    for dc in range(2):
                        tr.op("pe", lambda e, dc=dc: e.matmul(psa[:, 0:TA], lhsT=kmT[:, hh * 2 + dc, mt * 128:(mt + 1) * 128], rhs=qcT[:, hh * 2 + dc, :], start=(dc == 0), stop=(dc == 1)),
                              reads=[B_kmT, B_qcT], writes=[B_psa], flag=(dc == 1))
                    pt, B_pt = PTc.next()
                    tr.op("act", lambda e: e.activation(out=pt[:], in_=psa[:, 0:TA], func=AF.Exp, scale=1.0 / 16.0), reads=[B_psa], writes=[B_pt])
                    pts.append((pt, B_pt))
                return pts

            def pv(hh, pts):
                for j in range(NJ):
                    psv, B_psv = PSV.next()
                    for mt in range(2):
                        pt, B_pt = pts[mt]
                        tr.op("pe", lambda e, mt=mt: e.matmul(psv[:, 0:257], lhsT=pt[:, j * 128:(j + 1) * 128], rhs=vm[:, mt, hh, :], start=(mt == 0), stop=(mt == 1)),
                              reads=[B_pt, B_vm], writes=[B_psv], flag=(mt == 1))
                    st, B_st = ST.next()
                    tr.op("dve", lambda e: e.reciprocal(out=st[:, 0:1], in_=psv[:, 256:257]), reads=[B_psv], writes=[B_st])
                    otm, B_otm = otms[j]
                    tr.op("dve", lambda e: e.tensor_scalar(out=otm[:, hh * 256:(hh + 1) * 256], in0=psv[:, 0:256], scalar1=st[:, 0:1], scalar2=None, op0=ALU.mult),
                          reads=[B_psv, B_st], writes=[B_otm])

            prev = sc(0)
            for hh in range(1, 4):
                cur = sc(hh)
                pv(hh - 1, prev)
                prev = cur
            pv(3, prev)

        def back3(t):
            oT, B_oT = OT.next()
            for j in range(NJ):
                otm, B_otm = t["otms"][j]
                ptb, B_ptb = PTB.next()
                for k in range(NCH):
                    tr.op("pe", lambda e, k=k: e.transpose(out=ptb[:, k, :], in_=otm[:, k * 128:(k + 1) * 128], identity=identB[:]), reads=[B_otm, B_identB], writes=[B_ptb], flag=(k == NCH - 1))
                cp2(oT[:, :, j * 128:(j + 1) * 128], ptb[:], [B_ptb], [B_oT])
            t["oT"] = (oT, B_oT)

        def back4(t):
            oT, B_oT = t["oT"]
            o0 = t["o0"]
            for j in range(NJ):
                pso, B_pso = PSO.next()
                for hf in range(2):
                    for k in range(NCH):
                        tr.op("pe", lambda e, k=k, hf=hf: e.matmul(pso[:, hf, :], lhsT=oT[:, k, j * 128:(j + 1) * 128], rhs=wo[:, k, hf * 512:(hf + 1) * 512], start=(k == 0), stop=(k == NCH - 1)),
                              reads=[B_oT, B_wo], writes=[B_pso], flag=(k == NCH - 1))
                x2, B_x2 = X2.next()
                x1, B_x1 = t["x1l"][j]
                ln_tm(ST, lnp, B_lnp, 0, pso, B_pso, x1, B_x1, x2, B_x2, evac=EV2)
                tr.dma("pool", x2_s[o0 + j * 128:o0 + (j + 1) * 128, :], x2[:], reads=[B_x2])

        cur = front1(0)
        front2(cur)
        for idx in range(len(tiles)):
            nx = front1(idx + 1) if idx + 1 < len(tiles) else None
            back1(cur)
            if nx is not None:
                front2(nx)
            back3(cur)
            back4(cur)
            cur = nx
        tr.barrier()

    es_w5.close()
    if debug == "p4a":
        return finish(nc, tr, es, y, N_own)

    TT = 256
    NS = TT // 128
    with ExitStack() as ph:
        wi, B_wi = load_w(ph, "wi", ffn_wi, 2 * DFF, colblk=1408)
        wo2, B_wo2 = load_w(ph, "wo2", ffn_wo, D, kchunks=NFF, own=True)
        lnp, B_lnp = sb(tr, ph, nc, "lnp3", [128, 2, D], F32)
        tr.dma("sync", lnp[:].rearrange("p a d -> p (a d)"), bc[:, 4 * D:6 * D], chan=B_lnp, writes=[B_lnp])
        X2 = Rot(tr, ph, nc, "x2in", [128, D], F32, 3 * NS)
        XB = Rot(tr, ph, nc, "x2b", [128, D], BF16, 2)
        aT, B_aT = sb(tr, ph, nc, "aT", [128, NFF, TT], BF16)
        SG = Rot(tr, ph, nc, "sg", [128, TT], F32, 3)
        YO = Rot(tr, ph, nc, "yo", [128, D], F32, 2)
        ST = Rot(tr, ph, nc, "st5", [128, 16], F32, 4)
        PSA = Rot(tr, ph, nc, "psb", [128, 512], F32, 4, psum=True)
        PSO = Rot(tr, ph, nc, "pso5", [128, 2, 512], F32, 1, psum=True)
        PTB = Rot(tr, ph, nc, "p5ptb", [128, 8, 128], BF16, 2, psum=True)
        X2T = Rot(tr, ph, nc, "x2Tr", [128, NCH, TT], BF16, 2)
        tiles = [(s, tt) for s in range(n_seq) for tt in range(seqs[s][1] // TT)]

        def front(idx):
            s, tt = tiles[idx]
            o0 = own_off[s] + tt * TT
            x2T, B_x2T = X2T.next()
            x2s = []
            for j in range(NS):
                x2, B_x2 = X2.next()
                tr.dma("sync", x2[:], x2_s[o0 + j * 128:o0 + (j + 1) * 128, :], writes=[B_x2])
                x2s.append((x2, B_x2))
                xb, B_xb = XB.next()
                tr.op("act", lambda e: e.copy(out=xb[:], in_=x2[:]), reads=[B_x2], writes=[B_xb])
                ptb, B_ptb = PTB.next()
                for k in range(NCH):
                    tr.op("pe", lambda e, k=k: e.transpose(out=ptb[:, k, :], in_=xb[:, k * 128:(k + 1) * 128], identity=identB[:]), reads=[B_xb, B_identB], writes=[B_ptb], flag=(k == NCH - 1))
                tr.op("act", lambda e: e.copy(out=x2T[:, :, j * 128:(j + 1) * 128], in_=ptb[:]), reads=[B_ptb], writes=[B_x2T])
            return dict(o0=o0, x2T=(x2T, B_x2T), x2s=x2s)

        cur = front(0)
        for idx in range(len(tiles)):
            o0 = cur["o0"]
            x2T, B_x2T = cur["x2T"]
            x2s = cur["x2s"]
            for f in range(NFF):
                psg, B_psg = PSA.next()
                psu, B_psu = PSA.next()
                for k in range(NCH):
                    tr.op("pe", lambda e, k=k: e.matmul(psg[:, 0:TT], lhsT=wi[:, k, f * 128:(f + 1) * 128], rhs=x2T[:, k, :], start=(k == 0), stop=(k == NCH - 1)),
                          reads=[B_wi, B_x2T], writes=[B_psg], flag=(k == NCH - 1))
                for k in range(NCH):
                    tr.op("pe", lambda e, k=k: e.matmul(psu[:, 0:TT], lhsT=wi[:, k, DFF + f * 128:DFF + (f + 1) * 128], rhs=x2T[:, k, :], start=(k == 0), stop=(k == NCH - 1)),
                          reads=[B_wi, B_x2T], writes=[B_psu], flag=(k == NCH - 1))
                sg, B_sg = SG.next()
                tr.op("act", lambda e: e.activation(out=sg[:], in_=psg[:, 0:TT], func=AF.Silu), reads=[B_psg], writes=[B_sg])
                tr.op("dve", lambda e: e.tensor_tensor(out=aT[:, f, :], in0=psu[:, 0:TT], in1=sg[:], op=ALU.mult), reads=[B_psu, B_sg], writes=[B_aT])
            nx = front(idx + 1) if idx + 1 < len(tiles) else None
            for j in range(NS):
                pso, B_pso = PSO.next()
                for hf in range(2):
                    for f in range(NFF):
                        tr.op("pe", lambda e, f=f, hf=hf: e.matmul(pso[:, hf, :], lhsT=aT[:, f, j * 128:(j + 1) * 128], rhs=wo2[:, f, hf * 512:(hf + 1) * 512], start=(f == 0), stop=(f == NFF - 1)),
                              reads=[B_aT, B_wo2], writes=[B_pso], flag=(f == NFF - 1))
                yo, B_yo = YO.next()
                x2, B_x2 = x2s[j]
                ln_tm(ST, lnp, B_lnp, 0, pso, B_pso, x2, B_x2, yo, B_yo)
                tr.dma("pool", y[o0 + j * 128:o0 + (j + 1) * 128, :], yo[:], reads=[B_yo])
            cur = nx
        tr.barrier()
    return finish(nc, tr, es, y, N_own)


def finish(nc, tr, es, y, N_own):
    if tr.barcount == 0 or any(tr.cnt[e] for e in tr.cnt):
        tr.barrier()
    es.close()
    return nc


def _rope_table(T):
    half = 8
    inv = (np.float32(ROPE_THETA) ** (-np.arange(0, 16, 2, dtype=np.float32) / np.float32(16))).astype(np.float32)
    ang = (np.arange(T, dtype=np.float32)[:, None] * inv[None, :]).astype(np.float32)
    return np.concatenate([np.cos(ang), np.sin(ang)], axis=1).astype(np.float32)


def _fm(v):
    return np.ascontiguousarray(np.asarray(v, np.float32).reshape(NCH, 128).T)


def make_core_inputs(inp, assign, shared):
    seq_list, rev = assign
    xs = np.concatenate([x for (x, m, to) in seq_list], axis=0)
    mems = np.stack([m for (x, m, to) in seq_list], axis=0)
    ropes = []
    for (x, m, to) in seq_list:
        tb = _rope_table(x.shape[0])
        ropes.append(tb[::-1] if rev else tb)
    rope = np.concatenate(ropes, axis=0)
    nt = rope.shape[0] // 128
    rope = np.ascontiguousarray(rope.reshape(nt, 128, 16).transpose(1, 0, 2).reshape(128, nt * 16))
    A, B = (1, 0) if rev else (0, 1)
    cw = np.asarray(inp["conv_w"][0], np.float32)
    z = np.zeros_like(cw[0])
    taps = [z, cw[3], cw[2], cw[1], cw[0]] if rev else [cw[0], cw[1], cw[2], cw[3], z]
    pp = np.zeros((128, NPP), np.float32)
    for j in range(5):
        pp[:, PP_TAP + j:PP_TAP + 40:5] = _fm(taps[j])
    pp[:, PP_CB:PP_CB + 8] = _fm(inp["conv_b"][0])
    pp[:, PP_BAA:PP_BAA + 8] = _fm(inp["lru_ba"][0, A])
    pp[:, PP_BXA:PP_BXA + 8] = _fm(inp["lru_bx"][0, A])
    pp[:, PP_BAB:PP_BAB + 8] = _fm(inp["lru_ba"][0, B])
    pp[:, PP_BXB:PP_BXB + 8] = _fm(inp["lru_bx"][0, B])
    pp[:, PP_LAA:PP_LAA + 8] = _fm(inp["lru_a"][0, A])
    pp[:, PP_LAB:PP_LAB + 8] = _fm(inp["lru_a"][0, B])
    lru_w = np.ascontiguousarray(np.stack([inp["lru_wa"][0, A], inp["lru_wx"][0, A], inp["lru_wa"][0, B], inp["lru_wx"][0, B]], axis=0).astype(np.float32))
    d = dict(shared)
    d.update(xs=np.ascontiguousarray(xs), mems=np.ascontiguousarray(mems), rope=rope, pp=pp, lru_w=lru_w)
    return d


def make_shared(inp):
    bcv = np.concatenate([np.asarray(inp[k][0], np.float32).ravel() for k in
                          ("ln1_g", "ln1_b", "ln2_g", "ln2_b", "ln3_g", "ln3_b", "subln_g", "lambda_q1", "lambda_k1", "lambda_q2", "lambda_k2")])
    assert bcv.shape[0] == NBC
    bc = np.ascontiguousarray(np.broadcast_to(bcv[None, :], (128, NBC)))
    f = lambda k: np.ascontiguousarray(np.asarray(inp[k][0], np.float32))
    return dict(w_in=f("w_in"), p_attn=f("p_attn"), p_lru=f("p_lru"), w_mix=f("w_mix_out"), xa_wq=f("xa_wq"),
                xa_wkv=f("xa_wkv"), xa_wo=f("xa_wo"), ffn_wi=f("ffn_w_in"), ffn_wo=f("ffn_w_out"), bc=bc,
                ident=np.eye(128, dtype=np.float32))


def kernel(**inp):
    return _run(inp, 8)


def _run(inp, n, debug=False):
    xp = np.asarray(inp["x_prompt"], np.float32)
    xsm = np.asarray(inp["x_sample"], np.float32)
    mp = np.asarray(inp["mem_prompt"], np.float32)
    ms = np.asarray(inp["mem_sample"], np.float32)
    Bp, Tp, _ = xp.shape
    Bs, Ts, _ = xsm.shape
    per = Bs // n
    seqs = [(Ts, Ts)] * per + [(Tp, Tp // 2)]
    shared = make_shared(inp)
    in_maps = []
    for c in range(n):
        rev = (c % 2 == 1)
        sl = []
        for i in range(per):
            x = xsm[c * per + i]
            sl.append((x[::-1] if rev else x, ms[c * per + i], Ts))
        x = xp[c // 2]
        sl.append((x[::-1] if rev else x, mp[c // 2], Tp // 2))
        in_maps.append(make_core_inputs(inp, (sl, rev), shared))
    nc = build_program(seqs, debug=debug)
    res = run_bass_kernel_spmd(nc, in_maps, core_ids=list(range(n)))
    if debug:
        return res.results, seqs
    y_p = np.empty((Bp, Tp, D), np.float32)
    y_s = np.empty((Bs, Ts, D), np.float32)
    for c in range(n):
        yc = res.results[c]["y"]
        rev = (c % 2 == 1)
        for i in range(per):
            blk = yc[i * Ts:(i + 1) * Ts]
            y_s[c * per + i] = blk[::-1] if rev else blk
        blk = yc[per * Ts:per * Ts + Tp // 2]
        if rev:
            y_p[c // 2, Tp // 2:] = blk[::-1]
        else:
            y_p[c // 2, :Tp // 2] = blk
    return (y_p, y_s)
```
